# Optimizing a Trainium2 kernel written in Bass

```python
import math
import jax
import jax.numpy as jnp
from jax import lax
import numpy as np

D_MODEL = 1024
BATCH = 32
SEQ = 256
DEPTH = 2
DEC_BATCH = 8
DEC_SEQ = 2048
PAST_LEN = 256

GRID_W = 64
Q_BLOCK = 128
HEAD_DIM = 64
GA_HEADS = 8
GA_KV_HEADS = 2
NA_HEADS = 8
NA_WIN_ROWS = 8
NA_WIN_COLS = 16
SSM_WIDTH = 512
SSM_GROUP_CH = 16
SSM_GROUPS = SSM_WIDTH // SSM_GROUP_CH
SSM_STATE = 64
D_FF = 2816
N_BRANCH = 3
ROPE_THETA = 10000.0
EPS = 1e-6
STEP_MIN = 1e-3
STEP_MAX = 1e-1
GA_Q_W = GA_HEADS * HEAD_DIM
GA_KV_W = GA_KV_HEADS * HEAD_DIM
NA_W = NA_HEADS * HEAD_DIM
IN_SPLITS = (GA_Q_W, GA_KV_W, GA_KV_W, SSM_WIDTH, NA_W, NA_W, NA_W, N_BRANCH * D_MODEL)
IN_WIDTH = GA_Q_W + 2 * GA_KV_W + SSM_WIDTH + 3 * NA_W + N_BRANCH * D_MODEL

kernel_name = 'hybrid_flow_trunk_step'


def rmsnorm(x, g):
    xf = x.astype(jnp.float32)
    y = xf * lax.rsqrt(jnp.mean(xf * xf, axis=-1, keepdims=True) + EPS)
    return (y * g.astype(jnp.float32)).astype(x.dtype)


def adaln(cvec, w, b):
    m = jax.nn.silu(cvec) @ w + b
    return m.reshape(cvec.shape[0], 6, D_MODEL)


def axial_rope(x):
    B, L, H, dh = x.shape
    nf = dh // 4
    t = jnp.arange(L)
    pos = jnp.stack([t // GRID_W, t % GRID_W]).astype(jnp.float32)
    inv = ROPE_THETA ** (-jnp.arange(nf, dtype=jnp.float32) / nf)
    ang = (pos[:, :, None] * inv).transpose(1, 0, 2)[:, None]
    cos, sin = jnp.cos(ang), jnp.sin(ang)
    xf = x.astype(jnp.float32).reshape(B, L, H, 2, 2, nf)
    x1, x2 = xf[..., 0, :], xf[..., 1, :]
    out = jnp.stack([x1 * cos - x2 * sin, x1 * sin + x2 * cos], axis=-2)
    return out.reshape(B, L, H, dh).astype(x.dtype)


def block_attention(q, k, v):
    B, Lq, H, dh = q.shape
    G = k.shape[2]
    R = H // G
    nb = Lq // Q_BLOCK
    scale = dh ** -0.5
    qb = q.reshape(B, nb, Q_BLOCK, G, R, dh).swapaxes(0, 1)

    def one_block(q_i):
        s = jnp.einsum('bqgrd,bkgd->bgrqk', q_i, k).astype(jnp.float32) * scale
        p = jax.nn.softmax(s, axis=-1).astype(v.dtype)
        return jnp.einsum('bgrqk,bkgd->bqgrd', p, v)

    out = lax.map(one_block, qb)
    return out.swapaxes(0, 1).reshape(B, Lq, H * dh)


def neighbourhood_attention(q, k, v, kc, vc, rpb):
    B, L, H, dh = q.shape
    rows = L // GRID_W
    kr_n = min(NA_WIN_ROWS, rows)
    kc_n = NA_WIN_COLS
    K = kr_n * kc_n
    t = jnp.arange(L)
    r = t // GRID_W
    col = t % GRID_W
    r0 = jnp.clip(r - kr_n // 2, 0, rows - kr_n)
    c0 = jnp.clip(col - kc_n // 2, 0, GRID_W - kc_n)
    key_r = r0[:, None] + jnp.arange(kr_n)
    key_c = c0[:, None] + jnp.arange(kc_n)
    idx = (key_r[:, :, None] * GRID_W + key_c[:, None, :]).reshape(L, K)
    dr = (key_r - r[:, None] + NA_WIN_ROWS - 1)[:, :, None]
    dc = (key_c - col[:, None] + NA_WIN_COLS - 1)[:, None, :]
    bias = rpb[:, dr, dc].reshape(H, L, K)
    nb = L // Q_BLOCK
    scale = dh ** -0.5
    qb = q.reshape(B, nb, Q_BLOCK, H, dh).swapaxes(0, 1)
    idxb = idx.reshape(nb, Q_BLOCK, K)
    biasb = bias.reshape(H, nb, Q_BLOCK, K).swapaxes(0, 1)

    def one_block(args):
        q_i, idx_i, bias_i = args
        k_i = k[:, idx_i]
        v_i = v[:, idx_i]
        s_loc = jnp.einsum('bqhd,bqkhd->bhqk', q_i, k_i).astype(jnp.float32) * scale + bias_i.astype(jnp.float32)
        s_ctx = jnp.einsum('bqhd,bchd->bhqc', q_i, kc).astype(jnp.float32) * scale
        p = jax.nn.softmax(jnp.concatenate([s_loc, s_ctx], axis=-1), axis=-1).astype(v.dtype)
        return (jnp.einsum('bhqk,bqkhd->bqhd', p[..., :K], v_i)
                + jnp.einsum('bhqc,bchd->bqhd', p[..., K:], vc))

    out = lax.map(one_block, (qb, idxb, biasb))
    return out.swapaxes(0, 1).reshape(B, L, H * dh)


def linear_recurrence(bu, lam_bar, h0, reverse):
    a = jnp.broadcast_to(lam_bar, bu.shape)

    def combine(e1, e2):
        a1, b1 = e1
        a2, b2 = e2
        return a2 * a1, a2 * b1 + b2

    a_cum, b_cum = lax.associative_scan(combine, (a, bu), axis=1, reverse=reverse)
    return b_cum + a_cum * h0[:, None]


def s5_mixer(u, lp, h0):
    B, L, _ = u.shape
    f32 = jnp.float32
    ug = u.astype(f32).reshape(B, L, SSM_GROUPS, SSM_GROUP_CH).astype(jnp.complex64)
    outs = []
    finals = []
    for d in range(2):
        lam = lax.complex(lp['lam_re'][d].astype(f32), lp['lam_im'][d].astype(f32))
        dt = jnp.exp(lp['log_step'][d].astype(f32))[:, None]
        lam_bar = jnp.exp(lam * dt)
        b = lax.complex(lp['b_re'][d].astype(f32), lp['b_im'][d].astype(f32))
        b_bar = ((lam_bar - 1.0) / lam)[..., None] * b
        bu = jnp.einsum('blgh,gph->blgp', ug, b_bar)
        states = linear_recurrence(bu, lam_bar, h0[:, d], reverse=(d == 1))
        cmat = lax.complex(lp['c_re'][d].astype(f32), lp['c_im'][d].astype(f32))
        outs.append(jnp.real(jnp.einsum('blgp,ghp->blgh', states, cmat)))
        finals.append(states[:, L - 1] if d == 0 else states[:, 0])
    y = (outs[0] + outs[1]).reshape(B, L, SSM_WIDTH) + lp['ssm_d'].astype(f32) * u.astype(f32)
    g = jax.nn.gelu(y).astype(u.dtype)
    out = g * jax.nn.sigmoid(g @ lp['w_glu'])
    return out, jnp.stack(finals, axis=1)


def conv_ffn(h, w_up, conv_w, conv_b, w_down):
    u = h @ w_up
    up = jnp.pad(u, ((0, 0), (1, 1), (0, 0)))
    u = up[:, :-2] * conv_w[0] + up[:, 1:-1] * conv_w[1] + up[:, 2:] * conv_w[2] + conv_b
    a, g = jnp.split(u, 2, axis=-1)
    return (jax.nn.silu(g) * a) @ w_down


def trunk_layer(x, mod, lp, cache):
    B, L, _ = x.shape
    sh_m, sc_m, gt_m, sh_f, sc_f, gt_f = [mod[:, i][:, None] for i in range(6)]
    h = rmsnorm(x, lp['norm_g'][0]) * (1 + sc_m) + sh_m
    z = h @ lp['w_in']
    offs = []
    acc = 0
    for s in IN_SPLITS[:-1]:
        acc += s
        offs.append(acc)
    zq, zk, zv, zu, nq, nk, nv, zg = jnp.split(z, offs, axis=-1)
    qa = rmsnorm(zq.reshape(B, L, GA_HEADS, HEAD_DIM), lp['qk_g'][0])
    ka = rmsnorm(zk.reshape(B, L, GA_KV_HEADS, HEAD_DIM), lp['qk_g'][1])
    va = zv.reshape(B, L, GA_KV_HEADS, HEAD_DIM)
    qn = nq.reshape(B, L, NA_HEADS, HEAD_DIM)
    kn = nk.reshape(B, L, NA_HEADS, HEAD_DIM)
    vn = nv.reshape(B, L, NA_HEADS, HEAD_DIM)
    if cache is None:
        ya = block_attention(qa, ka, va)
        yc = block_attention(qn, kn, vn)
        h0 = jnp.zeros((B, 2, SSM_GROUPS, SSM_STATE), jnp.complex64)
        yb, fin = s5_mixer(zu, lp, h0)
        ctx_tensors = (ka, va, kn, vn, jnp.real(fin), jnp.imag(fin))
    else:
        ck_a, cv_a, ck_n, cv_n, s_re, s_im = cache
        qa = axial_rope(qa)
        ka = axial_rope(ka)
        ya = block_attention(qa, jnp.concatenate([ka, ck_a], axis=1), jnp.concatenate([va, cv_a], axis=1))
        yc = neighbourhood_attention(qn, kn, vn, ck_n, cv_n, lp['na_rpb'])
        h0 = lax.complex(s_re.astype(jnp.float32), s_im.astype(jnp.float32))
        yb, _ = s5_mixer(zu, lp, h0)
        ctx_tensors = None
    gates = jax.nn.sigmoid(zg.astype(jnp.float32)).astype(x.dtype).reshape(B, L, N_BRANCH, D_MODEL)
    merged = (gates[:, :, 0] * (ya @ lp['w_br_a']) + gates[:, :, 1] * (yb @ lp['w_br_b'])
              + gates[:, :, 2] * (yc @ lp['w_br_c']))
    x = x + gt_m * rmsnorm(merged @ lp['w_out'], lp['norm_g'][1])
    h = rmsnorm(x, lp['norm_g'][2]) * (1 + sc_f) + sh_f
    f = conv_ffn(h, lp['w_up'], lp['conv_w'], lp['conv_b'], lp['w_down'])
    x = x + gt_f * rmsnorm(f, lp['norm_g'][3])
    return x, ctx_tensors


def setup_inputs(seed: int = 0) -> dict:
    key = jax.random.key(seed)
    ks = jax.random.split(key, 40)
    f32 = jnp.float32
    D = D_MODEL
    G = SSM_GROUPS
    P = SSM_STATE
    Hg = SSM_GROUP_CH

    def nrm(k, shape, s):
        return s * jax.random.normal(k, shape, f32)

    return {
        'x_prompt': nrm(ks[0], (BATCH, SEQ, D), 1.0),
        'x_sample': nrm(ks[1], (DEC_BATCH, DEC_SEQ, D), 1.0),
        'c': nrm(ks[2], (DEC_BATCH, D), 1.0),
        'cache_ga_k': nrm(ks[3], (DEC_BATCH, DEPTH, PAST_LEN, GA_KV_HEADS, HEAD_DIM), 1.0),
        'cache_ga_v': nrm(ks[4], (DEC_BATCH, DEPTH, PAST_LEN, GA_KV_HEADS, HEAD_DIM), 1.0),
        'cache_na_k': nrm(ks[5], (DEC_BATCH, DEPTH, PAST_LEN, NA_HEADS, HEAD_DIM), 1.0),
        'cache_na_v': nrm(ks[6], (DEC_BATCH, DEPTH, PAST_LEN, NA_HEADS, HEAD_DIM), 1.0),
        'state_ssm_re': nrm(ks[7], (DEC_BATCH, DEPTH, 2, G, P), 0.1),
        'state_ssm_im': nrm(ks[8], (DEC_BATCH, DEPTH, 2, G, P), 0.1),
        'c_ctx': nrm(ks[9], (D,), 1.0),
        'w_mod': nrm(ks[10], (DEPTH, D, 6 * D), 0.5 * D ** -0.5),
        'b_mod': nrm(ks[11], (DEPTH, 6 * D), 0.01),
        'norm_g': 1.0 + nrm(ks[12], (DEPTH, 4, D), 0.01),
        'w_in': nrm(ks[13], (DEPTH, D, IN_WIDTH), D ** -0.5),
        'qk_norm_g': 1.0 + nrm(ks[14], (DEPTH, 2, HEAD_DIM), 0.01),
        'na_rpb': nrm(ks[15], (DEPTH, NA_HEADS, 2 * NA_WIN_ROWS - 1, 2 * NA_WIN_COLS - 1), 0.1),
        'ssm_lam_re': -0.5 + nrm(ks[16], (DEPTH, 2, G, P), 0.01),
        'ssm_lam_im': math.pi * jnp.arange(P, dtype=f32) + nrm(ks[17], (DEPTH, 2, G, P), 0.01),
        'ssm_log_step': jax.random.uniform(ks[18], (DEPTH, 2, G), f32, math.log(STEP_MIN), math.log(STEP_MAX)),
        'ssm_b_re': nrm(ks[19], (DEPTH, 2, G, P, Hg), (2 * Hg) ** -0.5),
        'ssm_b_im': nrm(ks[20], (DEPTH, 2, G, P, Hg), (2 * Hg) ** -0.5),
        'ssm_c_re': nrm(ks[21], (DEPTH, 2, G, Hg, P), P ** -0.5),
        'ssm_c_im': nrm(ks[22], (DEPTH, 2, G, Hg, P), P ** -0.5),
        'ssm_d': nrm(ks[23], (DEPTH, SSM_WIDTH), 1.0),
        'w_glu': nrm(ks[24], (DEPTH, SSM_WIDTH, SSM_WIDTH), SSM_WIDTH ** -0.5),
        'w_br_a': nrm(ks[25], (DEPTH, GA_Q_W, D), GA_Q_W ** -0.5),
        'w_br_b': nrm(ks[26], (DEPTH, SSM_WIDTH, D), SSM_WIDTH ** -0.5),
        'w_br_c': nrm(ks[27], (DEPTH, NA_W, D), NA_W ** -0.5),
        'w_out': nrm(ks[28], (DEPTH, D, D), D ** -0.5),
        'w_up': nrm(ks[29], (DEPTH, D, 2 * D_FF), D ** -0.5),
        'conv_w': nrm(ks[30], (DEPTH, 3, 2 * D_FF), 3 ** -0.5),
        'conv_b': nrm(ks[31], (DEPTH, 2 * D_FF), 0.01),
        'w_down': nrm(ks[32], (DEPTH, D_FF, D), D_FF ** -0.5),
    }


def reference(x_prompt, x_sample, c, cache_ga_k, cache_ga_v, cache_na_k, cache_na_v, state_ssm_re, state_ssm_im,
              c_ctx, w_mod, b_mod, norm_g, w_in, qk_norm_g, na_rpb, ssm_lam_re, ssm_lam_im, ssm_log_step,
              ssm_b_re, ssm_b_im, ssm_c_re, ssm_c_im, ssm_d, w_glu, w_br_a, w_br_b, w_br_c, w_out,
              w_up, conv_w, conv_b, w_down):
    y_p = x_prompt
    y_s = x_sample
    ga_k, ga_v, na_k, na_v, s_re, s_im = [], [], [], [], [], []
    for l in range(DEPTH):
        lp = {
            'norm_g': norm_g[l], 'w_in': w_in[l], 'qk_g': qk_norm_g[l], 'na_rpb': na_rpb[l],
            'lam_re': ssm_lam_re[l], 'lam_im': ssm_lam_im[l], 'log_step': ssm_log_step[l],
            'b_re': ssm_b_re[l], 'b_im': ssm_b_im[l], 'c_re': ssm_c_re[l], 'c_im': ssm_c_im[l],
            'ssm_d': ssm_d[l], 'w_glu': w_glu[l], 'w_br_a': w_br_a[l], 'w_br_b': w_br_b[l],
            'w_br_c': w_br_c[l], 'w_out': w_out[l], 'w_up': w_up[l], 'conv_w': conv_w[l],
            'conv_b': conv_b[l], 'w_down': w_down[l],
        }
        mod_ctx = adaln(c_ctx[None], w_mod[l], b_mod[l])
        mod_lat = adaln(c, w_mod[l], b_mod[l])
        y_p, ctx_t = trunk_layer(y_p, mod_ctx, lp, None)
        cache_l = (cache_ga_k[:, l], cache_ga_v[:, l], cache_na_k[:, l], cache_na_v[:, l],
                   state_ssm_re[:, l], state_ssm_im[:, l])
        y_s, _ = trunk_layer(y_s, mod_lat, lp, cache_l)
        ga_k.append(ctx_t[0])
        ga_v.append(ctx_t[1])
        na_k.append(ctx_t[2])
        na_v.append(ctx_t[3])
        s_re.append(ctx_t[4])
        s_im.append(ctx_t[5])
    new_ga_k = jnp.stack(ga_k, axis=1)
    new_ga_v = jnp.stack(ga_v, axis=1)
    new_na_k = jnp.stack(na_k, axis=1)
    new_na_v = jnp.stack(na_v, axis=1)
    new_ssm_re = jnp.stack(s_re, axis=1)
    new_ssm_im = jnp.stack(s_im, axis=1)
    return (y_p, y_s, new_ga_k, new_ga_v, new_na_k, new_na_v, new_ssm_re, new_ssm_im)
```

```python
import math
from contextlib import ExitStack

import numpy as np
import ml_dtypes
import concourse.bass as bass
import concourse.mybir as mybir
from concourse.bass_utils import run_bass_kernel_spmd

F32 = mybir.dt.float32
BF16 = mybir.dt.bfloat16
AF = mybir.ActivationFunctionType
ALU = mybir.AluOpType

D = 1024
DEPTH = 2
NCORES = 8
SEQ = 256
DEC_SEQ = 2048
NP_TOK = 1024
NTOK = NP_TOK + DEC_SEQ
HD = 64
IN_W = 5888
D_FF = 2816
EPS = 1e-6
O_GQ, O_GK, O_GV, O_U, O_NQ, O_NK, O_NV, O_G = 0, 512, 640, 768, 1280, 1792, 2304, 2816


class Res:
    __slots__ = ("w", "r", "name", "psum")

    def __init__(self, name="", psum=False):
        self.w = {}
        self.r = {}
        self.name = name
        self.psum = psum


class T:
    __slots__ = ("ap", "res")

    def __init__(self, ap, res):
        self.ap = ap
        self.res = res

    def __getitem__(self, k):
        return T(self.ap[k], self.res)

    def v(self, ap):
        return T(ap, self.res)


class Eng:
    def __init__(self, name, h, sem):
        self.name = name
        self.h = h
        self.sem = sem
        self.count = 0
        self.waited = {}
        self.n_ops = 0
        self.n_waits = 0
        self.prog = []


def _aps(x):
    return x.ap if isinstance(x, T) else x


class Ctx:
    def __init__(self, nc, es, n_dma_sems=40):
        self.nc = nc
        self.engs = {}
        for name, h in (("pe", nc.tensor), ("act", nc.scalar), ("dve", nc.vector),
                        ("pool", nc.gpsimd), ("sp", nc.sync)):
            sem = es.enter_context(nc.semaphore("sem_" + name))
            self.engs[name] = Eng(name, h, sem)
        self.dma_sems = {}
        self.dma_cnt = {}
        self.dma_i = {}
        for q, n in (("sp", 24), ("pool", 24), ("act", 8)):
            self.dma_sems[q] = [es.enter_context(nc.semaphore("dsem_%s%d" % (q, i))) for i in range(n)]
            self.dma_cnt[q] = [0] * n
            self.dma_i[q] = 0
        self.out_clock = {}
        self.n_inst = 0

    def _wait(self, eng, need):
        for sem, val in need.items():
            if eng.waited.get(sem, 0) < val:
                eng.prog.append(("w", sem, val))
                eng.waited[sem] = val
                eng.n_waits += 1

    def op(self, ename, fn, reads=(), writes=(), signal=True):
        eng = self.engs[ename]
        need = {}
        for r in reads:
            for s, v in r.w.items():
                if s is eng.sem and ename == "pe":
                    continue
                if need.get(s, 0) < v:
                    need[s] = v
            if r.psum:
                for s, v in r.r.items():
                    if s is eng.sem:
                        continue
                    if need.get(s, 0) < v:
                        need[s] = v
        for w in writes:
            for d in (w.w, w.r):
                for s, v in d.items():
                    if s is eng.sem and ename == "pe":
                        continue
                    if need.get(s, 0) < v:
                        need[s] = v
        self._wait(eng, need)
        self.n_inst += 1
        eng.n_ops += 1
        if signal:
            eng.count += 1
            eng.prog.append(("o", fn, eng.sem, 1))
            idx = eng.count
        else:
            eng.prog.append(("o", fn, None, 0))
            idx = eng.count + 1
        inst = None
        for w in writes:
            w.w = {eng.sem: idx}
            w.r = {}
        for r in reads:
            if r.r.get(eng.sem, 0) < idx:
                r.r[eng.sem] = idx
        return inst

    def dma(self, q, out, in_, reads=(), writes=(), is_output=False, **kw):
        eng = self.engs[q]
        need = {}
        for r in reads:
            for s, v in r.w.items():
                if need.get(s, 0) < v:
                    need[s] = v
        for w in writes:
            for d in (w.w, w.r):
                for s, v in d.items():
                    if need.get(s, 0) < v:
                        need[s] = v
        self._wait(eng, need)
        i = self.dma_i[q] % len(self.dma_sems[q])
        self.dma_i[q] += 1
        sem = self.dma_sems[q][i]
        if self.dma_cnt[q][i] > 0:
            self._wait(eng, {sem: self.dma_cnt[q][i]})
        self.dma_cnt[q][i] += 16
        val = self.dma_cnt[q][i]
        eng.prog.append(("o", (lambda e, out=out, in_=in_, kw=kw: e.dma_start(out=out, in_=in_, **kw)), sem, 16))
        self.n_inst += 1
        eng.n_ops += 1
        for w in writes:
            w.w = {sem: val}
            w.r = {}
        for r in reads:
            if r.r.get(sem, 0) < val:
                r.r[sem] = val
        if is_output:
            self.out_clock[sem] = max(self.out_clock.get(sem, 0), val)

    def _rw(self, outs, ins):
        writes = [o.res for o in outs if isinstance(o, T)]
        reads = [i.res for i in ins if isinstance(i, T)]
        return reads, writes

    def mm(self, out, lhsT, rhs, start=True, stop=True, signal=None):
        reads, writes = self._rw([out], [lhsT, rhs])
        if not start:
            reads = reads
        return self.op("pe", lambda e: e.matmul(_aps(out), lhsT=_aps(lhsT), rhs=_aps(rhs), start=start, stop=stop),
                       reads, writes, signal=(stop if signal is None else signal))

    def transpose(self, out, in_, ident, signal=True):
        reads, writes = self._rw([out], [in_, ident])
        return self.op("pe", lambda e: e.transpose(_aps(out), _aps(in_), _aps(ident)), reads, writes, signal=signal)

    def act(self, out, in_, func, bias=None, scale=None, accum_out=None):
        ins = [in_]
        kw = {}
        if bias is not None:
            kw["bias"] = _aps(bias)
            ins.append(bias)
        if scale is not None:
            kw["scale"] = _aps(scale)
            ins.append(scale)
        outs = [out]
        if accum_out is not None:
            kw["accum_out"] = _aps(accum_out)
            outs.append(accum_out)
        reads, writes = self._rw(outs, ins)
        return self.op("act", lambda e: e.activation(out=_aps(out), in_=_aps(in_), func=func, **kw), reads, writes)

    def tt(self, out, in0, in1, op, eng="dve"):
        reads, writes = self._rw([out], [in0, in1])
        return self.op(eng, lambda e: e.tensor_tensor(out=_aps(out), in0=_aps(in0), in1=_aps(in1), op=op), reads, writes)

    def ts(self, out, in0, s1, s2, op0, op1=None, eng="dve"):
        reads, writes = self._rw([out], [in0, s1, s2])
        if op1 is None:
            return self.op(eng, lambda e: e.tensor_single_scalar(out=_aps(out), in_=_aps(in0), scalar=_aps(s1), op=op0),
                           reads, writes)
        return self.op(eng, lambda e: e.tensor_scalar(out=_aps(out), in0=_aps(in0), scalar1=_aps(s1), scalar2=_aps(s2),
                                                      op0=op0, op1=op1), reads, writes)

    def stt(self, out, in0, scalar, in1, op0, op1, eng="dve"):
        reads, writes = self._rw([out], [in0, scalar, in1])
        return self.op(eng, lambda e: e.scalar_tensor_tensor(out=_aps(out), in0=_aps(in0), scalar=_aps(scalar),
                                                             in1=_aps(in1), op0=op0, op1=op1), reads, writes)

    def copy(self, out, in_, eng="dve"):
        reads, writes = self._rw([out], [in_])
        if eng == "act":
            return self.op("act", lambda e: e.activation(out=_aps(out), in_=_aps(in_), func=AF.Copy), reads, writes)
        return self.op(eng, lambda e: e.tensor_copy(out=_aps(out), in_=_aps(in_)), reads, writes)

    def recip(self, out, in_):
        reads, writes = self._rw([out], [in_])
        return self.op("dve", lambda e: e.reciprocal(out=_aps(out), in_=_aps(in_)), reads, writes)

    def memset(self, out, val, eng="dve"):
        reads, writes = self._rw([out], [])
        return self.op(eng, lambda e: e.memset(_aps(out), val), reads, writes)

    def finish(self):
        eng = self.engs["sp"]
        self._wait(eng, self.out_clock)
        need = {}
        for n in ("pe", "act", "dve", "pool"):
            e = self.engs[n]
            if e.count > 0:
                need[e.sem] = e.count
        self._wait(eng, need)
        self.emit()

    def emit(self):
        nc = self.nc

        def replay(eng, h):
            for it in eng.prog:
                if it[0] == "w":
                    h.wait_ge(it[1], it[2])
                else:
                    inst = it[1](h)
                    if it[2] is not None:
                        inst.then_inc(it[2], it[3])

        with nc.Block() as block:
            @block.tensor
            def _(e):
                replay(self.engs["pe"], e)

            @block.scalar
            def _(e):
                replay(self.engs["act"], e)

            @block.vector
            def _(e):
                replay(self.engs["dve"], e)

            @block.gpsimd
            def _(e):
                replay(self.engs["pool"], e)

            @block.sync
            def _(e):
                replay(self.engs["sp"], e)


class Arena:
    def __init__(self, ap_f32, nbytes):
        self.base = ap_f32
        self.size = nbytes
        self.free = [(0, nbytes)]
        self.hist = []

    def alloc(self, nbytes, name=""):
        nbytes = (nbytes + 63) // 64 * 64
        for i, (s, e) in enumerate(self.free):
            if e - s >= nbytes:
                self.free[i] = (s + nbytes, e)
                if self.free[i][0] == self.free[i][1]:
                    del self.free[i]
                break
        else:
            raise RuntimeError("SBUF arena OOM for %s (%d bytes); free=%s" % (name, nbytes, self.free))
        res = Res(name)
        st, en = s, s + nbytes
        keep = []
        for (hs, he, hr) in self.hist:
            if hs < en and st < he:
                for d in (hr.w, hr.r):
                    for k, v in d.items():
                        if res.w.get(k, 0) < v:
                            res.w[k] = v
                if hs >= st and he <= en:
                    continue
            keep.append((hs, he, hr))
        self.hist = keep
        return Tile(self, st, nbytes, res)

    def release(self, tile):
        s, e = tile.start, tile.start + tile.nbytes
        self.hist.append((s, e, tile.res))
        self.free.append((s, e))
        self.free.sort()
        merged = []
        for (a, b) in self.free:
            if merged and merged[-1][1] == a:
                merged[-1] = (merged[-1][0], b)
            else:
                merged.append((a, b))
        self.free = merged


class Tile:
    def __init__(self, arena, start, nbytes, res):
        self.arena = arena
        self.start = start
        self.nbytes = nbytes
        self.res = res

    def f32(self, n=None):
        n = self.nbytes // 4 if n is None else n
        return T(self.arena.base[:, self.start // 4: self.start // 4 + n], self.res)

    def bf(self, n=None):
        n = self.nbytes // 2 if n is None else n
        ap = self.arena.base[:, self.start // 4: self.start // 4 + (n + 1) // 2].bitcast(BF16)
        return T(ap[:, 0:n], self.res)

    def free(self):
        self.arena.release(self)


class Ring:
    def __init__(self, arena, n, nbytes, name):
        self.tiles = [arena.alloc(nbytes, "%s%d" % (name, i)) for i in range(n)]
        self.i = 0

    def next(self):
        t = self.tiles[self.i % len(self.tiles)]
        self.i += 1
        return t

    def free(self):
        for t in self.tiles:
            t.free()


def _host_consts():
    c = {}
    c["ident_f"] = np.eye(128, dtype=np.float32)
    p = np.arange(128)
    d = p % 64
    a = d // 32
    f = d % 16
    t = np.arange(DEC_SEQ)
    inv = (10000.0 ** (-np.arange(16, dtype=np.float32) / 16)).astype(np.float32)
    pos = np.where(a[:, None] == 0, (t // 64)[None, :], (t % 64)[None, :]).astype(np.float32)
    ang = pos * inv[f][:, None]
    c["rope_cos"] = np.cos(ang).astype(np.float32)
    c["rope_sin"] = np.sin(ang).astype(np.float32)
    P = np.zeros((128, 128), np.float32)
    for m in range(128):
        dm = m % 64
        b = (dm % 32) // 16
        if b == 0:
            P[m, m + 16] = -1.0
        else:
            P[m, m - 16] = 1.0
    c["rotT"] = np.ascontiguousarray(P.T)
    bd = np.zeros((128, 128), np.float32)
    bd[:64, :64] = 1.0 / 64
    bd[64:, 64:] = 1.0 / 64
    c["bd64"] = bd
    def valid(i, jb):
        mk = np.zeros((128, 128), np.float32)
        for a in range(2):
            for b in range(2):
                kr = 2 * jb + a
                r = 2 * i + b
                r0 = min(max(r - 4, 0), 24)
                if not (r0 <= kr < r0 + 8):
                    continue
                cc = np.arange(64)
                c0 = np.clip(cc - 8, 0, 48)
                kc = np.arange(64)
                ok = (kc[:, None] >= c0[None, :]) & (kc[:, None] < c0[None, :] + 16)
                mk[a * 64:(a + 1) * 64, b * 64:(b + 1) * 64] = ok
        return mk
    cls = [(6, 6 + d) for d in range(-2, 3)] + [(0, j) for j in range(4)] + [(1, j) for j in range(4)] + \
          [(14, j) for j in range(12, 16)] + [(15, j) for j in range(12, 16)]
    pvs = np.zeros((128, 3, 128), np.float32)
    pvs[64, 0, :] = 1.0
    for d_ in range(64):
        pvs[d_, 1, d_] = 1.0
        pvs[d_, 2, 64 + d_] = 1.0
    c["pv_sel"] = pvs
    wa = np.zeros((128, 8, 240), np.float32)
    for a in range(8):
        for hh in range(16):
            wa[a * 16 + hh, a, hh + 112] = 1.0
    c["ssm_wa"] = wa
    sh = np.arange(128) // 16
    c["ssm_maskF"] = (sh[:, None] <= sh[None, :]).astype(np.float32)
    c["ssm_maskB"] = (sh[:, None] >= sh[None, :]).astype(np.float32)
    c["na_mask"] = np.ascontiguousarray(np.stack([valid(i, j) for (i, j) in cls], 1))
    return c


class Builder:
    def __init__(self, debug=None, wplan=None):
        self.wplan = wplan
        self.wrec = []
        self.debug = debug or []
        self.nc = bass.Bass("TRN2", target_bir_lowering=False)
        self.dbg_out = {}

    def din(self, name, shape, dt=F32):
        return self.nc.dram_tensor(name, list(shape), dt, kind="ExternalInput").ap()

    def dout(self, name, shape, dt=F32):
        return self.nc.dram_tensor(name, list(shape), dt, kind="ExternalOutput").ap()

    def dscr(self, name, shape, dt=F32):
        return self.nc.dram_tensor(name, list(shape), dt).ap()

    def build(self):
        nc = self.nc
        I = {}
        I["xin"] = self.din("xin", [NTOK, D])
        I["cvecT"] = self.din("cvecT", [128, 8, 2])
        I["w_mod"] = self.din("w_mod", [DEPTH, D, 6 * D])
        I["b_modT"] = self.din("b_modT", [128, DEPTH, 48])
        I["norm_gT"] = self.din("norm_gT", [128, DEPTH, 4, 8])
        I["w_in"] = self.din("w_in", [DEPTH, D, IN_W])
        I["qk_gT"] = self.din("qk_gT", [128, DEPTH, 2])
        for k, shp in (("ident_f", [128, 128]), ("rope_cos", [128, DEC_SEQ]), ("rope_sin", [128, DEC_SEQ]),
                       ("rotT", [128, 128]), ("bd64", [128, 128])):
            I[k] = self.din(k, shp)
        for nm, shp in (("w_br_a", [DEPTH, 512, D]), ("w_br_b", [DEPTH, 512, D]), ("w_br_c", [DEPTH, 512, D]),
                        ("w_out", [DEPTH, D, D]), ("w_up", [DEPTH, D, 2 * D_FF]), ("w_down", [DEPTH, D_FF, D]),
                        ("conv_wT", [128, DEPTH, 3, 44]), ("conv_bT", [128, DEPTH, 44])):
            I[nm] = self.din(nm, shp)
        for nm, shp in (("ssm_wa", [128, 8, 240]), ("ssm_maskF", [128, 128]), ("ssm_maskB", [128, 128]),
                        ("ssm_lam_re", [DEPTH, 2, 32, 64]), ("ssm_lam_im", [DEPTH, 2, 32, 64]),
                        ("ssm_log_step", [DEPTH, 2, 32]), ("ssm_b_re", [DEPTH, 2, 32, 64, 16]),
                        ("ssm_b_im", [DEPTH, 2, 32, 64, 16]), ("ssm_c_re", [DEPTH, 2, 512, 64]),
                        ("ssm_c_im", [DEPTH, 2, 512, 64]), ("ssm_d", [DEPTH, 512]), ("w_glu", [DEPTH, 512, 512]),
                        ("state_re", [DEPTH, 2, 32, 64]), ("state_im", [DEPTH, 2, 32, 64])):
            I[nm] = self.din(nm, shp)
        I["pv_sel"] = self.din("pv_sel", [128, 3, 128])
        I["cache_na_k"] = self.din("cache_na_k", [DEPTH, 256, 512])
        I["cache_na_v"] = self.din("cache_na_v", [DEPTH, 256, 512])
        I["na_rpb"] = self.din("na_rpb", [DEPTH, 8, 15, 31])
        I["na_mask"] = self.din("na_mask", [128, 21, 128])
        I["cache_ga_k"] = self.din("cache_ga_k", [DEPTH, 256, 128])
        I["cache_ga_v"] = self.din("cache_ga_v", [DEPTH, 256, 128])
        self.I = I
        O = {}
        O["y"] = self.dout("y", [NTOK, D])
        O["ga_k"] = self.dout("ga_k", [4, DEPTH, 256, 128])
        O["ga_v"] = self.dout("ga_v", [4, DEPTH, 256, 128])
        O["na_k"] = self.dout("na_k", [4, DEPTH, 256, 512])
        O["na_v"] = self.dout("na_v", [4, DEPTH, 256, 512])
        O["ssm_re"] = self.dout("ssm_re", [4, DEPTH, 2, 32, 64])
        O["ssm_im"] = self.dout("ssm_im", [4, DEPTH, 2, 32, 64])
        self.ssmw = self.dscr("ssmw", [DEPTH, 7, 32, 128, 128], BF16)
        self.ssmw_res = Res("ssmw")
        self.vp = self.dscr("na_vp", [8 * 15 * 127 + 128])
        self.vp_res = Res("na_vp")
        self.O = O
        for name, shape in self.debug:
            self.dbg_out[name] = self.dout("dbg_" + name, shape)
        self.xs = self.dscr("xs", [8, 128, NTOK])
        self.xs_res = [[Res("xs%d_%d" % (k, t)) for t in range(NTOK // 512)] for k in range(8)]

        with ExitStack() as es:
            arena_t = es.enter_context(nc.sbuf_tensor("arena", [128, 52000], F32))
            ps_t = es.enter_context(nc.psum_tensor("ps", [128, 4096], F32))
            self.ctx = Ctx(nc, es)
            self.arena = Arena(arena_t[:, :], 52000 * 4)
            self.ps_banks = [T(ps_t[:, b * 512:(b + 1) * 512], Res("ps%d" % b, psum=True)) for b in range(8)]
            self.ps_pi = {}
            self.program()
            self.ctx.finish()
        return nc

    PS_POOLS = {"mm": (0, 1, 2, 3), "acc": (4, 5), "aux": (6, 7)}

    def psum(self, pool="mm"):
        banks = self.PS_POOLS[pool]
        i = self.ps_pi.get(pool, 0)
        self.ps_pi[pool] = i + 1
        return self.ps_banks[banks[i % len(banks)]]

    W_SLOTS = 14
    W_AHEAD = 7

    def wring_init(self, nslots=None, slot_bytes=2048):
        nslots = nslots or self.W_SLOTS
        self.wslots = [self.arena.alloc(slot_bytes, "wslot%d" % i) for i in range(nslots)]
        self.w_i = 0
        self.w_issued = 0
        self.w_tiles = {}

    def _wissue(self, i, srcs, kchunks):
        slot = self.wslots[i % len(self.wslots)]
        ncols = sum(a.shape[1] for a in srcs)
        assert kchunks * ncols * 2 <= slot.nbytes
        t = slot.bf(kchunks * ncols)
        dst = t.ap.rearrange("p (k c) -> p k c", k=kchunks)
        o = 0
        for a in srcs:
            n = a.shape[1]
            self.ctx.dma("pool", dst[:, :, o:o + n], a.rearrange("(k p) c -> p k c", p=128), writes=[t.res])
            o += n
        self.w_tiles[i] = T(dst, t.res)

    def wload(self, src_ap, kchunks, ncols=None):
        srcs = list(src_ap) if isinstance(src_ap, (list, tuple)) else [src_ap]
        idx = self.w_i
        self.w_i += 1
        if self.wplan is None:
            self.wrec.append(([(a.tensor.name, a.offset, [list(x) for x in a.ap]) for a in srcs], kchunks))
            self._wissue(idx, srcs, kchunks)
            return self.w_tiles.pop(idx)
        rec = self.wplan[idx]
        assert rec[1] == kchunks and rec[0][0][1] == srcs[0].offset and rec[0][0][0] == srcs[0].tensor.name
        hi = min(idx + self.W_AHEAD, len(self.wplan) - 1)
        while self.w_issued <= hi:
            r_srcs, r_k = self.wplan[self.w_issued]
            aps = [bass.AP(self.I[nm].tensor, off, apl) for (nm, off, apl) in r_srcs]
            self._wissue(self.w_issued, aps, r_k)
            self.w_issued += 1
        return self.w_tiles.pop(idx)

    def program(self):
        c = self.ctx
        A = self.arena
        I = self.I
        cst = A.alloc(4 * (128 + 64 + 64 + 64 + 64 + 64), "consts")
        base = cst.f32()
        self.ident_f = base[:, 0:128]
        o = 128
        self.ident_b = T(base.ap[:, o:o + 64].bitcast(BF16), cst.res); o += 64
        self.ones_b = T(base.ap[:, o:o + 64].bitcast(BF16), cst.res); o += 64
        self.bd64_b = T(base.ap[:, o:o + 64].bitcast(BF16), cst.res); o += 64
        self.rot_b = T(base.ap[:, o:o + 64].bitcast(BF16), cst.res); o += 64
        c.dma("sp", self.ident_f.ap, I["ident_f"], writes=[cst.res])
        c.dma("pool", self.ident_b.ap, I["ident_f"], writes=[cst.res])
        c.dma("pool", self.bd64_b.ap, I["bd64"], writes=[cst.res])
        c.dma("pool", self.rot_b.ap, I["rotT"], writes=[cst.res])
        c.memset(self.ones_b, 1.0 / 1024)
        pvt = A.alloc(3 * 128 * 2, "pvsel")
        self.pvsel = T(pvt.bf().ap.rearrange("p (w m) -> p w m", w=3), pvt.res)
        c.dma("pool", self.pvsel.ap, I["pv_sel"], writes=[pvt.res])
        lot = A.alloc(512 * 2, "pvlo")
        self.pvlo = lot.bf()
        c.memset(self.pvlo, 0.0)
        zt = A.alloc(512 * 2, "zeros")
        self.zeros_b = zt.bf()
        c.memset(self.zeros_b, 0.0)
        qkg = A.alloc(4 * DEPTH * 2, "qkg")
        self.qkg = T(qkg.f32(DEPTH * 2).ap.rearrange("p (l j) -> p l j", l=DEPTH), qkg.res)
        c.dma("sp", qkg.f32(DEPTH * 2).ap, I["qk_gT"].rearrange("p l j -> p (l j)"), writes=[qkg.res])
        cw = A.alloc(4 * DEPTH * 4 * 44, "convw")
        cwv = cw.f32(DEPTH * 4 * 44).ap.rearrange("p (l j c) -> p l j c", l=DEPTH, j=4)
        self.convw = T(cwv, cw.res)
        for l in range(DEPTH):
            c.dma("sp", cwv[:, l, 0:3, :], I["conv_wT"][:, l, :, :], writes=[cw.res])
            c.dma("sp", cwv[:, l, 3, :], I["conv_bT"][:, l, :], writes=[cw.res])
        self.wring_init()
        self.r_st32 = Ring(A, 3, 128 * 4, "rst32")
        self.lam8 = []
        import os
        for l in range(DEPTH if not os.environ.get("NOS5") else 0):
            self.ssm_prep(l)
        self.adaln()
        import os
        for grp in ((0, 1) if not os.environ.get("G0") else (0,)):
            self.group_pass(grp)

    def adaln(self):
        c = self.ctx
        A = self.arena
        I = self.I
        mod_t = A.alloc(4 * DEPTH * 2 * 6 * 8, "modS")
        self.modS = T(mod_t.f32().ap.rearrange("p (l n i k) -> p l n i k", l=DEPTH, n=2, i=6), mod_t.res)
        tmp = A.alloc(4 * 288, "adaln_tmp")
        tb = tmp.f32()
        cT = tb[:, 0:16]
        bm = tb[:, 16:112]
        raw = tb[:, 112:208]
        ng = tb[:, 208:272]
        scb = T(tb.ap[:, 272:280].bitcast(BF16), tmp.res)
        c.dma("sp", cT.ap, I["cvecT"].rearrange("p k n -> p (k n)"), writes=[tmp.res])
        c.dma("sp", bm.ap, I["b_modT"].rearrange("p l c -> p (l c)"), writes=[tmp.res])
        c.dma("sp", ng.ap, I["norm_gT"].rearrange("p l i k -> p (l i k)"), writes=[tmp.res])
        c.act(scb, cT, AF.Silu)
        scb3 = scb.ap.rearrange("p (k n) -> p k n", n=2)
        ngv = ng.ap.rearrange("p (l i k) -> p l i k", l=DEPTH, i=4)
        for l in range(DEPTH):
            ps = self.psum()
            loads = []
            for cc in range(48):
                loads.append(lambda cc=cc: self.wload(I["w_mod"][l][:, cc * 128:(cc + 1) * 128], 8, 128))
            pend = []
            DEPTHQ = 4
            for cc in range(48 + DEPTHQ):
                if cc < 48:
                    pend.append(loads[cc]())
                if cc >= DEPTHQ:
                    j = cc - DEPTHQ
                    w = pend[j]
                    for k in range(8):
                        c.mm(ps[:, 2 * j:2 * j + 2], w[:, k, :], T(scb3[:, k, :], scb.res), start=(k == 0), stop=(k == 7))
            rawv = raw.ap.rearrange("p (c n) -> p c n", n=2)
            bmv = bm.ap.rearrange("p (l c) -> p l c", l=DEPTH)[:, l, :]
            for n in range(2):
                c.tt(T(rawv[:, :, n], tmp.res), T(ps.ap[:, 0:96].rearrange("p (c n) -> p c n", n=2)[:, :, n], ps.res),
                     T(bmv, tmp.res), ALU.add)
            for n in range(2):
                def R(i):
                    return T(rawv[:, i * 8:(i + 1) * 8, n], tmp.res)
                m = self.modS
                c.stt(T(m.ap[:, l, n, 0, :], m.res), R(1), 1.0, T(ngv[:, l, 0, :], tmp.res), ALU.add, ALU.mult)
                c.copy(T(m.ap[:, l, n, 1, :], m.res), R(0))
                c.tt(T(m.ap[:, l, n, 2, :], m.res), R(2), T(ngv[:, l, 1, :], tmp.res), ALU.mult)
                c.stt(T(m.ap[:, l, n, 3, :], m.res), R(4), 1.0, T(ngv[:, l, 2, :], tmp.res), ALU.add, ALU.mult)
                c.copy(T(m.ap[:, l, n, 4, :], m.res), R(3))
                c.tt(T(m.ap[:, l, n, 5, :], m.res), R(5), T(ngv[:, l, 3, :], tmp.res), ALU.mult)
        if "modS" in self.dbg_out:
            c.dma("sp", self.dbg_out["modS"], mod_t.f32().ap, reads=[mod_t.res], is_output=True)
        tmp.free()

    def group_pass(self, grp):
        self.grp = grp
        self.t0 = 0 if grp == 0 else NP_TOK
        self.nt = NP_TOK if grp == 0 else DEC_SEQ
        self.ntt = self.nt // 512
        A = self.arena
        c = self.ctx
        self.r_sq = Ring(A, 3, 512 * 2, "rsq")
        self.r_f32 = Ring(A, 6, 512 * 4, "rf32")
        self.r_small = Ring(A, 4, 64, "rsmall")
        self._pass_t0 = None
        for l in range(DEPTH):
            self.hT = [A.alloc(8 * 512 * 2, "hT%d" % i) for i in range(self.ntt)]
            self.norm_in(l, first=(l == 0), which=0)
            if l == 0 and ("hT%d" % grp) in self.dbg_out:
                for i, h in enumerate(self.hT):
                    self.ctx.dma("pool", self.dbg_out["hT%d" % grp][:, :, i * 512:(i + 1) * 512],
                                 h.bf().ap.rearrange("p (k t) -> p k t", k=8), reads=[h.res], is_output=True)
            self.mixer(l)
            self.ffn(l)
            for h in self.h2T:
                h.free()
        for r in (self.r_sq, self.r_f32, self.r_small):
            r.free()

    def qk_norm_rope(self, ps, l, which, rope_off, out_bf, f32_out=None):
        c = self.ctx
        sq = self.r_sq.next().bf()
        c.act(sq, ps, AF.Square)
        ps2 = self.psum()
        c.mm(ps2, self.bd64_b, sq)
        rs = self.r_f32.next().f32()
        c.act(rs, ps2, AF.Ln, bias=EPS, scale=1.0)
        c.act(rs, rs, AF.Exp, scale=-0.5)
        qn = self.r_f32.next().f32()
        c.tt(qn, ps, rs, ALU.mult)
        g = T(self.qkg.ap[:, l, which:which + 1], self.qkg.res)
        if rope_off is None:
            if f32_out is not None:
                c.act(f32_out, qn, AF.Identity, scale=g)
                c.copy(out_bf, f32_out)
            else:
                c.act(out_bf, qn, AF.Identity, scale=g)
            return
        qg = self.r_sq.next().bf()
        c.act(qg, qn, AF.Identity, scale=g)
        ps3 = self.psum()
        c.mm(ps3, self.rot_b, qg)
        t1 = self.r_f32.next().f32()
        c.tt(t1, qg, self.rope_cos[:, rope_off:rope_off + 512], ALU.mult)
        t2 = self.r_f32.next().f32()
        c.tt(t2, ps3, self.rope_sin[:, rope_off:rope_off + 512], ALU.mult)
        c.tt(out_bf, t1, t2, ALU.add)

    def mixer(self, l):
        A = self.arena
        self.yaT = [A.alloc(4 * 512 * 2, "yaT%d" % i) for i in range(self.ntt)]
        self.mixer_ga(l)
        if ("ya%d" % self.grp) in self.dbg_out and l == 0:
            for i, y in enumerate(self.yaT):
                self.ctx.dma("pool", self.dbg_out["ya%d" % self.grp][:, :, i * 512:(i + 1) * 512],
                             y.bf().ap.rearrange("p (k t) -> p k t", k=4), reads=[y.res], is_output=True)
        import os
        if os.environ.get("NOS5"):
            self.ybT = [A.alloc(4 * 512 * 2, "ybT%d" % i) for i in range(self.ntt)]
            for y in self.ybT:
                self.ctx.memset(y.bf(), 0.0)
        else:
            self.mixer_s5(l)
        self.ycT = [A.alloc(4 * 512 * 2, "ycT%d" % i) for i in range(self.ntt)]
        if os.environ.get("NONA"):
            for y in self.ycT:
                self.ctx.memset(y.bf(), 0.0)
        else:
            self.mixer_na(l)
        self.merge(l)
        for y in self.yaT + self.ybT + self.ycT + self.hT:
            y.free()
        self.h2T = [A.alloc(8 * 512 * 2, "h2T%d" % i) for i in range(self.ntt)]
        self.proj_out_residual(l, self.mT, "w_out", 8, 2, self.h2T, last=False)
        for m_ in self.mT:
            m_.free()

    def merge(self, l):
        c = self.ctx
        A = self.arena
        I = self.I
        self.mT = [A.alloc(8 * 512 * 2, "mT%d" % i) for i in range(self.ntt)]
        ys = (self.yaT, self.ybT, self.ycT)
        wn = ("w_br_a", "w_br_b", "w_br_c")
        for fo in range(8):
            wb = [self.wload(I[wn[b]][l][:, fo * 128:(fo + 1) * 128], 4) for b in range(3)]
            wg = [self.wload(I["w_in"][l][:, O_G + b * 1024 + fo * 128: O_G + b * 1024 + (fo + 1) * 128], 8)
                  for b in range(3)]
            for tt in range(self.ntt):
                h = self.hT[tt]
                hv = h.bf().ap.rearrange("p (k t) -> p k t", k=8)
                acc = self.r_f32.next().f32()
                for b in range(3):
                    y = ys[b][tt]
                    yv = y.bf().ap.rearrange("p (k t) -> p k t", k=4)
                    pp = self.psum()
                    for k in range(4):
                        c.mm(pp, wb[b][:, k, :], T(yv[:, k, :], y.res), start=(k == 0), stop=(k == 3))
                    pg = self.psum()
                    for k in range(8):
                        c.mm(pg, wg[b][:, k, :], T(hv[:, k, :], h.res), start=(k == 0), stop=(k == 7))
                    sg = self.r_f32.next().f32()
                    c.act(sg, pg, AF.Sigmoid)
                    if b == 0:
                        c.tt(acc, sg, pp, ALU.mult)
                    else:
                        c.tt(sg, sg, pp, ALU.mult)
                        if b == 1:
                            c.tt(acc, acc, sg, ALU.add)
                        else:
                            mv = self.mT[tt].bf().ap.rearrange("p (k t) -> p k t", k=8)
                            c.tt(T(mv[:, fo, :], self.mT[tt].res), acc, sg, ALU.add)

    def proj_out_residual(self, l, inT, wname, nk, ig, h2T, last):
        c = self.ctx
        A = self.arena
        I = self.I
        n = self.grp
        m = self.modS
        for tt in range(len(inT)):
            gt = self.t0 // 512 + tt if not hasattr(self, "_pass_t0") or self._pass_t0 is None else self._pass_t0 // 512 + tt
            src = inT[tt]
            sv = src.bf().ap.rearrange("p (k t) -> p k t", k=nk)
            ot = A.alloc(8 * 512 * 4, "oT")
            ov = ot.f32().ap.rearrange("p (k t) -> p k t", k=8)
            xt = A.alloc(8 * 512 * 4, "xT")
            xv = xt.f32().ap.rearrange("p (k t) -> p k t", k=8)
            c.dma("sp", xv, self.xs[:, :, gt * 512:(gt + 1) * 512].rearrange("k p t -> p k t"),
                  reads=[self.xs_res[k][gt] for k in range(8)], writes=[xt.res])
            msq = self.psum("acc")
            for fo in range(8):
                ps = self.psum()
                k0 = 0
                first = True
                while k0 < nk:
                    kn = min(8, nk - k0)
                    w = self.wload(I[wname][l][k0 * 128:(k0 + kn) * 128, fo * 128:(fo + 1) * 128], kn)
                    for k in range(kn):
                        c.mm(ps, w[:, k, :], T(sv[:, k0 + k, :], src.res), start=first, stop=(k0 + k == nk - 1))
                        first = False
                    k0 += kn
                c.copy(T(ov[:, fo, :], ot.res), ps, eng="act")
                sq = self.r_sq.next().bf()
                c.tt(sq, T(ov[:, fo, :], ot.res), T(ov[:, fo, :], ot.res), ALU.mult)
                c.mm(msq, self.ones_b, sq, start=(fo == 0), stop=(fo == 7), signal=True)
            rs = self.r_f32.next().f32()
            c.act(rs, msq, AF.Ln, bias=EPS, scale=1.0)
            c.act(rs, rs, AF.Exp, scale=-0.5)
            for k in range(8):
                c.tt(T(ov[:, k, :], ot.res), T(ov[:, k, :], ot.res), rs, ALU.mult)
                c.stt(T(xv[:, k, :], xt.res), T(ov[:, k, :], ot.res), T(m.ap[:, l, n, ig, k:k + 1], m.res),
                      T(xv[:, k, :], xt.res), ALU.mult, ALU.add)
                if not (last and l == DEPTH - 1):
                    c.dma("sp", self.xs[k][:, gt * 512:(gt + 1) * 512], xv[:, k, :], reads=[xt.res],
                          writes=[self.xs_res[k][gt]])
            ot.free()
            if not last:
                self.norm_tile(l, xt, h2T[tt], 3, 4)
            elif l == DEPTH - 1:
                for j in range(4):
                    orow = A.alloc(1024 * 4, "orow")
                    for k in range(8):
                        pt = self.psum("aux")
                        c.transpose(pt[:, 0:128], T(xv[:, k, j * 128:(j + 1) * 128], xt.res), self.ident_f)
                        c.copy(orow.f32()[:, k * 128:(k + 1) * 128], pt[:, 0:128], eng=("act" if k % 2 else "dve"))
                    c.dma("sp", self.O["y"][gt * 512 + j * 128: gt * 512 + (j + 1) * 128, :], orow.f32().ap,
                          reads=[orow.res], is_output=True)
                    orow.free()
            xt.free()

    def ffn(self, l):
        c = self.ctx
        A = self.arena
        I = self.I
        cwv = self.convw
        self.r_ub = Ring(A, 2, 1026 * 4 + 56, "rub")
        self.r_v = Ring(A, 3, 1024 * 4, "rv")
        for p0 in range(0, self.nt, 1024):
            tts = [p0 // 512, p0 // 512 + 1]
            act = A.alloc(22 * 1024 * 2, "ffn_act")
            av = act.bf().ap.rearrange("p (j t) -> p j t", j=22)
            nseg = 4 if self.grp == 0 else 1
            L = 1024 // nseg
            for j in range(22):
                wa = self.wload(I["w_up"][l][:, j * 128:(j + 1) * 128], 8)
                wg = self.wload(I["w_up"][l][:, D_FF + j * 128: D_FF + (j + 1) * 128], 8)
                vs = []
                for which, w in ((0, wa), (1, wg)):
                    cc = j + 22 * which
                    ub = self.r_ub.next()
                    ubv = ub.f32(1026)
                    for ti, tt in enumerate(tts):
                        h = self.h2T[tt]
                        hv = h.bf().ap.rearrange("p (k t) -> p k t", k=8)
                        ps = self.psum()
                        for k in range(8):
                            c.mm(ps, w[:, k, :], T(hv[:, k, :], h.res), start=(k == 0), stop=(k == 7))
                        c.copy(ubv[:, 1 + ti * 512: 1 + (ti + 1) * 512], ps, eng=("act" if ti == 0 else "dve"))
                    for side, col in ((0, 0), (1, 1025)):
                        tok = p0 - 1 if side == 0 else p0 + 1024
                        if self.grp == 0 or tok < 0 or tok >= self.nt:
                            c.memset(ubv[:, col:col + 1], 0.0)
                        else:
                            h = self.h2T[tok // 512]
                            hv = h.bf().ap.rearrange("p (k t) -> p k t", k=8)
                            ps = self.psum("aux")
                            for k in range(8):
                                c.mm(ps[:, 0:1], w[:, k, :], T(hv[:, k, tok % 512: tok % 512 + 1], h.res),
                                     start=(k == 0), stop=(k == 7))
                            c.copy(ubv[:, col:col + 1], ps[:, 0:1])
                    v = self.r_v.next().f32(1024)
                    c.act(v, ubv[:, 1:1025], AF.Identity, bias=T(cwv.ap[:, l, 3, cc:cc + 1], cwv.res),
                          scale=T(cwv.ap[:, l, 1, cc:cc + 1], cwv.res))
                    v3 = T(v.ap.rearrange("p (s t) -> p s t", s=nseg), v.res)
                    ul = T(ubv.ap[:, 1:1025].rearrange("p (s t) -> p s t", s=nseg), ub.res)
                    if nseg == 1:
                        c.stt(v, ubv[:, 0:1024], T(cwv.ap[:, l, 0, cc:cc + 1], cwv.res), v, ALU.mult, ALU.add)
                        c.stt(v, ubv[:, 2:1026], T(cwv.ap[:, l, 2, cc:cc + 1], cwv.res), v, ALU.mult, ALU.add)
                    else:
                        c.stt(v3[:, :, 1:L], ul[:, :, 0:L - 1], T(cwv.ap[:, l, 0, cc:cc + 1], cwv.res), v3[:, :, 1:L],
                              ALU.mult, ALU.add)
                        c.stt(v3[:, :, 0:L - 1], ul[:, :, 1:L], T(cwv.ap[:, l, 2, cc:cc + 1], cwv.res),
                              v3[:, :, 0:L - 1], ALU.mult, ALU.add)
                    vs.append(v)
                c.act(vs[1], vs[1], AF.Silu)
                c.tt(T(av[:, j, :], act.res), vs[0], vs[1], ALU.mult)
            self._ffn_down(l, av, act, tts)
            act.free()
        self.r_ub.free()
        self.r_v.free()

    def _ffn_down(self, l, av, act, tts):
        c = self.ctx

        class V:
            def __init__(s_, ap, res):
                s_.ap_ = ap
                s_.res = res

            def bf(s_):
                return s_

            @property
            def ap(s_):
                return s_

            def rearrange(s_, *a, **k):
                return s_.ap_
        inT = [V(av[:, :, ti * 512:(ti + 1) * 512], act.res) for ti in range(2)]
        self._pass_t0 = self.t0 + tts[0] * 512
        self.proj_out_residual(l, inT, "w_down", 22, 5, None, last=True)
        self._pass_t0 = None

    def attn_rings(self, on):
        A = self.arena
        if on:
            self.r_p = Ring(A, 4, 512 * 2, "rp")
            self.r_q = Ring(A, 2, 4 * 512 * 2, "rq")
            self.r_yt = Ring(A, 3, 4 * 64 * 2, "ryt")
            self.r_st = Ring(A, 4, 128 * 4, "rst")
            self.r_osb = Ring(A, 2, 512 * 2, "rosb")
        else:
            for r in (self.r_p, self.r_q, self.r_yt, self.r_st, self.r_osb):
                r.free()

    def mixer_ga(self, l):
        c = self.ctx
        A = self.arena
        I = self.I
        grp = self.grp
        nt = self.nt
        self.attn_rings(True)
        if grp == 1:
            self.rope_t = A.alloc(2 * DEC_SEQ * 4, "rope")
            rv = self.rope_t.f32().ap.rearrange("p (j t) -> p j t", j=2)
            self.rope_cos = T(rv[:, 0, :], self.rope_t.res)
            self.rope_sin = T(rv[:, 1, :], self.rope_t.res)
            c.dma("sp", rv[:, 0, :], self.I["rope_cos"], writes=[self.rope_t.res])
            c.dma("sp", rv[:, 1, :], self.I["rope_sin"], writes=[self.rope_t.res])
        nkeys = nt if grp == 0 else nt + 256
        nblk = nkeys // 128
        W = I["w_in"][l]
        kT = A.alloc(nkeys * 2, "ga_kT")
        kTv = kT.bf()
        va = A.alloc(nblk * 2 * 65 * 2, "ga_vaug")
        vab = va.bf(nblk * 2 * 65)
        vav = vab.ap.rearrange("p (b g e) -> p b g e", b=nblk, g=2)
        c.memset(vab, 1.0)
        wk = self.wload(W[:, O_GK:O_GK + 128], 8)
        wv = self.wload(W[:, O_GV:O_GV + 128], 8)
        import os
        SKIP = os.environ.get("SKIP", "")
        LIM = int(os.environ.get("LIM", "99"))
        for tt in range(min(self.ntt, LIM)):
            h = self.hT[tt]
            hv = h.bf().ap.rearrange("p (k t) -> p k t", k=8)
            if "K" not in SKIP:
                ps = self.psum()
                for k in range(8):
                    c.mm(ps, wk[:, k, :], T(hv[:, k, :], h.res), start=(k == 0), stop=(k == 7))
            if "K" in SKIP:
                pass
            elif "N" in SKIP:
                if "X" not in SKIP:
                    c.copy(kTv[:, tt * 512:(tt + 1) * 512], ps)
            elif grp == 0:
                kf = self.r_f32.next().f32()
                self.qk_norm_rope(ps, l, 1, None, kTv[:, tt * 512:(tt + 1) * 512], f32_out=kf)
                for j in range(4):
                    pt = self.psum("aux")
                    c.transpose(pt[:, 0:128], kf[:, j * 128:(j + 1) * 128], self.ident_f)
                    st = self.r_st.next().f32()
                    c.copy(st, pt[:, 0:128])
                    tok = tt * 512 + j * 128
                    c.dma("sp", self.O["ga_k"][tok // 256, l, tok % 256:tok % 256 + 128, :], st.ap,
                          reads=[st.res], is_output=True)
            else:
                self.qk_norm_rope(ps, l, 1, tt * 512, kTv[:, tt * 512:(tt + 1) * 512])
            for j in range(4 if "V" not in SKIP else 0):
                pv = self.psum("aux" if "A" in SKIP else "mm")
                for k in range(8):
                    c.mm(pv[:, 0:128], T(hv[:, k, j * 128:(j + 1) * 128], h.res), wv[:, k, :],
                         start=(k == 0), stop=(k == 7))
                blk = tt * 4 + j
                c.copy(T(vav[:, blk, :, 0:64], va.res),
                       T(pv.ap[:, 0:128].rearrange("p (g d) -> p g d", g=2), pv.res), eng="act")
                if grp == 0:
                    st = self.r_st.next().f32()
                    c.copy(st, pv[:, 0:128])
                    tok = tt * 512 + j * 128
                    c.dma("sp", self.O["ga_v"][tok // 256, l, tok % 256:tok % 256 + 128, :], st.ap,
                          reads=[st.res], is_output=True)
        if grp == 1 and "C" not in SKIP:
            for j in range(2):
                st = self.r_st.next().f32()
                c.dma("sp", st.ap, I["cache_ga_k"][l, j * 128:(j + 1) * 128, :], writes=[st.res])
                pt = self.psum("aux")
                c.transpose(pt[:, 0:128], st, self.ident_f)
                c.copy(kTv[:, nt + j * 128: nt + (j + 1) * 128], pt[:, 0:128])
            for j in range(2):
                c.dma("pool", vav[:, 16 + j, :, 0:64],
                      I["cache_ga_v"][l, j * 128:(j + 1) * 128, :].rearrange("p (g d) -> p g d", g=2),
                      writes=[va.res])
        import os
        CUT = int(os.environ.get("CUT", "99"))
        for tt in range(self.ntt if CUT > 0 else 0):
            h = self.hT[tt]
            hv = h.bf().ap.rearrange("p (k t) -> p k t", k=8)
            qt = self.r_q.next()
            qv = qt.bf().ap.rearrange("p (h t) -> p h t", h=4)
            for hl in range(4):
                w = self.wload([W[:, O_GQ + hl * 64:O_GQ + hl * 64 + 64],
                                W[:, O_GQ + (4 + hl) * 64:O_GQ + (4 + hl) * 64 + 64]], 8)
                ps = self.psum()
                for k in range(8):
                    c.mm(ps, w[:, k, :], T(hv[:, k, :], h.res), start=(k == 0), stop=(k == 7))
                self.qk_norm_rope(ps, l, 0, (tt * 512 if grp == 1 else None), T(qv[:, hl, :], qt.res))
            yv = self.yaT[tt].bf().ap.rearrange("p (k t) -> p k t", k=4)
            its = []
            for sub in range(4):
                tok0 = tt * 512 + sub * 128
                if grp == 0:
                    sq_ = tok0 // 256
                    blocks = [2 * sq_, 2 * sq_ + 1]
                else:
                    blocks = list(range(nblk))
                for g in range(2):
                    for bi, kb in enumerate(blocks):
                        its.append((sub, g, bi, kb, len(blocks)))

            def emit_score(it):
                sub, g, bi, kb, nb = it
                s_ps = self.psum()
                c.mm(T(s_ps.ap.rearrange("p (h t) -> p h t", h=4), s_ps.res),
                     kTv[g * 64:(g + 1) * 64, kb * 128:(kb + 1) * 128],
                     T(qv[g * 64:(g + 1) * 64, :, sub * 128:(sub + 1) * 128], qt.res))
                return s_ps
            s_next = emit_score(its[0])
            o_ps = None
            for n_, it in enumerate(its):
                sub, g, bi, kb, nb = it
                s_ps = s_next
                if n_ + 1 < len(its):
                    s_next = emit_score(its[n_ + 1])
                pT = self.r_p.next().bf()
                c.act(pT, s_ps, AF.Exp, scale=0.125)
                if bi == 0:
                    o_ps = self.psum("acc")
                c.mm(o_ps[0:65, :], T(vav[:, kb, g, :], va.res), pT, start=(bi == 0), stop=(bi == nb - 1))
                if bi != nb - 1:
                    continue
                osb = self.r_osb.next().bf()
                c.copy(osb[0:65, :], o_ps[0:65, :], eng="act")
                c.tt(self.pvlo[64:65, :], o_ps[64:65, :], osb[64:65, :], ALU.subtract)
                dps = self.psum()
                c.mm(dps, self.pvsel[0:65, 0, :], osb[0:65, :], start=True, stop=False)
                c.mm(dps, self.pvsel[0:65, 0, :], self.pvlo[0:65, :], start=False, stop=True)
                rden = self.r_f32.next().f32()
                c.recip(rden, dps)
                for jp in range(2):
                    pp = self.psum("aux")
                    for e_ in range(2):
                        hl = 2 * jp + e_
                        c.mm(pp[:, 0:128], self.pvsel[0:65, 1 + e_, :], osb[0:65, hl * 128:(hl + 1) * 128],
                             start=(e_ == 0), stop=(e_ == 1))
                    for e_ in range(2):
                        hl = 2 * jp + e_
                        hs = slice(e_ * 64, (e_ + 1) * 64)
                        c.tt(T(yv[hs, 2 * g + jp, sub * 128:(sub + 1) * 128], self.yaT[tt].res), pp[hs, 0:128],
                             rden[hs, hl * 128:(hl + 1) * 128], ALU.mult)
        kT.free()
        va.free()
        self.attn_rings(False)
        if grp == 1:
            self.rope_t.free()

    def mixer_na(self, l):
        c = self.ctx
        A = self.arena
        I = self.I
        grp = self.grp
        nt = self.nt
        nkeys = nt if grp == 0 else nt + 256
        nblk = nkeys // 128
        W = I["w_in"][l]
        self.attn_rings(True)
        if grp == 1:
            zt = self.r_f32.next().f32()
            c.memset(zt, 0.0)
            c.dma("sp", zt.ap[0:120, 48:79], I["na_rpb"][l].rearrange("h r j -> (h r) j"), writes=[zt.res])
            nvp = 8 * 15 * 127
            c.dma("sp", self.vp[0:nvp].rearrange("(a j) -> a j", j=127), zt.ap[0:120, 0:127], reads=[zt.res],
                  writes=[self.vp_res])
            mk = A.alloc(21 * 128 * 2, "na_mask")
            mkv = mk.bf().ap.rearrange("p (c q) -> p c q", c=21)
            c.dma("pool", mkv, I["na_mask"], writes=[mk.res])
            hraw = A.alloc(7 * 2 * 64 * 4, "na_hraw")
            hrv = hraw.f32().ap.rearrange("p (d b c) -> p d b c", d=7, b=2)
            eb = A.alloc(7 * 128 * 2, "na_eb")
            ebv = eb.bf().ap.rearrange("p (d b c) -> p d b c", d=7, b=2)
            etabs = [A.alloc(21 * 128 * 2, "na_etab%d" % i) for i in range(2)]
        for j in range(4):
            kT = A.alloc(nkeys * 2, "na_kT")
            kTv = kT.bf()
            qT = A.alloc(nt * 2, "na_qT")
            qTv = qT.bf()
            va = A.alloc(nblk * 2 * 65 * 2, "na_vaug")
            vab = va.bf(nblk * 2 * 65)
            vav = vab.ap.rearrange("p (b g e) -> p b g e", b=nblk, g=2)
            c.memset(vab, 1.0)
            wq = self.wload(W[:, O_NQ + j * 128:O_NQ + (j + 1) * 128], 8)
            wk = self.wload(W[:, O_NK + j * 128:O_NK + (j + 1) * 128], 8)
            wv = self.wload(W[:, O_NV + j * 128:O_NV + (j + 1) * 128], 8)
            for tt in range(self.ntt):
                h = self.hT[tt]
                hv = h.bf().ap.rearrange("p (k t) -> p k t", k=8)
                ps = self.psum()
                for k in range(8):
                    c.mm(ps, wq[:, k, :], T(hv[:, k, :], h.res), start=(k == 0), stop=(k == 7))
                c.copy(qTv[:, tt * 512:(tt + 1) * 512], ps, eng="act")
                ps = self.psum()
                for k in range(8):
                    c.mm(ps, wk[:, k, :], T(hv[:, k, :], h.res), start=(k == 0), stop=(k == 7))
                if grp == 0:
                    kf = self.r_f32.next().f32()
                    c.copy(kf, ps, eng="act")
                    c.copy(kTv[:, tt * 512:(tt + 1) * 512], kf)
                    for jj in range(4):
                        pt = self.psum("aux")
                        c.transpose(pt[:, 0:128], kf[:, jj * 128:(jj + 1) * 128], self.ident_f)
                        st = self.r_st.next().f32()
                        c.copy(st, pt[:, 0:128])
                        tok = tt * 512 + jj * 128
                        c.dma("sp", self.O["na_k"][tok // 256, l, tok % 256:tok % 256 + 128, j * 128:(j + 1) * 128],
                              st.ap, reads=[st.res], is_output=True)
                else:
                    c.copy(kTv[:, tt * 512:(tt + 1) * 512], ps)
                for jj in range(4):
                    pv = self.psum()
                    for k in range(8):
                        c.mm(pv[:, 0:128], T(hv[:, k, jj * 128:(jj + 1) * 128], h.res), wv[:, k, :],
                             start=(k == 0), stop=(k == 7))
                    blk = tt * 4 + jj
                    c.copy(T(vav[:, blk, :, 0:64], va.res),
                           T(pv.ap[:, 0:128].rearrange("p (g d) -> p g d", g=2), pv.res), eng="act")
                    if grp == 0:
                        st = self.r_st.next().f32()
                        c.copy(st, pv[:, 0:128])
                        tok = tt * 512 + jj * 128
                        c.dma("sp", self.O["na_v"][tok // 256, l, tok % 256:tok % 256 + 128, j * 128:(j + 1) * 128],
                              st.ap, reads=[st.res], is_output=True)
            if grp == 1:
                for b in range(2):
                    st = self.r_st.next().f32()
                    c.dma("sp", st.ap, I["cache_na_k"][l, b * 128:(b + 1) * 128, j * 128:(j + 1) * 128], writes=[st.res])
                    pt = self.psum("aux")
                    c.transpose(pt[:, 0:128], st, self.ident_f)
                    c.copy(kTv[:, nt + b * 128: nt + (b + 1) * 128], pt[:, 0:128])
                    c.dma("pool", vav[:, 16 + b, :, 0:64],
                          I["cache_na_v"][l, b * 128:(b + 1) * 128, j * 128:(j + 1) * 128].rearrange("p (g d) -> p g d", g=2),
                          writes=[va.res])
                for hh in range(2):
                    hd = 2 * j + hh
                    hres = [Res("hraw%d" % q_) for q_ in range(28)]
                    for r_ in hres:
                        r_.w = dict(hraw.res.w)
                        r_.r = dict(hraw.res.r)
                    c.op("dve", lambda e: e.memset(hraw.f32().ap, 0.0), reads=[], writes=hres)
                    for di, dj in enumerate(range(-3, 4)):
                        for a in range(2):
                            for b in range(2):
                                dr = 2 * dj + a - b + 7
                                if 0 <= dr <= 14:
                                    off = (hd * 15 + dr) * 127
                                    src = bass.AP(self.vp.tensor, off, [[1, 64], [1, 64]])
                                    c.dma("sp", hrv[a * 64:(a + 1) * 64, di, b, :], src, reads=[self.vp_res],
                                          writes=[hres[di * 4 + a * 2 + b]])
                    c.op("act", lambda e: e.activation(out=ebv, in_=hrv[:, :, :, ::-1], func=AF.Exp),
                         reads=hres, writes=[eb.res])
                    hraw.res.w = {}
                    hraw.res.r = {}
                    for r_ in hres:
                        for d_, dst_ in ((r_.w, hraw.res.w), (r_.r, hraw.res.r)):
                            for k_, v_ in d_.items():
                                if dst_.get(k_, 0) < v_:
                                    dst_[k_] = v_
                    et = etabs[hh]
                    etv = et.bf().ap.rearrange("p (c q) -> p c q", c=21)
                    ebc = eb.bf().ap.rearrange("p (d q) -> p d q", d=7)
                    for (c0_, n_, d0_) in ((0, 5, 1), (5, 4, 3), (9, 4, 2), (13, 4, 1), (17, 4, 0)):
                        c.tt(T(etv[:, c0_:c0_ + n_, :], et.res), T(ebc[:, d0_:d0_ + n_, :], eb.res),
                             T(mkv[:, c0_:c0_ + n_, :], mk.res), ALU.mult)
            import os
            NST = int(os.environ.get("NA_STAGE", "9"))
            if NST < 2:
                for y in self.ycT:
                    if j == 0:
                        c.memset(y.bf(), 0.0)
            elif grp == 0:
                for seq in range(4):
                    pTs = []
                    for kb in range(2):
                        pT = self.r_p.next().bf()
                        for hh in range(2):
                            s_ps = self.psum()
                            c.mm(s_ps[:, 0:256],
                                 kTv[hh * 64:(hh + 1) * 64, (seq * 2 + kb) * 128:(seq * 2 + kb + 1) * 128],
                                 qTv[hh * 64:(hh + 1) * 64, seq * 256:(seq + 1) * 256])
                            c.act(pT[:, hh * 256:(hh + 1) * 256], s_ps[:, 0:256], AF.Exp, scale=0.125)
                        pTs.append(pT)
                    if NST < 3:
                        if j == 0 and seq == 0:
                            for y in self.ycT:
                                c.memset(y.bf(), 0.0)
                        continue
                    for sub in range(2):
                        o_ps = self.psum("acc")
                        c.mm(o_ps[:, 0:130], self.zeros_b[:, 0:128], self.zeros_b[:, 0:130], start=True, stop=False,
                             signal=False)
                        for kb in range(2):
                            for hh in range(2):
                                last = (kb == 1 and hh == 1)
                                c.mm(o_ps[:, hh * 65:(hh + 1) * 65],
                                     pTs[kb][:, hh * 256 + sub * 128: hh * 256 + (sub + 1) * 128],
                                     T(vav[:, seq * 2 + kb, hh, :], va.res), start=False, stop=last, signal=last)
                        self._na_finish(o_ps, j, seq * 256 + sub * 128)
            else:
                its = []
                for i in range(16):
                    if i == 0:
                        loc, cls0 = [0, 1, 2, 3], 5
                    elif i == 1:
                        loc, cls0 = [0, 1, 2, 3], 9
                    elif i == 14:
                        loc, cls0 = [12, 13, 14, 15], 13
                    elif i == 15:
                        loc, cls0 = [12, 13, 14, 15], 17
                    else:
                        loc, cls0 = list(range(i - 2, i + 3)), 0
                    blocks = loc + [16, 17]
                    for hh in range(2):
                        for part in range(2):
                            its.append((i, hh, part, blocks[0:4] if part == 0 else blocks[4:], cls0, len(loc)))

                def emit_score(it):
                    i, hh, part, bl, cls0, nloc = it
                    s_ps = self.psum()
                    for bi, kb in enumerate(bl):
                        c.mm(s_ps[:, bi * 128:(bi + 1) * 128], kTv[hh * 64:(hh + 1) * 64, kb * 128:(kb + 1) * 128],
                             qTv[hh * 64:(hh + 1) * 64, i * 128:(i + 1) * 128])
                    return s_ps
                LOOK = 2
                pend = [emit_score(its[n_]) for n_ in range(min(LOOK, len(its)))]
                o_ps = None
                for n_, it in enumerate(its):
                    i, hh, part, bl, cls0, nloc = it
                    s_ps = pend.pop(0)
                    if n_ + LOOK < len(its):
                        pend.append(emit_score(its[n_ + LOOK]))
                    etv = etabs[hh].bf().ap.rearrange("p (c q) -> p c q", c=21)
                    pT = self.r_p.next().bf()
                    nb = len(bl)
                    c.act(pT[:, 0:nb * 128], s_ps[:, 0:nb * 128], AF.Exp, scale=0.125)
                    if part == 0:
                        nl = 4
                        c.tt(pT[:, 0:nl * 128], pT[:, 0:nl * 128],
                             T(etv[:, cls0:cls0 + nl, :].rearrange("p c q -> p (c q)"), etabs[hh].res), ALU.mult)
                    elif nloc == 5:
                        c.tt(pT[:, 0:128], pT[:, 0:128], T(etv[:, 4, :], etabs[hh].res), ALU.mult)
                    if hh == 0 and part == 0:
                        o_ps = self.psum("acc")
                        c.mm(o_ps[:, 0:130], self.zeros_b[:, 0:128], self.zeros_b[:, 0:130], start=True, stop=False,
                             signal=False)
                    for bi, kb in enumerate(bl):
                        last = (hh == 1 and part == 1 and bi == nb - 1)
                        c.mm(o_ps[:, hh * 65:(hh + 1) * 65], pT[:, bi * 128:(bi + 1) * 128],
                             T(vav[:, kb, hh, :], va.res), start=False, stop=last, signal=last)
                    if hh == 1 and part == 1:
                        self._na_finish(o_ps, j, i * 128)
            kT.free()
            qT.free()
            va.free()
        if grp == 1:
            for t_ in [mk, hraw, eb] + etabs:
                t_.free()
        self.attn_rings(False)

    def _na_finish(self, o_ps, j, tok0):
        c = self.ctx
        ov = o_ps.ap[:, 0:130].rearrange("p (h e) -> p h e", h=2)
        rec = self.r_small.next().f32()[:, 0:2]
        c.recip(rec, T(ov[:, :, 64], o_ps.res))
        yt = self.r_yt.next()
        ytv = yt.bf(128).ap.rearrange("p (h e) -> p h e", h=2)
        c.tt(T(ytv, yt.res), T(ov[:, :, 0:64], o_ps.res),
             T(rec.ap.unsqueeze(2).broadcast_to([128, 2, 64]), rec.res), ALU.mult)
        tp = self.psum("aux")
        tpb = T(tp.ap.bitcast(BF16)[:, 0:128], tp.res)
        c.transpose(tpb, yt.bf(128), self.ident_b)
        tt = tok0 // 512
        yv = self.ycT[tt].bf().ap.rearrange("p (k t) -> p k t", k=4)
        c.copy(T(yv[:, j, tok0 % 512: tok0 % 512 + 128], self.ycT[tt].res), tpb)

    def _cmul(self, o_re, o_im, a_re, a_im, b_re, b_im, tmp):
        c = self.ctx
        c.tt(o_re, a_re, b_re, ALU.mult)
        c.tt(tmp, a_im, b_im, ALU.mult)
        c.tt(o_re, o_re, tmp, ALU.subtract)
        c.tt(o_im, a_re, b_im, ALU.mult)
        c.tt(tmp, a_im, b_re, ALU.mult)
        c.tt(o_im, o_im, tmp, ALU.add)

    def _trig(self, out, bb, shift, tmp_v, tmp_n):
        c = self.ctx
        PI = math.pi
        c.ts(tmp_v, bb, shift + PI, None, ALU.add)
        c.ts(tmp_n, tmp_v, 2 * PI, None, ALU.is_ge)
        c.stt(tmp_n, tmp_v, 4 * PI, tmp_n, ALU.is_ge, ALU.add)
        c.stt(tmp_n, tmp_v, 6 * PI, tmp_n, ALU.is_ge, ALU.add)
        c.stt(tmp_v, tmp_n, -2 * PI, tmp_v, ALU.mult, ALU.add)
        c.act(out, tmp_v, AF.Sin, bias=self.negpi, scale=1.0)

    def ssm_prep(self, l):
        c = self.ctx
        A = self.arena
        I = self.I
        if not hasattr(self, "negpi"):
            t_ = A.alloc(64, "negpi")
            self.negpi = t_.f32()[:, 0:1]
            c.memset(self.negpi, -math.pi)
        small = A.alloc(4 * 32 * 24, "s5small")
        sm = small.f32().ap.rearrange("p (i g) -> p i g", g=32)

        def S(i):
            return T(sm[:, i, :], small.res)
        pw = A.alloc(4 * 2 * 16 * 32, "s5pw")
        pwv = pw.f32().ap.rearrange("p (r k g) -> p r k g", r=2, k=16)

        def PW(r, k):
            return T(pwv[:, r, k + 7, :], pw.res)
        bt = A.alloc(4 * 2 * 512, "s5B")
        btv = bt.f32().ap.rearrange("p (r g h) -> p r g h", r=2, g=32)
        ct = A.alloc(4 * 2 * 512, "s5C")
        ctv = ct.f32().ap.rearrange("p (r g h) -> p r g h", r=2, g=32)
        bb_t = A.alloc(4 * 2 * 512, "s5Bbar")
        bbv = bb_t.f32().ap.rearrange("p (r g h) -> p r g h", r=2, g=32)
        pr = A.alloc(4 * 4096, "s5pr")
        pi_ = A.alloc(4 * 4096, "s5pi")
        tmp = A.alloc(4 * 4096, "s5tmp")
        xd = A.alloc(2 * 4096, "s5X")
        yd = A.alloc(2 * 4096, "s5Y")
        zt = [A.alloc(2 * 4096, "s5ZT%d" % i) for i in range(2)]
        macc = A.alloc(4 * 4096, "s5M")
        cin = [A.alloc(2 * 4096, "s5Cin%d" % i) for i in range(4)]
        for t_ in cin:
            c.memset(t_.bf(), 0.0)
        msk = A.alloc(4 * 256, "s5mask")
        mskv = msk.f32().ap.rearrange("p (d c) -> p d c", d=2)
        c.dma("sp", mskv[:, 0, :], I["ssm_maskF"], writes=[msk.res])
        c.dma("sp", mskv[:, 1, :], I["ssm_maskB"], writes=[msk.res])
        dcol = A.alloc(4 * 32, "s5dcol")
        for s_ in range(8):
            c.dma("sp", dcol.f32().ap[s_ * 16:(s_ + 1) * 16, :], I["ssm_d"][l].rearrange("(g h) -> h g", h=16),
                  writes=[dcol.res], allow_slow_non_contiguous=True)
        lam8 = A.alloc(4 * 2 * 2 * 32, "lam8_%d" % l)
        l8v = lam8.f32().ap.rearrange("p (a r g) -> p a r g", a=2, r=2)
        self.lam8.append(T(l8v, lam8.res))
        r4 = lambda t_: t_.f32().ap.rearrange("p (g s h) -> p g s h", g=32, s=8)
        r4b = lambda t_: t_.bf().ap.rearrange("p (g s h) -> p g s h", g=32, s=8)
        for d in range(2):
            for ri, nm in ((0, "ssm_lam_re"), (1, "ssm_lam_im")):
                st = self.r_st32.next().f32()
                for hf in range(2):
                    c.dma("sp", st.ap[0:32, hf * 64:(hf + 1) * 64], I[nm][l, d], writes=[st.res])
                pt = self.psum("aux")
                c.transpose(pt[:, 0:32], st[0:32, :], self.ident_f[0:32, 0:32])
                c.copy(S(ri), pt[:, 0:32])
            c.dma("sp", sm[:, 2, :], I["ssm_log_step"][l, d:d + 1, :].partition_broadcast(128)
                  if hasattr(I["ssm_log_step"], "partition_broadcast") else I["ssm_log_step"][l, d:d + 1, :].broadcast_to([128, 32]),
                  writes=[small.res])
            for ri, nm in ((0, "ssm_b_re"), (1, "ssm_b_im")):
                for hf in range(2):
                    c.dma("sp", btv[hf * 64:(hf + 1) * 64, ri, :, :], I[nm][l, d].rearrange("g p h -> p g h"),
                          writes=[bt.res])
            for ri, nm in ((0, "ssm_c_re"), (1, "ssm_c_im")):
                for ch in range(4):
                    st = self.r_st32.next().f32()
                    for hf in range(2):
                        c.dma("sp", st.ap[:, hf * 64:(hf + 1) * 64], I[nm][l, d, ch * 128:(ch + 1) * 128, :],
                              writes=[st.res])
                    pt = self.psum("aux")
                    c.transpose(pt[:, 0:128], st, self.ident_f)
                    c.copy(T(ctv[:, ri, ch * 8:(ch + 1) * 8, :], ct.res),
                           T(pt.ap[:, 0:128].rearrange("p (g h) -> p g h", g=8), pt.res))
            c.act(S(2), S(2), AF.Exp)
            c.tt(S(3), S(0), S(2), ALU.mult)
            c.tt(S(4), S(1), S(2), ALU.mult)
            c.act(S(5), S(3), AF.Exp)
            c.act(S(6), S(3), AF.Exp, scale=-1.0)
            self._trig(S(7), S(4), 0.0, S(9), S(10))
            self._trig(S(8), S(4), math.pi / 2, S(9), S(10))
            c.memset(PW(0, 0), 1.0)
            c.memset(PW(1, 0), 0.0)
            c.tt(PW(0, 1), S(5), S(8), ALU.mult)
            c.tt(PW(1, 1), S(5), S(7), ALU.mult)
            c.tt(PW(0, -1), S(6), S(8), ALU.mult)
            c.tt(S(11), S(6), S(7), ALU.mult)
            c.ts(PW(1, -1), S(11), -1.0, None, ALU.mult)
            for k in range(1, 8):
                self._cmul(PW(0, k + 1), PW(1, k + 1), PW(0, k), PW(1, k), PW(0, 1), PW(1, 1), S(12))
            for k in range(1, 7):
                self._cmul(PW(0, -k - 1), PW(1, -k - 1), PW(0, -k), PW(1, -k), PW(0, -1), PW(1, -1), S(12))
            c.tt(S(15), S(0), S(0), ALU.mult)
            c.tt(S(16), S(1), S(1), ALU.mult)
            c.tt(S(15), S(15), S(16), ALU.add)
            c.recip(S(15), S(15))
            c.ts(S(16), PW(0, 1), -1.0, None, ALU.add)
            c.tt(S(17), S(16), S(0), ALU.mult)
            c.tt(S(18), PW(1, 1), S(1), ALU.mult)
            c.tt(S(17), S(17), S(18), ALU.add)
            c.tt(S(13), S(17), S(15), ALU.mult)
            c.tt(S(17), PW(1, 1), S(0), ALU.mult)
            c.tt(S(18), S(16), S(1), ALU.mult)
            c.tt(S(17), S(17), S(18), ALU.subtract)
            c.tt(S(14), S(17), S(15), ALU.mult)

            def bc_g(t_):
                return T(t_.ap.unsqueeze(2).broadcast_to([128, 32, 16]), t_.res)
            t512 = T(tmp.f32().ap[:, 0:512].rearrange("p (g h) -> p g h", g=32), tmp.res)
            self._cmul(T(bbv[:, 0], bb_t.res), T(bbv[:, 1], bb_t.res), bc_g(S(13)), bc_g(S(14)),
                       T(btv[:, 0], bt.res), T(btv[:, 1], bt.res), t512)
            hs = slice(d * 64, (d + 1) * 64)
            c.copy(T(l8v[hs, 0, 0, :], lam8.res), PW(0, 8)[hs])
            c.copy(T(l8v[hs, 0, 1, :], lam8.res), PW(0, 8)[hs])
            c.copy(T(l8v[hs, 1, 1, :], lam8.res), PW(1, 8)[hs])
            c.ts(T(l8v[hs, 1, 0, :], lam8.res), PW(1, 8)[hs], -1.0, None, ALU.mult)

            def pw_view(k0, step):
                base_re = pwv[:, 0, k0 + 7, :]
                base_im = pwv[:, 1, k0 + 7, :]
                def mk(b_):
                    return T(bass.AP(b_.tensor, b_.offset, [list(b_.ap[0]), [1, 32], [32 * step, 8], [0, 16]]), pw.res)
                return mk(base_re), mk(base_im)

            def mat_view(v4):
                def mk(r):
                    b_ = v4[:, r]
                    return T(bass.AP(b_.tensor, b_.offset, [list(b_.ap[0]), [16, 32], [0, 8], [1, 16]]), None)
                return mk(0), mk(1)
            prv, piv, tmv = T(r4(pr), pr.res), T(r4(pi_), pi_.res), T(r4(tmp), tmp.res)
            a_re, a_im = pw_view(7, -1) if d == 0 else pw_view(0, 1)
            b_re, b_im = mat_view(bbv)
            b_re.res = b_im.res = bb_t.res
            self._cmul(prv, piv, a_re, a_im, b_re, b_im, tmv)
            xv, yv_ = r4b(xd), r4b(yd)
            c.copy(T(xv[0:64], xd.res), prv[0:64])
            c.copy(T(xv[64:128], xd.res), piv[64:128], eng="act")
            c.copy(T(r4b(zt[0])[hs], zt[0].res), prv[hs])
            c.copy(T(r4b(zt[1])[hs], zt[1].res), piv[hs], eng="act")
            a_re, a_im = pw_view(-7, 1) if d == 0 else pw_view(0, -1)
            b_re, b_im = mat_view(ctv)
            b_re.res = b_im.res = ct.res
            self._cmul(prv, piv, a_re, a_im, b_re, b_im, tmv)
            c.copy(T(yv_[0:64], yd.res), prv[0:64])
            c.ts(T(yv_[64:128], yd.res), piv[64:128], -1.0, None, ALU.mult)
            mv = macc.f32().ap.rearrange("p (g c) -> p g c", g=32)
            xf = xd.bf().ap.rearrange("p (g c) -> p g c", g=32)
            yf = yd.bf().ap.rearrange("p (g c) -> p g c", g=32)
            for g in range(32):
                ps = self.psum()
                c.mm(ps[:, 0:128], T(xf[:, g, :], xd.res), T(yf[:, g, :], yd.res))
                if d == 0:
                    c.tt(T(mv[:, g, :], macc.res), ps[:, 0:128], T(mskv[:, 0, :], msk.res), ALU.mult)
                else:
                    t128 = T(tmp.f32().ap[:, 0:128], tmp.res)
                    c.tt(t128, ps[:, 0:128], T(mskv[:, 1, :], msk.res), ALU.mult)
                    c.tt(T(mv[:, g, :], macc.res), T(mv[:, g, :], macc.res), t128, ALU.add)
            a_re, a_im = pw_view(1, 1) if d == 0 else pw_view(8, -1)
            self._cmul(prv, piv, a_re, a_im, b_re, b_im, tmv)
            c.copy(T(r4b(cin[2 * d])[hs], cin[2 * d].res), prv[hs])
            c.ts(T(r4b(cin[2 * d + 1])[hs], cin[2 * d + 1].res), piv[hs], -1.0, None, ALU.mult)
        mv = macc.f32().ap.rearrange("p (g c) -> p g c", g=32)
        mb = A.alloc(2 * 4096, "s5Mb")
        mbv = mb.bf().ap.rearrange("p (g c) -> p g c", g=32)
        for g in range(32):
            c.stt(T(mbv[:, g, :], mb.res), self.ident_f, T(dcol.f32().ap[:, g:g + 1], dcol.res), T(mv[:, g, :], macc.res),
                  ALU.mult, ALU.add)
        c.dma("sp", self.ssmw[l, 0].rearrange("g p c -> p g c"), mbv, reads=[mb.res], writes=[self.ssmw_res])
        for ci in range(4):
            c.dma("sp", self.ssmw[l, 3 + ci].rearrange("g p c -> p g c"),
                  cin[ci].bf().ap.rearrange("p (g c) -> p g c", g=32), reads=[cin[ci].res], writes=[self.ssmw_res])
        for ri in range(2):
            zf = zt[ri].bf().ap.rearrange("p (g c) -> p g c", g=32)
            zo = A.alloc(2 * 4096, "s5Zo")
            zov = zo.bf().ap.rearrange("p (g c) -> p g c", g=32)
            for g in range(32):
                tp = self.psum("aux")
                tpb = T(tp.ap.bitcast(BF16)[:, 0:128], tp.res)
                c.transpose(tpb, T(zf[:, g, :], zt[ri].res), self.ident_b)
                c.copy(T(zov[:, g, :], zo.res), tpb, eng=("act" if g % 2 else "dve"))
            c.dma("sp", self.ssmw[l, 1 + ri].rearrange("g p c -> p g c"), zov, reads=[zo.res], writes=[self.ssmw_res])
            zo.free()
        for t_ in [small, pw, bt, ct, bb_t, pr, pi_, tmp, xd, yd, macc, msk, dcol, mb] + zt + cin:
            t_.free()

    def mixer_s5(self, l):
        c = self.ctx
        A = self.arena
        I = self.I
        grp = self.grp
        nt = self.nt
        N = nt // 8
        nseq = 4 if grp == 0 else 1
        cps = N // nseq
        W = I["w_in"][l]
        wa = A.alloc(8 * 240 * 2, "s5wa")
        wav = wa.bf().ap.rearrange("p (a j) -> p a j", a=8)
        c.dma("pool", wav, I["ssm_wa"], writes=[wa.res])
        zu = A.alloc(4 * nt * 2, "s5zu")
        zuv = zu.bf().ap.rearrange("p (k t) -> p k t", k=4)
        for kc in range(4):
            w = self.wload(W[:, O_U + kc * 128:O_U + (kc + 1) * 128], 8)
            for tt in range(self.ntt):
                h = self.hT[tt]
                hv = h.bf().ap.rearrange("p (k t) -> p k t", k=8)
                ps = self.psum()
                for k in range(8):
                    c.mm(ps, w[:, k, :], T(hv[:, k, :], h.res), start=(k == 0), stop=(k == 7))
                c.copy(T(zuv[:, kc, tt * 512:(tt + 1) * 512], zu.res), ps, eng=("act" if tt % 2 else "dve"))
        U = A.alloc(32 * N * 2, "s5U")
        Uv = U.bf().ap.rearrange("p (g n) -> p g n", g=32)
        for g in range(32):
            kc, gl = g // 8, g % 8
            ps = self.psum()
            zs = zuv[:, kc, :].rearrange("p (n s) -> p s n", s=8)
            for s_ in range(8):
                c.mm(ps[:, 0:N], T(wav[:, gl, 112 - 16 * s_: 240 - 16 * s_], wa.res), T(zs[:, s_, :], zu.res),
                     start=(s_ == 0), stop=(s_ == 7))
            c.copy(T(Uv[:, g, :], U.res), ps[:, 0:N], eng=("act" if g % 2 else "dve"))
        zu.free()
        Z = A.alloc(2 * 32 * nseq * (cps + 2) * 2, "s5Z")
        Zv = Z.bf(2 * 32 * nseq * (cps + 2)).ap.rearrange("p (r g q k) -> p r g q k", r=2, g=32, q=nseq)
        c.memset(Z.bf(), 0.0)
        St = [A.alloc(2 * 32 * nseq * 4, "s5S%d" % i) for i in range(2)]
        Sv = [T(t_.f32(2 * 32 * nseq).ap.rearrange("p (r g q) -> p r g q", r=2, g=32), t_.res) for t_ in St]
        if grp == 0:
            c.memset(Sv[0], 0.0)
        else:
            for ri, nm in ((0, "state_re"), (1, "state_im")):
                for d in range(2):
                    st = self.r_st32.next().f32()
                    for hf in range(2):
                        c.dma("sp", st.ap[0:32, hf * 64:(hf + 1) * 64], I[nm][l, d], writes=[st.res])
                    pt = self.psum("aux")
                    c.transpose(pt[:, 0:32], st[0:32, :], self.ident_f[0:32, 0:32])
                    hs = slice(d * 64, (d + 1) * 64)
                    c.copy(T(Sv[0].ap[hs, ri, :, 0], St[0].res), pt[hs, 0:32])
        c.copy(T(Zv[0:64, :, :, :, 0], Z.res), Sv[0][0:64], eng="act")
        c.copy(T(Zv[64:128, :, :, :, cps + 1], Z.res), Sv[0][64:128], eng="act")
        wst = Ring(A, 3, 7 * 128 * 2, "s5wst")
        def wtile(g):
            t_ = wst.next()
            v = t_.bf().ap.rearrange("p (k c) -> p k c", k=7)
            c.dma("sp", v, self.ssmw[l, :, g].rearrange("k p c -> p k c"), reads=[self.ssmw_res], writes=[t_.res])
            return T(v, t_.res)
        wt_n = wtile(0)
        for g in range(32):
            wt = wt_n
            if g + 1 < 32:
                wt_n = wtile(g + 1)
            for ri in range(2):
                ps = self.psum()
                c.mm(ps[:, 0:N], wt[:, 1 + ri, :], T(Uv[:, g, :], U.res))
                pv = ps.ap[:, 0:N].rearrange("p (q k) -> p q k", q=nseq)
                c.copy(T(Zv[:, ri, g, :, 1:cps + 1], Z.res), T(pv, ps.res), eng=("act" if ri else "dve"))
        l8 = self.lam8[l]
        sh_ = [128, 2, 32, nseq]
        lamA = T(l8.ap[:, 0].unsqueeze(3).broadcast_to(sh_), l8.res)
        lamB = T(l8.ap[:, 1].unsqueeze(3).broadcast_to(sh_), l8.res)
        tA = A.alloc(2 * 32 * nseq * 4, "s5tA")
        tB = A.alloc(2 * 32 * nseq * 4, "s5tB")
        tAv = T(tA.f32(2 * 32 * nseq).ap.rearrange("p (r g q) -> p r g q", r=2, g=32), tA.res)
        tBv = T(tB.f32(2 * 32 * nseq).ap.rearrange("p (r g q) -> p r g q", r=2, g=32), tB.res)
        cur = 0
        for k in range(cps):
            S_, Sn = Sv[cur], Sv[1 - cur]
            c.tt(tAv, S_, lamA, ALU.mult)
            c.tt(tBv, T(S_.ap[:, ::-1], S_.res), lamB, ALU.mult)
            c.tt(tAv, tAv, tBv, ALU.add)
            for hs, col in ((slice(0, 64), k + 1), (slice(64, 128), cps - k)):
                c.tt(Sn[hs], tAv[hs], T(Zv[hs, :, :, :, col], Z.res), ALU.add)
                c.copy(T(Zv[hs, :, :, :, col], Z.res), Sn[hs], eng="act")
            cur = 1 - cur
        if grp == 0:
            fin = Sv[cur]
            for ri, nm in ((0, "ssm_re"), (1, "ssm_im")):
                pt = self.psum("aux")
                c.transpose(pt[:, 0:128], T(fin.ap[:, ri].rearrange("p g q -> p (g q)"), fin.res), self.ident_f)
                st = self.r_st32.next().f32()
                c.copy(st, pt[:, 0:128])
                for q in range(4):
                    c.dma("sp", self.O[nm][q, l].rearrange("d g p -> g d p"),
                          st.ap[q::4, :].rearrange("g (d p) -> g d p", d=2), reads=[st.res], is_output=True)
        for t_ in St + [tA, tB]:
            t_.free()
        Y = A.alloc(32 * N * 2, "s5Y")
        Yv = Y.bf().ap.rearrange("p (g n) -> p g n", g=32)
        wt_n = wtile(0)
        for g in range(32):
            wt = wt_n
            if g + 1 < 32:
                wt_n = wtile(g + 1)
            ps = self.psum()
            c.mm(ps[:, 0:N], wt[:, 0, :], T(Uv[:, g, :], U.res), start=True, stop=False)
            psq = T(ps.ap[:, 0:N].rearrange("p (q k) -> p q k", q=nseq), ps.res)
            for ri in range(2):
                c.mm(psq, wt[:, 3 + ri, :], T(Zv[:, ri, g, :, 0:cps], Z.res), start=False, stop=False)
                c.mm(psq, wt[:, 5 + ri, :], T(Zv[:, ri, g, :, 2:cps + 2], Z.res), start=False, stop=(ri == 1))
            c.copy(T(Yv[:, g, :], Y.res), ps[:, 0:N], eng=("act" if g % 2 else "dve"))
        U.free()
        Z.free()
        wst.free()
        gT = A.alloc(4 * nt * 2, "s5g")
        gv = gT.bf().ap.rearrange("p (k t) -> p k t", k=4)
        for kc in range(4):
            for tp_ in range(8):
                ps = self.psum()
                for gl in range(8):
                    c.mm(ps[:, 0:N], T(wav[:, tp_, 112 - 16 * gl: 240 - 16 * gl], wa.res), T(Yv[:, kc * 8 + gl, :], Y.res),
                         start=(gl == 0), stop=(gl == 7))
                x_ = self.r_f32.next().f32()[:, 0:N]
                c.copy(x_, ps[:, 0:N], eng="act")
                u_ = self.r_f32.next().f32()[:, 0:N]
                c.tt(u_, x_, x_, ALU.mult)
                c.ts(u_, u_, 0.044715, 1.0, ALU.mult, ALU.add)
                c.tt(u_, u_, x_, ALU.mult)
                c.act(u_, u_, AF.Sigmoid, scale=1.5957691216057308)
                dst = gv[:, kc, :].rearrange("p (n s) -> p s n", s=8)[:, tp_, :]
                c.tt(T(dst, gT.res), x_, u_, ALU.mult)
        Y.free()
        wa.free()
        self.ybT = [A.alloc(4 * 512 * 2, "ybT%d" % i) for i in range(self.ntt)]
        for ko in range(4):
            w = self.wload(I["w_glu"][l][:, ko * 128:(ko + 1) * 128], 4)
            for tt in range(self.ntt):
                ps = self.psum()
                for k in range(4):
                    c.mm(ps, w[:, k, :], T(gv[:, k, tt * 512:(tt + 1) * 512], gT.res), start=(k == 0), stop=(k == 3))
                sg = self.r_f32.next().f32()
                c.act(sg, ps, AF.Sigmoid)
                yv = self.ybT[tt].bf().ap.rearrange("p (k t) -> p k t", k=4)
                c.tt(T(yv[:, ko, :], self.ybT[tt].res), T(gv[:, ko, tt * 512:(tt + 1) * 512], gT.res), sg, ALU.mult)
        gT.free()

    def load_x_tile(self, tt, first):
        c = self.ctx
        A = self.arena
        I = self.I
        gt = self.t0 // 512 + tt
        xt = A.alloc(8 * 512 * 4, "xT")
        xv = xt.f32().ap.rearrange("p (k t) -> p k t", k=8)
        import os
        if not first or os.environ.get("NOFP32"):
            c.dma("sp", xv, self.xs[:, :, gt * 512:(gt + 1) * 512].rearrange("k p t -> p k t"),
                  reads=[self.xs_res[k][gt] for k in range(8)], writes=[xt.res])
            return xt
        for k in range(8):
            pass
        tm = [A.alloc(1024 * 4, "xtm%d" % j) for j in range(4)]
        for j in range(4):
            c.dma("sp", tm[j].f32().ap, I["xin"][gt * 512 + j * 128: gt * 512 + (j + 1) * 128, :], writes=[tm[j].res])
        for k in range(8):
            ps = self.psum()
            for j in range(4):
                c.transpose(ps[:, j * 128:(j + 1) * 128], tm[j].f32()[:, k * 128:(k + 1) * 128], self.ident_f,
                            signal=(j == 3))
            c.copy(T(xv[:, k, :], xt.res), ps, eng=("act" if k % 2 else "dve"))
            c.dma("sp", self.xs[k][:, gt * 512:(gt + 1) * 512], xv[:, k, :], reads=[xt.res],
                  writes=[self.xs_res[k][gt]])
        for j in range(4):
            tm[j].free()
        return xt

    def norm_in(self, l, first, which):
        c = self.ctx
        A = self.arena
        n = self.grp
        ia, ib = (0, 1) if which == 0 else (3, 4)
        for tt in range(self.ntt):
            xt = self.load_x_tile(tt, first)
            self.norm_tile(l, xt, self.hT[tt], ia, ib)
            xt.free()

    def norm_tile(self, l, xt, hT, ia, ib):
        c = self.ctx
        A = self.arena
        n = self.grp
        if True:
            xv = xt.f32().ap.rearrange("p (k t) -> p k t", k=8)
            sq = A.alloc(512 * 2 * 2, "sq")
            sqv = sq.bf().ap.rearrange("p (b t) -> p b t", b=2)
            ps = self.psum()
            for k in range(8):
                c.act(T(sqv[:, k % 2, :], sq.res), T(xv[:, k, :], xt.res), AF.Square)
                c.mm(ps, self.ones_b, T(sqv[:, k % 2, :], sq.res), start=(k == 0), stop=(k == 7), signal=True)
            rs = A.alloc(512 * 4, "rstd")
            c.act(rs.f32(), ps, AF.Ln, bias=EPS, scale=1.0)
            c.act(rs.f32(), rs.f32(), AF.Exp, scale=-0.5)
            hv = hT.bf().ap.rearrange("p (k t) -> p k t", k=8)
            for k in range(8):
                c.tt(T(xv[:, k, :], xt.res), T(xv[:, k, :], xt.res), rs.f32(), ALU.mult)
                m = self.modS
                c.act(T(hv[:, k, :], hT.res), T(xv[:, k, :], xt.res), AF.Identity,
                      bias=T(m.ap[:, l, n, ib, k:k + 1], m.res), scale=T(m.ap[:, l, n, ia, k:k + 1], m.res))
            sq.free()
            rs.free()


def _prep_inputs(inp, core):
    f = np.float32
    m = {}
    xp = np.asarray(inp["x_prompt"], f)[4 * core:4 * core + 4].reshape(NP_TOK, D)
    xsam = np.asarray(inp["x_sample"], f)[core]
    m["xin"] = np.ascontiguousarray(np.concatenate([xp, xsam], 0))
    cv = np.stack([np.asarray(inp["c_ctx"], f), np.asarray(inp["c"], f)[core]], 0)
    m["cvecT"] = np.ascontiguousarray(cv.reshape(2, 8, 128).transpose(2, 1, 0))
    m["state_re"] = np.ascontiguousarray(np.asarray(inp["state_ssm_re"], f)[core])
    m["state_im"] = np.ascontiguousarray(np.asarray(inp["state_ssm_im"], f)[core])
    m["cache_na_k"] = np.ascontiguousarray(np.asarray(inp["cache_na_k"], f)[core].reshape(DEPTH, 256, 512))
    m["cache_na_v"] = np.ascontiguousarray(np.asarray(inp["cache_na_v"], f)[core].reshape(DEPTH, 256, 512))
    m["cache_ga_k"] = np.ascontiguousarray(np.asarray(inp["cache_ga_k"], f)[core].reshape(DEPTH, 256, 128))
    m["cache_ga_v"] = np.ascontiguousarray(np.asarray(inp["cache_ga_v"], f)[core].reshape(DEPTH, 256, 128))
    return m


def _shared_inputs(inp):
    f = np.float32
    m = {}
    m["w_mod"] = np.ascontiguousarray(np.asarray(inp["w_mod"], f))
    m["b_modT"] = np.ascontiguousarray(np.asarray(inp["b_mod"], f).reshape(DEPTH, 48, 128).transpose(2, 0, 1))
    m["norm_gT"] = np.ascontiguousarray(np.asarray(inp["norm_g"], f).reshape(DEPTH, 4, 8, 128).transpose(3, 0, 1, 2))
    m["w_in"] = np.ascontiguousarray(np.asarray(inp["w_in"], f))
    m["na_rpb"] = np.ascontiguousarray(np.asarray(inp["na_rpb"], f))
    for nm in ("ssm_lam_re", "ssm_lam_im", "ssm_log_step", "ssm_b_re", "ssm_b_im", "ssm_d", "w_glu"):
        m[nm] = np.ascontiguousarray(np.asarray(inp[nm], f))
    m["ssm_c_re"] = np.ascontiguousarray(np.asarray(inp["ssm_c_re"], f).reshape(DEPTH, 2, 512, 64))
    m["ssm_c_im"] = np.ascontiguousarray(np.asarray(inp["ssm_c_im"], f).reshape(DEPTH, 2, 512, 64))
    for nm in ("w_br_a", "w_br_b", "w_br_c", "w_out", "w_up", "w_down"):
        m[nm] = np.ascontiguousarray(np.asarray(inp[nm], f))
    m["conv_wT"] = np.ascontiguousarray(np.asarray(inp["conv_w"], f).reshape(DEPTH, 3, 44, 128).transpose(3, 0, 1, 2))
    m["conv_bT"] = np.ascontiguousarray(np.asarray(inp["conv_b"], f).reshape(DEPTH, 44, 128).transpose(2, 0, 1))
    qk = np.asarray(inp["qk_norm_g"], f)
    m["qk_gT"] = np.ascontiguousarray(np.concatenate([qk, qk], -1).transpose(2, 0, 1))
    m.update(_host_consts())
    return m


_DEBUG = []


def run(inputs, debug=None):
    dbg = debug if debug is not None else _DEBUG
    dry = Builder(debug=dbg)
    dry.build()
    b = Builder(debug=dbg, wplan=dry.wrec)
    nc = b.build()
    shared = _shared_inputs(inputs)
    in_maps = []
    for core in range(NCORES):
        m = dict(shared)
        m.update(_prep_inputs(inputs, core))
        in_maps.append(m)
    res = run_bass_kernel_spmd(nc, in_maps, core_ids=list(range(NCORES)))
    return res.results, b


def kernel(**inputs):
    results, b = run(inputs)
    f = np.float32
    y_p = np.concatenate([r["y"][:NP_TOK].reshape(4, SEQ, D) for r in results], 0).astype(f)
    y_s = np.stack([r["y"][NP_TOK:] for r in results], 0).astype(f)
    ga_k = np.concatenate([r["ga_k"].reshape(4, DEPTH, SEQ, 2, HD) for r in results], 0).astype(f)
    ga_v = np.concatenate([r["ga_v"].reshape(4, DEPTH, SEQ, 2, HD) for r in results], 0).astype(f)
    na_k = np.concatenate([r["na_k"].reshape(4, DEPTH, SEQ, 8, HD) for r in results], 0).astype(f)
    na_v = np.concatenate([r["na_v"].reshape(4, DEPTH, SEQ, 8, HD) for r in results], 0).astype(f)
    s_re = np.concatenate([r["ssm_re"].reshape(4, DEPTH, 2, 32, 64) for r in results], 0).astype(f)
    s_im = np.concatenate([r["ssm_im"].reshape(4, DEPTH, 2, 32, 64) for r in results], 0).astype(f)
    return (y_p, y_s, ga_k, ga_v, na_k, na_v, s_re, s_im)
```

```python
import math
from contextlib import ExitStack

import numpy as np
import ml_dtypes
import concourse.bass as bass
import concourse.mybir as mybir
from concourse.bass_utils import run_bass_kernel_spmd

F32 = mybir.dt.float32
BF16 = mybir.dt.bfloat16
AF = mybir.ActivationFunctionType
ALU = mybir.AluOpType

D = 1024
DEPTH = 2
NCORES = 8
SEQ = 256
DEC_SEQ = 2048
NP_TOK = 1024
NTOK = NP_TOK + DEC_SEQ
HD = 64
IN_W = 5888
D_FF = 2816
EPS = 1e-6
O_GQ, O_GK, O_GV, O_U, O_NQ, O_NK, O_NV, O_G = 0, 512, 640, 768, 1280, 1792, 2304, 2816


class Res:
    __slots__ = ("w", "r", "name", "psum")

    def __init__(self, name="", psum=False):
        self.w = {}
        self.r = {}
        self.name = name
        self.psum = psum


class T:
    __slots__ = ("ap", "res")

    def __init__(self, ap, res):
        self.ap = ap
        self.res = res

    def __getitem__(self, k):
        return T(self.ap[k], self.res)

    def v(self, ap):
        return T(ap, self.res)


class Eng:
    def __init__(self, name, h, sem):
        self.name = name
        self.h = h
        self.sem = sem
        self.count = 0
        self.waited = {}
        self.n_ops = 0
        self.n_waits = 0
        self.prog = []


def _aps(x):
    return x.ap if isinstance(x, T) else x


class Ctx:
    def __init__(self, nc, es, n_dma_sems=40):
        self.nc = nc
        self.engs = {}
        for name, h in (("pe", nc.tensor), ("act", nc.scalar), ("dve", nc.vector),
                        ("pool", nc.gpsimd), ("sp", nc.sync)):
            sem = es.enter_context(nc.semaphore("sem_" + name))
            self.engs[name] = Eng(name, h, sem)
        self.dma_sems = {}
        self.dma_cnt = {}
        self.dma_i = {}
        for q, n in (("sp", 24), ("pool", 24), ("act", 8)):
            self.dma_sems[q] = [es.enter_context(nc.semaphore("dsem_%s%d" % (q, i))) for i in range(n)]
            self.dma_cnt[q] = [0] * n
            self.dma_i[q] = 0
        self.out_clock = {}
        self.n_inst = 0

    def _wait(self, eng, need):
        for sem, val in need.items():
            if eng.waited.get(sem, 0) < val:
                eng.prog.append(("w", sem, val))
                eng.waited[sem] = val
                eng.n_waits += 1

    def op(self, ename, fn, reads=(), writes=(), signal=True):
        eng = self.engs[ename]
        need = {}
        for r in reads:
            for s, v in r.w.items():
                if s is eng.sem and ename == "pe":
                    continue
                if need.get(s, 0) < v:
                    need[s] = v
            if r.psum:
                for s, v in r.r.items():
                    if s is eng.sem:
                        continue
                    if need.get(s, 0) < v:
                        need[s] = v
        for w in writes:
            for d in (w.w, w.r):
                for s, v in d.items():
                    if s is eng.sem and ename == "pe":
                        continue
                    if need.get(s, 0) < v:
                        need[s] = v
        self._wait(eng, need)
        self.n_inst += 1
        eng.n_ops += 1
        if signal:
            eng.count += 1
            eng.prog.append(("o", fn, eng.sem, 1))
            idx = eng.count
        else:
            eng.prog.append(("o", fn, None, 0))
            idx = eng.count + 1
        inst = None
        for w in writes:
            w.w = {eng.sem: idx}
            w.r = {}
        for r in reads:
            if r.r.get(eng.sem, 0) < idx:
                r.r[eng.sem] = idx
        return inst

    def dma(self, q, out, in_, reads=(), writes=(), is_output=False, **kw):
        eng = self.engs[q]
        need = {}
        for r in reads:
            for s, v in r.w.items():
                if need.get(s, 0) < v:
                    need[s] = v
        for w in writes:
            for d in (w.w, w.r):
                for s, v in d.items():
                    if need.get(s, 0) < v:
                        need[s] = v
        self._wait(eng, need)
        i = self.dma_i[q] % len(self.dma_sems[q])
        self.dma_i[q] += 1
        sem = self.dma_sems[q][i]
        if self.dma_cnt[q][i] > 0:
            self._wait(eng, {sem: self.dma_cnt[q][i]})
        self.dma_cnt[q][i] += 16
        val = self.dma_cnt[q][i]
        eng.prog.append(("o", (lambda e, out=out, in_=in_, kw=kw: e.dma_start(out=out, in_=in_, **kw)), sem, 16))
        self.n_inst += 1
        eng.n_ops += 1
        for w in writes:
            w.w = {sem: val}
            w.r = {}
        for r in reads:
            if r.r.get(sem, 0) < val:
                r.r[sem] = val
        if is_output:
            self.out_clock[sem] = max(self.out_clock.get(sem, 0), val)

    def _rw(self, outs, ins):
        writes = [o.res for o in outs if isinstance(o, T)]
        reads = [i.res for i in ins if isinstance(i, T)]
        return reads, writes

    def mm(self, out, lhsT, rhs, start=True, stop=True, signal=None):
        reads, writes = self._rw([out], [lhsT, rhs])
        if not start:
            reads = reads
        return self.op("pe", lambda e: e.matmul(_aps(out), lhsT=_aps(lhsT), rhs=_aps(rhs), start=start, stop=stop),
                       reads, writes, signal=(stop if signal is None else signal))

    def transpose(self, out, in_, ident, signal=True):
        reads, writes = self._rw([out], [in_, ident])
        return self.op("pe", lambda e: e.transpose(_aps(out), _aps(in_), _aps(ident)), reads, writes, signal=signal)

    def act(self, out, in_, func, bias=None, scale=None, accum_out=None):
        ins = [in_]
        kw = {}
        if bias is not None:
            kw["bias"] = _aps(bias)
            ins.append(bias)
        if scale is not None:
            kw["scale"] = _aps(scale)
            ins.append(scale)
        outs = [out]
        if accum_out is not None:
            kw["accum_out"] = _aps(accum_out)
            outs.append(accum_out)
        reads, writes = self._rw(outs, ins)
        return self.op("act", lambda e: e.activation(out=_aps(out), in_=_aps(in_), func=func, **kw), reads, writes)

    def tt(self, out, in0, in1, op, eng="dve"):
        reads, writes = self._rw([out], [in0, in1])
        return self.op(eng, lambda e: e.tensor_tensor(out=_aps(out), in0=_aps(in0), in1=_aps(in1), op=op), reads, writes)

    def ts(self, out, in0, s1, s2, op0, op1=None, eng="dve"):
        reads, writes = self._rw([out], [in0, s1, s2])
        if op1 is None:
            return self.op(eng, lambda e: e.tensor_single_scalar(out=_aps(out), in_=_aps(in0), scalar=_aps(s1), op=op0),
                           reads, writes)
        return self.op(eng, lambda e: e.tensor_scalar(out=_aps(out), in0=_aps(in0), scalar1=_aps(s1), scalar2=_aps(s2),
                                                      op0=op0, op1=op1), reads, writes)

    def stt(self, out, in0, scalar, in1, op0, op1, eng="dve"):
        reads, writes = self._rw([out], [in0, scalar, in1])
        return self.op(eng, lambda e: e.scalar_tensor_tensor(out=_aps(out), in0=_aps(in0), scalar=_aps(scalar),
                                                             in1=_aps(in1), op0=op0, op1=op1), reads, writes)

    def copy(self, out, in_, eng="dve"):
        reads, writes = self._rw([out], [in_])
        if eng == "act":
            return self.op("act", lambda e: e.activation(out=_aps(out), in_=_aps(in_), func=AF.Copy), reads, writes)
        return self.op(eng, lambda e: e.tensor_copy(out=_aps(out), in_=_aps(in_)), reads, writes)

    def recip(self, out, in_):
        reads, writes = self._rw([out], [in_])
        return self.op("dve", lambda e: e.reciprocal(out=_aps(out), in_=_aps(in_)), reads, writes)

    def memset(self, out, val, eng="dve"):
        reads, writes = self._rw([out], [])
        return self.op(eng, lambda e: e.memset(_aps(out), val), reads, writes)

    def finish(self):
        eng = self.engs["sp"]
        self._wait(eng, self.out_clock)
        need = {}
        for n in ("pe", "act", "dve", "pool"):
            e = self.engs[n]
            if e.count > 0:
                need[e.sem] = e.count
        self._wait(eng, need)
        self.emit()

    def emit(self):
        nc = self.nc

        def replay(eng, h):
            for it in eng.prog:
                if it[0] == "w":
                    h.wait_ge(it[1], it[2])
                else:
                    inst = it[1](h)
                    if it[2] is not None:
                        inst.then_inc(it[2], it[3])

        with nc.Block() as block:
            @block.tensor
            def _(e):
                replay(self.engs["pe"], e)

            @block.scalar
            def _(e):
                replay(self.engs["act"], e)

            @block.vector
            def _(e):
                replay(self.engs["dve"], e)

            @block.gpsimd
            def _(e):
                replay(self.engs["pool"], e)

            @block.sync
            def _(e):
                replay(self.engs["sp"], e)


class Arena:
    def __init__(self, ap_f32, nbytes):
        self.base = ap_f32
        self.size = nbytes
        self.free = [(0, nbytes)]
        self.hist = []

    def alloc(self, nbytes, name=""):
        nbytes = (nbytes + 63) // 64 * 64
        for i, (s, e) in enumerate(self.free):
            if e - s >= nbytes:
                self.free[i] = (s + nbytes, e)
                if self.free[i][0] == self.free[i][1]:
                    del self.free[i]
                break
        else:
            raise RuntimeError("SBUF arena OOM for %s (%d bytes); free=%s" % (name, nbytes, self.free))
        res = Res(name)
        st, en = s, s + nbytes
        keep = []
        for (hs, he, hr) in self.hist:
            if hs < en and st < he:
                for d in (hr.w, hr.r):
                    for k, v in d.items():
                        if res.w.get(k, 0) < v:
                            res.w[k] = v
                if hs >= st and he <= en:
                    continue
            keep.append((hs, he, hr))
        self.hist = keep
        return Tile(self, st, nbytes, res)

    def release(self, tile):
        s, e = tile.start, tile.start + tile.nbytes
        self.hist.append((s, e, tile.res))
        self.free.append((s, e))
        self.free.sort()
        merged = []
        for (a, b) in self.free:
            if merged and merged[-1][1] == a:
                merged[-1] = (merged[-1][0], b)
            else:
                merged.append((a, b))
        self.free = merged


class Tile:
    def __init__(self, arena, start, nbytes, res):
        self.arena = arena
        self.start = start
        self.nbytes = nbytes
        self.res = res

    def f32(self, n=None):
        n = self.nbytes // 4 if n is None else n
        return T(self.arena.base[:, self.start // 4: self.start // 4 + n], self.res)

    def bf(self, n=None):
        n = self.nbytes // 2 if n is None else n
        ap = self.arena.base[:, self.start // 4: self.start // 4 + (n + 1) // 2].bitcast(BF16)
        return T(ap[:, 0:n], self.res)

    def free(self):
        self.arena.release(self)


class Ring:
    def __init__(self, arena, n, nbytes, name):
        self.tiles = [arena.alloc(nbytes, "%s%d" % (name, i)) for i in range(n)]
        self.i = 0

    def next(self):
        t = self.tiles[self.i % len(self.tiles)]
        self.i += 1
        return t

    def free(self):
        for t in self.tiles:
            t.free()


def _host_consts():
    c = {}
    c["ident_f"] = np.eye(128, dtype=np.float32)
    p = np.arange(128)
    d = p % 64
    a = d // 32
    f = d % 16
    t = np.arange(DEC_SEQ)
    inv = (10000.0 ** (-np.arange(16, dtype=np.float32) / 16)).astype(np.float32)
    pos = np.where(a[:, None] == 0, (t // 64)[None, :], (t % 64)[None, :]).astype(np.float32)
    ang = pos * inv[f][:, None]
    c["rope_cos"] = np.cos(ang).astype(np.float32)
    c["rope_sin"] = np.sin(ang).astype(np.float32)
    P = np.zeros((128, 128), np.float32)
    for m in range(128):
        dm = m % 64
        b = (dm % 32) // 16
        if b == 0:
            P[m, m + 16] = -1.0
        else:
            P[m, m - 16] = 1.0
    c["rotT"] = np.ascontiguousarray(P.T)
    bd = np.zeros((128, 128), np.float32)
    bd[:64, :64] = 1.0 / 64
    bd[64:, 64:] = 1.0 / 64
    c["bd64"] = bd
    def valid(i, jb):
        mk = np.zeros((128, 128), np.float32)
        for a in range(2):
            for b in range(2):
                kr = 2 * jb + a
                r = 2 * i + b
                r0 = min(max(r - 4, 0), 24)
                if not (r0 <= kr < r0 + 8):
                    continue
                cc = np.arange(64)
                c0 = np.clip(cc - 8, 0, 48)
                kc = np.arange(64)
                ok = (kc[:, None] >= c0[None, :]) & (kc[:, None] < c0[None, :] + 16)
                mk[a * 64:(a + 1) * 64, b * 64:(b + 1) * 64] = ok
        return mk
    cls = [(6, 6 + d) for d in range(-2, 3)] + [(0, j) for j in range(4)] + [(1, j) for j in range(4)] + \
          [(14, j) for j in range(12, 16)] + [(15, j) for j in range(12, 16)]
    wa = np.zeros((128, 8, 240), np.float32)
    for a in range(8):
        for hh in range(16):
            wa[a * 16 + hh, a, hh + 112] = 1.0
    c["ssm_wa"] = wa
    sh = np.arange(128) // 16
    c["ssm_maskF"] = (sh[:, None] <= sh[None, :]).astype(np.float32)
    c["ssm_maskB"] = (sh[:, None] >= sh[None, :]).astype(np.float32)
    c["na_mask"] = np.ascontiguousarray(np.stack([valid(i, j) for (i, j) in cls], 1))
    return c


class Builder:
    def __init__(self, debug=None, wplan=None):
        self.wplan = wplan
        self.wrec = []
        self.debug = debug or []
        self.nc = bass.Bass("TRN2", target_bir_lowering=False)
        self.dbg_out = {}

    def din(self, name, shape, dt=F32):
        return self.nc.dram_tensor(name, list(shape), dt, kind="ExternalInput").ap()

    def dout(self, name, shape, dt=F32):
        return self.nc.dram_tensor(name, list(shape), dt, kind="ExternalOutput").ap()

    def dscr(self, name, shape, dt=F32):
        return self.nc.dram_tensor(name, list(shape), dt).ap()

    def build(self):
        nc = self.nc
        I = {}
        I["xin"] = self.din("xin", [NTOK, D])
        I["cvecT"] = self.din("cvecT", [128, 8, 2])
        I["w_mod"] = self.din("w_mod", [DEPTH, D, 6 * D])
        I["b_modT"] = self.din("b_modT", [128, DEPTH, 48])
        I["norm_gT"] = self.din("norm_gT", [128, DEPTH, 4, 8])
        I["w_in"] = self.din("w_in", [DEPTH, D, IN_W])
        I["qk_gT"] = self.din("qk_gT", [128, DEPTH, 2])
        for k, shp in (("ident_f", [128, 128]), ("rope_cos", [128, DEC_SEQ]), ("rope_sin", [128, DEC_SEQ]),
                       ("rotT", [128, 128]), ("bd64", [128, 128])):
            I[k] = self.din(k, shp)
        for nm, shp in (("w_br_a", [DEPTH, 512, D]), ("w_br_b", [DEPTH, 512, D]), ("w_br_c", [DEPTH, 512, D]),
                        ("w_out", [DEPTH, D, D]), ("w_up", [DEPTH, D, 2 * D_FF]), ("w_down", [DEPTH, D_FF, D]),
                        ("conv_wT", [128, DEPTH, 3, 44]), ("conv_bT", [128, DEPTH, 44])):
            I[nm] = self.din(nm, shp)
        for nm, shp in (("ssm_wa", [128, 8, 240]), ("ssm_maskF", [128, 128]), ("ssm_maskB", [128, 128]),
                        ("ssm_lam_re", [DEPTH, 2, 32, 64]), ("ssm_lam_im", [DEPTH, 2, 32, 64]),
                        ("ssm_log_step", [DEPTH, 2, 32]), ("ssm_b_re", [DEPTH, 2, 32, 64, 16]),
                        ("ssm_b_im", [DEPTH, 2, 32, 64, 16]), ("ssm_c_re", [DEPTH, 2, 512, 64]),
                        ("ssm_c_im", [DEPTH, 2, 512, 64]), ("ssm_d", [DEPTH, 512]), ("w_glu", [DEPTH, 512, 512]),
                        ("state_re", [DEPTH, 2, 32, 64]), ("state_im", [DEPTH, 2, 32, 64])):
            I[nm] = self.din(nm, shp)
        I["cache_na_k"] = self.din("cache_na_k", [DEPTH, 256, 512])
        I["cache_na_v"] = self.din("cache_na_v", [DEPTH, 256, 512])
        I["na_rpb"] = self.din("na_rpb", [DEPTH, 8, 15, 31])
        I["na_mask"] = self.din("na_mask", [128, 21, 128])
        I["cache_ga_k"] = self.din("cache_ga_k", [DEPTH, 256, 128])
        I["cache_ga_v"] = self.din("cache_ga_v", [DEPTH, 256, 128])
        self.I = I
        O = {}
        O["y"] = self.dout("y", [NTOK, D])
        O["ga_k"] = self.dout("ga_k", [4, DEPTH, 256, 128])
        O["ga_v"] = self.dout("ga_v", [4, DEPTH, 256, 128])
        O["na_k"] = self.dout("na_k", [4, DEPTH, 256, 512])
        O["na_v"] = self.dout("na_v", [4, DEPTH, 256, 512])
        O["ssm_re"] = self.dout("ssm_re", [4, DEPTH, 2, 32, 64])
        O["ssm_im"] = self.dout("ssm_im", [4, DEPTH, 2, 32, 64])
        self.ssmw = self.dscr("ssmw", [DEPTH, 7, 32, 128, 128], BF16)
        self.ssmw_res = Res("ssmw")
        self.vp = self.dscr("na_vp", [8 * 15 * 127 + 128])
        self.vp_res = Res("na_vp")
        self.O = O
        for name, shape in self.debug:
            self.dbg_out[name] = self.dout("dbg_" + name, shape)
        self.xs = self.dscr("xs", [8, 128, NTOK])
        self.xs_res = [[Res("xs%d_%d" % (k, t)) for t in range(NTOK // 512)] for k in range(8)]

        with ExitStack() as es:
            arena_t = es.enter_context(nc.sbuf_tensor("arena", [128, 52000], F32))
            ps_t = es.enter_context(nc.psum_tensor("ps", [128, 4096], F32))
            self.ctx = Ctx(nc, es)
            self.arena = Arena(arena_t[:, :], 52000 * 4)
            self.ps_banks = [T(ps_t[:, b * 512:(b + 1) * 512], Res("ps%d" % b, psum=True)) for b in range(8)]
            self.ps_pi = {}
            self.program()
            self.ctx.finish()
        return nc

    PS_POOLS = {"mm": (0, 1, 2, 3), "acc": (4, 5), "aux": (6, 7), "all8": (0, 1, 2, 3, 4, 5, 6, 7)}

    def psum(self, pool="mm"):
        banks = self.PS_POOLS[pool]
        i = self.ps_pi.get(pool, 0)
        self.ps_pi[pool] = i + 1
        return self.ps_banks[banks[i % len(banks)]]

    W_SLOTS = 14
    W_AHEAD = 7

    def wring_init(self, nslots=None, slot_bytes=2048):
        nslots = nslots or self.W_SLOTS
        self.wslots = [self.arena.alloc(slot_bytes, "wslot%d" % i) for i in range(nslots)]
        self.w_i = 0
        self.w_issued = 0
        self.w_tiles = {}

    def _wissue(self, i, srcs, kchunks):
        slot = self.wslots[i % len(self.wslots)]
        ncols = sum(a.shape[1] for a in srcs)
        assert kchunks * ncols * 2 <= slot.nbytes
        t = slot.bf(kchunks * ncols)
        dst = t.ap.rearrange("p (k c) -> p k c", k=kchunks)
        o = 0
        for a in srcs:
            n = a.shape[1]
            self.ctx.dma("pool", dst[:, :, o:o + n], a.rearrange("(k p) c -> p k c", p=128), writes=[t.res])
            o += n
        self.w_tiles[i] = T(dst, t.res)

    def wload(self, src_ap, kchunks, ncols=None):
        srcs = list(src_ap) if isinstance(src_ap, (list, tuple)) else [src_ap]
        idx = self.w_i
        self.w_i += 1
        if self.wplan is None:
            self.wrec.append(([(a.tensor.name, a.offset, [list(x) for x in a.ap]) for a in srcs], kchunks))
            self._wissue(idx, srcs, kchunks)
            return self.w_tiles.pop(idx)
        rec = self.wplan[idx]
        assert rec[1] == kchunks and rec[0][0][1] == srcs[0].offset and rec[0][0][0] == srcs[0].tensor.name
        hi = min(idx + self.W_AHEAD, len(self.wplan) - 1)
        while self.w_issued <= hi:
            r_srcs, r_k = self.wplan[self.w_issued]
            aps = [bass.AP(self.I[nm].tensor, off, apl) for (nm, off, apl) in r_srcs]
            self._wissue(self.w_issued, aps, r_k)
            self.w_issued += 1
        return self.w_tiles.pop(idx)

    def program(self):
        c = self.ctx
        A = self.arena
        I = self.I
        cst = A.alloc(4 * (128 + 64 + 64 + 64 + 64 + 64), "consts")
        base = cst.f32()
        self.ident_f = base[:, 0:128]
        o = 128
        self.ident_b = T(base.ap[:, o:o + 64].bitcast(BF16), cst.res); o += 64
        self.ones_b = T(base.ap[:, o:o + 64].bitcast(BF16), cst.res); o += 64
        self.bd64_b = T(base.ap[:, o:o + 64].bitcast(BF16), cst.res); o += 64
        self.rot_b = T(base.ap[:, o:o + 64].bitcast(BF16), cst.res); o += 64
        c.dma("sp", self.ident_f.ap, I["ident_f"], writes=[cst.res])
        c.dma("pool", self.ident_b.ap, I["ident_f"], writes=[cst.res])
        c.dma("pool", self.bd64_b.ap, I["bd64"], writes=[cst.res])
        c.dma("pool", self.rot_b.ap, I["rotT"], writes=[cst.res])
        c.memset(self.ones_b, 1.0 / 1024)
        zt = A.alloc(512 * 2, "zeros")
        self.zeros_b = zt.bf()
        c.memset(self.zeros_b, 0.0)
        qkg = A.alloc(4 * DEPTH * 2, "qkg")
        self.qkg = T(qkg.f32(DEPTH * 2).ap.rearrange("p (l j) -> p l j", l=DEPTH), qkg.res)
        c.dma("sp", qkg.f32(DEPTH * 2).ap, I["qk_gT"].rearrange("p l j -> p (l j)"), writes=[qkg.res])
        cw = A.alloc(4 * DEPTH * 4 * 44, "convw")
        cwv = cw.f32(DEPTH * 4 * 44).ap.rearrange("p (l j c) -> p l j c", l=DEPTH, j=4)
        self.convw = T(cwv, cw.res)
        for l in range(DEPTH):
            c.dma("sp", cwv[:, l, 0:3, :], I["conv_wT"][:, l, :, :], writes=[cw.res])
            c.dma("sp", cwv[:, l, 3, :], I["conv_bT"][:, l, :], writes=[cw.res])
        self.wring_init()
        self.r_st32 = Ring(A, 3, 128 * 4, "rst32")
        self.lam8 = []
        import os
        for l in range(DEPTH if not os.environ.get("NOS5") else 0):
            self.ssm_prep(l)
        self.adaln()
        import os
        for grp in ((0, 1) if not os.environ.get("G0") else (0,)):
            self.group_pass(grp)

    def adaln(self):
        c = self.ctx
        A = self.arena
        I = self.I
        mod_t = A.alloc(4 * DEPTH * 2 * 6 * 8, "modS")
        self.modS = T(mod_t.f32().ap.rearrange("p (l n i k) -> p l n i k", l=DEPTH, n=2, i=6), mod_t.res)
        tmp = A.alloc(4 * 288, "adaln_tmp")
        tb = tmp.f32()
        cT = tb[:, 0:16]
        bm = tb[:, 16:112]
        raw = tb[:, 112:208]
        ng = tb[:, 208:272]
        scb = T(tb.ap[:, 272:280].bitcast(BF16), tmp.res)
        c.dma("sp", cT.ap, I["cvecT"].rearrange("p k n -> p (k n)"), writes=[tmp.res])
        c.dma("sp", bm.ap, I["b_modT"].rearrange("p l c -> p (l c)"), writes=[tmp.res])
        c.dma("sp", ng.ap, I["norm_gT"].rearrange("p l i k -> p (l i k)"), writes=[tmp.res])
        c.act(scb, cT, AF.Silu)
        scb3 = scb.ap.rearrange("p (k n) -> p k n", n=2)
        ngv = ng.ap.rearrange("p (l i k) -> p l i k", l=DEPTH, i=4)
        for l in range(DEPTH):
            ps = self.psum()
            loads = []
            for cc in range(48):
                loads.append(lambda cc=cc: self.wload(I["w_mod"][l][:, cc * 128:(cc + 1) * 128], 8, 128))
            pend = []
            DEPTHQ = 4
            for cc in range(48 + DEPTHQ):
                if cc < 48:
                    pend.append(loads[cc]())
                if cc >= DEPTHQ:
                    j = cc - DEPTHQ
                    w = pend[j]
                    for k in range(8):
                        c.mm(ps[:, 2 * j:2 * j + 2], w[:, k, :], T(scb3[:, k, :], scb.res), start=(k == 0), stop=(k == 7))
            rawv = raw.ap.rearrange("p (c n) -> p c n", n=2)
            bmv = bm.ap.rearrange("p (l c) -> p l c", l=DEPTH)[:, l, :]
            for n in range(2):
                c.tt(T(rawv[:, :, n], tmp.res), T(ps.ap[:, 0:96].rearrange("p (c n) -> p c n", n=2)[:, :, n], ps.res),
                     T(bmv, tmp.res), ALU.add)
            for n in range(2):
                def R(i):
                    return T(rawv[:, i * 8:(i + 1) * 8, n], tmp.res)
                m = self.modS
                c.stt(T(m.ap[:, l, n, 0, :], m.res), R(1), 1.0, T(ngv[:, l, 0, :], tmp.res), ALU.add, ALU.mult)
                c.copy(T(m.ap[:, l, n, 1, :], m.res), R(0))
                c.tt(T(m.ap[:, l, n, 2, :], m.res), R(2), T(ngv[:, l, 1, :], tmp.res), ALU.mult)
                c.stt(T(m.ap[:, l, n, 3, :], m.res), R(4), 1.0, T(ngv[:, l, 2, :], tmp.res), ALU.add, ALU.mult)
                c.copy(T(m.ap[:, l, n, 4, :], m.res), R(3))
                c.tt(T(m.ap[:, l, n, 5, :], m.res), R(5), T(ngv[:, l, 3, :], tmp.res), ALU.mult)
        if "modS" in self.dbg_out:
            c.dma("sp", self.dbg_out["modS"], mod_t.f32().ap, reads=[mod_t.res], is_output=True)
        tmp.free()

    def group_pass(self, grp):
        self.grp = grp
        self.t0 = 0 if grp == 0 else NP_TOK
        self.nt = NP_TOK if grp == 0 else DEC_SEQ
        self.ntt = self.nt // 512
        A = self.arena
        c = self.ctx
        self.r_sq = Ring(A, 3, 512 * 2, "rsq")
        self.r_f32 = Ring(A, 6, 512 * 4, "rf32")
        self.r_small = Ring(A, 4, 64, "rsmall")
        self._pass_t0 = None
        for l in range(DEPTH):
            self.hT = [A.alloc(8 * 512 * 2, "hT%d" % i) for i in range(self.ntt)]
            self.norm_in(l, first=(l == 0), which=0)
            if l == 0 and ("hT%d" % grp) in self.dbg_out:
                for i, h in enumerate(self.hT):
                    self.ctx.dma("pool", self.dbg_out["hT%d" % grp][:, :, i * 512:(i + 1) * 512],
                                 h.bf().ap.rearrange("p (k t) -> p k t", k=8), reads=[h.res], is_output=True)
            self.mixer(l)
            self.ffn(l)
            for h in self.h2T:
                h.free()
        for r in (self.r_sq, self.r_f32, self.r_small):
            r.free()

    def qk_norm_rope(self, ps, l, which, rope_off, out_bf, f32_out=None):
        c = self.ctx
        sq = self.r_sq.next().bf()
        c.act(sq, ps, AF.Square)
        ps2 = self.psum()
        c.mm(ps2, self.bd64_b, sq)
        rs = self.r_f32.next().f32()
        c.act(rs, ps2, AF.Ln, bias=EPS, scale=1.0)
        c.act(rs, rs, AF.Exp, scale=-0.5)
        qn = self.r_f32.next().f32()
        c.tt(qn, ps, rs, ALU.mult)
        g = T(self.qkg.ap[:, l, which:which + 1], self.qkg.res)
        if rope_off is None:
            if f32_out is not None:
                c.act(f32_out, qn, AF.Identity, scale=g)
                c.copy(out_bf, f32_out)
            else:
                c.act(out_bf, qn, AF.Identity, scale=g)
            return
        qg = self.r_sq.next().bf()
        c.act(qg, qn, AF.Identity, scale=g)
        ps3 = self.psum()
        c.mm(ps3, self.rot_b, qg)
        t1 = self.r_f32.next().f32()
        c.tt(t1, qg, self.rope_cos[:, rope_off:rope_off + 512], ALU.mult)
        t2 = self.r_f32.next().f32()
        c.tt(t2, ps3, self.rope_sin[:, rope_off:rope_off + 512], ALU.mult)
        c.tt(out_bf, t1, t2, ALU.add)

    def mixer(self, l):
        A = self.arena
        self.yaT = [A.alloc(4 * 512 * 2, "yaT%d" % i) for i in range(self.ntt)]
        self.mixer_ga(l)
        if ("ya%d" % self.grp) in self.dbg_out and l == 0:
            for i, y in enumerate(self.yaT):
                self.ctx.dma("pool", self.dbg_out["ya%d" % self.grp][:, :, i * 512:(i + 1) * 512],
                             y.bf().ap.rearrange("p (k t) -> p k t", k=4), reads=[y.res], is_output=True)
        import os
        if os.environ.get("NOS5"):
            self.ybT = [A.alloc(4 * 512 * 2, "ybT%d" % i) for i in range(self.ntt)]
            for y in self.ybT:
                self.ctx.memset(y.bf(), 0.0)
        else:
            self.mixer_s5(l)
        self.ycT = [A.alloc(4 * 512 * 2, "ycT%d" % i) for i in range(self.ntt)]
        if os.environ.get("NONA"):
            for y in self.ycT:
                self.ctx.memset(y.bf(), 0.0)
        else:
            self.mixer_na(l)
        self.merge(l)
        for y in self.yaT + self.ybT + self.ycT + self.hT:
            y.free()
        self.h2T = [A.alloc(8 * 512 * 2, "h2T%d" % i) for i in range(self.ntt)]
        self.proj_out_residual(l, self.mT, "w_out", 8, 2, self.h2T, last=False)
        for m_ in self.mT:
            m_.free()

    def merge(self, l):
        c = self.ctx
        A = self.arena
        I = self.I
        self.mT = [A.alloc(8 * 512 * 2, "mT%d" % i) for i in range(self.ntt)]
        ys = (self.yaT, self.ybT, self.ycT)
        wn = ("w_br_a", "w_br_b", "w_br_c")
        for fo in range(8):
            wb = [self.wload(I[wn[b]][l][:, fo * 128:(fo + 1) * 128], 4) for b in range(3)]
            wg = [self.wload(I["w_in"][l][:, O_G + b * 1024 + fo * 128: O_G + b * 1024 + (fo + 1) * 128], 8)
                  for b in range(3)]
            for tt in range(self.ntt):
                h = self.hT[tt]
                hv = h.bf().ap.rearrange("p (k t) -> p k t", k=8)
                acc = self.r_f32.next().f32()
                for b in range(3):
                    y = ys[b][tt]
                    yv = y.bf().ap.rearrange("p (k t) -> p k t", k=4)
                    pp = self.psum("all8")
                    for k in range(4):
                        c.mm(pp, wb[b][:, k, :], T(yv[:, k, :], y.res), start=(k == 0), stop=(k == 3))
                    pg = self.psum("all8")
                    for k in range(8):
                        c.mm(pg, wg[b][:, k, :], T(hv[:, k, :], h.res), start=(k == 0), stop=(k == 7))
                    sg = self.r_f32.next().f32()
                    c.act(sg, pg, AF.Sigmoid)
                    if b == 0:
                        c.tt(acc, sg, pp, ALU.mult)
                    else:
                        c.tt(sg, sg, pp, ALU.mult)
                        if b == 1:
                            c.tt(acc, acc, sg, ALU.add)
                        else:
                            mv = self.mT[tt].bf().ap.rearrange("p (k t) -> p k t", k=8)
                            c.tt(T(mv[:, fo, :], self.mT[tt].res), acc, sg, ALU.add)

    def proj_out_residual(self, l, inT, wname, nk, ig, h2T, last):
        c = self.ctx
        A = self.arena
        I = self.I
        n = self.grp
        m = self.modS
        for tt in range(len(inT)):
            gt = self.t0 // 512 + tt if not hasattr(self, "_pass_t0") or self._pass_t0 is None else self._pass_t0 // 512 + tt
            src = inT[tt]
            sv = src.bf().ap.rearrange("p (k t) -> p k t", k=nk)
            ot = A.alloc(8 * 512 * 4, "oT")
            ov = ot.f32().ap.rearrange("p (k t) -> p k t", k=8)
            xt = A.alloc(8 * 512 * 4, "xT")
            xv = xt.f32().ap.rearrange("p (k t) -> p k t", k=8)
            c.dma("sp", xv, self.xs[:, :, gt * 512:(gt + 1) * 512].rearrange("k p t -> p k t"),
                  reads=[self.xs_res[k][gt] for k in range(8)], writes=[xt.res])
            msq = self.psum("acc")
            for fo in range(8):
                ps = self.psum()
                k0 = 0
                first = True
                while k0 < nk:
                    kn = min(8, nk - k0)
                    w = self.wload(I[wname][l][k0 * 128:(k0 + kn) * 128, fo * 128:(fo + 1) * 128], kn)
                    for k in range(kn):
                        c.mm(ps, w[:, k, :], T(sv[:, k0 + k, :], src.res), start=first, stop=(k0 + k == nk - 1))
                        first = False
                    k0 += kn
                c.copy(T(ov[:, fo, :], ot.res), ps, eng="act")
                sq = self.r_sq.next().bf()
                c.tt(sq, T(ov[:, fo, :], ot.res), T(ov[:, fo, :], ot.res), ALU.mult)
                c.mm(msq, self.ones_b, sq, start=(fo == 0), stop=(fo == 7), signal=True)
            rs = self.r_f32.next().f32()
            c.act(rs, msq, AF.Ln, bias=EPS, scale=1.0)
            c.act(rs, rs, AF.Exp, scale=-0.5)
            for k in range(8):
                c.tt(T(ov[:, k, :], ot.res), T(ov[:, k, :], ot.res), rs, ALU.mult)
                c.stt(T(xv[:, k, :], xt.res), T(ov[:, k, :], ot.res), T(m.ap[:, l, n, ig, k:k + 1], m.res),
                      T(xv[:, k, :], xt.res), ALU.mult, ALU.add)
                if not (last and l == DEPTH - 1):
                    c.dma("sp", self.xs[k][:, gt * 512:(gt + 1) * 512], xv[:, k, :], reads=[xt.res],
                          writes=[self.xs_res[k][gt]])
            ot.free()
            if not last:
                self.norm_tile(l, xt, h2T[tt], 3, 4)
            elif l == DEPTH - 1:
                for j in range(4):
                    orow = A.alloc(1024 * 4, "orow")
                    for k in range(8):
                        pt = self.psum("aux")
                        c.transpose(pt[:, 0:128], T(xv[:, k, j * 128:(j + 1) * 128], xt.res), self.ident_f)
                        c.copy(orow.f32()[:, k * 128:(k + 1) * 128], pt[:, 0:128], eng=("act" if k % 2 else "dve"))
                    c.dma("sp", self.O["y"][gt * 512 + j * 128: gt * 512 + (j + 1) * 128, :], orow.f32().ap,
                          reads=[orow.res], is_output=True)
                    orow.free()
            xt.free()

    def ffn(self, l):
        c = self.ctx
        A = self.arena
        I = self.I
        cwv = self.convw
        self.r_ub = Ring(A, 2, 1026 * 4 + 56, "rub")
        self.r_v = Ring(A, 3, 1024 * 4, "rv")
        for p0 in range(0, self.nt, 1024):
            tts = [p0 // 512, p0 // 512 + 1]
            act = A.alloc(22 * 1024 * 2, "ffn_act")
            av = act.bf().ap.rearrange("p (j t) -> p j t", j=22)
            nseg = 4 if self.grp == 0 else 1
            L = 1024 // nseg
            for j in range(22):
                wa = self.wload(I["w_up"][l][:, j * 128:(j + 1) * 128], 8)
                wg = self.wload(I["w_up"][l][:, D_FF + j * 128: D_FF + (j + 1) * 128], 8)
                vs = []
                for which, w in ((0, wa), (1, wg)):
                    cc = j + 22 * which
                    ub = self.r_ub.next()
                    ubv = ub.f32(1026)
                    for ti, tt in enumerate(tts):
                        h = self.h2T[tt]
                        hv = h.bf().ap.rearrange("p (k t) -> p k t", k=8)
                        ps = self.psum("all8")
                        for k in range(8):
                            c.mm(ps, w[:, k, :], T(hv[:, k, :], h.res), start=(k == 0), stop=(k == 7))
                        c.copy(ubv[:, 1 + ti * 512: 1 + (ti + 1) * 512], ps, eng="act")
                    for side, col in ((0, 0), (1, 1025)):
                        tok = p0 - 1 if side == 0 else p0 + 1024
                        if self.grp == 0 or tok < 0 or tok >= self.nt:
                            c.memset(ubv[:, col:col + 1], 0.0)
                        else:
                            h = self.h2T[tok // 512]
                            hv = h.bf().ap.rearrange("p (k t) -> p k t", k=8)
                            ps = self.psum("all8")
                            for k in range(8):
                                c.mm(ps[:, 0:1], w[:, k, :], T(hv[:, k, tok % 512: tok % 512 + 1], h.res),
                                     start=(k == 0), stop=(k == 7))
                            c.copy(ubv[:, col:col + 1], ps[:, 0:1])
                    v = self.r_v.next().f32(1024)
                    c.act(v, ubv[:, 1:1025], AF.Identity, bias=T(cwv.ap[:, l, 3, cc:cc + 1], cwv.res),
                          scale=T(cwv.ap[:, l, 1, cc:cc + 1], cwv.res))
                    v3 = T(v.ap.rearrange("p (s t) -> p s t", s=nseg), v.res)
                    ul = T(ubv.ap[:, 1:1025].rearrange("p (s t) -> p s t", s=nseg), ub.res)
                    if nseg == 1:
                        c.stt(v, ubv[:, 0:1024], T(cwv.ap[:, l, 0, cc:cc + 1], cwv.res), v, ALU.mult, ALU.add)
                        c.stt(v, ubv[:, 2:1026], T(cwv.ap[:, l, 2, cc:cc + 1], cwv.res), v, ALU.mult, ALU.add)
                    else:
                        c.stt(v3[:, :, 1:L], ul[:, :, 0:L - 1], T(cwv.ap[:, l, 0, cc:cc + 1], cwv.res), v3[:, :, 1:L],
                              ALU.mult, ALU.add)
                        c.stt(v3[:, :, 0:L - 1], ul[:, :, 1:L], T(cwv.ap[:, l, 2, cc:cc + 1], cwv.res),
                              v3[:, :, 0:L - 1], ALU.mult, ALU.add)
                    vs.append(v)
                c.act(vs[1], vs[1], AF.Silu)
                c.tt(T(av[:, j, :], act.res), vs[0], vs[1], ALU.mult)
            self._ffn_down(l, av, act, tts)
            act.free()
        self.r_ub.free()
        self.r_v.free()

    def _ffn_down(self, l, av, act, tts):
        c = self.ctx

        class V:
            def __init__(s_, ap, res):
                s_.ap_ = ap
                s_.res = res

            def bf(s_):
                return s_

            @property
            def ap(s_):
                return s_

            def rearrange(s_, *a, **k):
                return s_.ap_
        inT = [V(av[:, :, ti * 512:(ti + 1) * 512], act.res) for ti in range(2)]
        self._pass_t0 = self.t0 + tts[0] * 512
        self.proj_out_residual(l, inT, "w_down", 22, 5, None, last=True)
        self._pass_t0 = None

    def attn_rings(self, on):
        A = self.arena
        if on:
            self.r_p = Ring(A, 4, 512 * 2, "rp")
            self.r_q = Ring(A, 2, 4 * 512 * 2, "rq")
            self.r_yt = Ring(A, 3, 4 * 64 * 2, "ryt")
            self.r_st = Ring(A, 4, 128 * 4, "rst")
        else:
            for r in (self.r_p, self.r_q, self.r_yt, self.r_st):
                r.free()

    def mixer_ga(self, l):
        c = self.ctx
        A = self.arena
        I = self.I
        grp = self.grp
        nt = self.nt
        self.attn_rings(True)
        if grp == 1:
            self.rope_t = A.alloc(2 * DEC_SEQ * 4, "rope")
            rv = self.rope_t.f32().ap.rearrange("p (j t) -> p j t", j=2)
            self.rope_cos = T(rv[:, 0, :], self.rope_t.res)
            self.rope_sin = T(rv[:, 1, :], self.rope_t.res)
            c.dma("sp", rv[:, 0, :], self.I["rope_cos"], writes=[self.rope_t.res])
            c.dma("sp", rv[:, 1, :], self.I["rope_sin"], writes=[self.rope_t.res])
        nkeys = nt if grp == 0 else nt + 256
        nblk = nkeys // 128
        W = I["w_in"][l]
        kT = A.alloc(nkeys * 2, "ga_kT")
        kTv = kT.bf()
        va = A.alloc(nblk * 2 * 65 * 2, "ga_vaug")
        vab = va.bf(nblk * 2 * 65)
        vav = vab.ap.rearrange("p (b g e) -> p b g e", b=nblk, g=2)
        c.memset(vab, 1.0)
        wk = self.wload(W[:, O_GK:O_GK + 128], 8)
        wv = self.wload(W[:, O_GV:O_GV + 128], 8)
        import os
        SKIP = os.environ.get("SKIP", "")
        LIM = int(os.environ.get("LIM", "99"))
        for tt in range(min(self.ntt, LIM)):
            h = self.hT[tt]
            hv = h.bf().ap.rearrange("p (k t) -> p k t", k=8)
            if "K" not in SKIP:
                ps = self.psum()
                for k in range(8):
                    c.mm(ps, wk[:, k, :], T(hv[:, k, :], h.res), start=(k == 0), stop=(k == 7))
            if "K" in SKIP:
                pass
            elif "N" in SKIP:
                if "X" not in SKIP:
                    c.copy(kTv[:, tt * 512:(tt + 1) * 512], ps)
            elif grp == 0:
                kf = self.r_f32.next().f32()
                self.qk_norm_rope(ps, l, 1, None, kTv[:, tt * 512:(tt + 1) * 512], f32_out=kf)
                for j in range(4):
                    pt = self.psum("aux")
                    c.transpose(pt[:, 0:128], kf[:, j * 128:(j + 1) * 128], self.ident_f)
                    st = self.r_st.next().f32()
                    c.copy(st, pt[:, 0:128])
                    tok = tt * 512 + j * 128
                    c.dma("sp", self.O["ga_k"][tok // 256, l, tok % 256:tok % 256 + 128, :], st.ap,
                          reads=[st.res], is_output=True)
            else:
                self.qk_norm_rope(ps, l, 1, tt * 512, kTv[:, tt * 512:(tt + 1) * 512])
            for j in range(4 if "V" not in SKIP else 0):
                pv = self.psum("aux" if "A" in SKIP else "mm")
                for k in range(8):
                    c.mm(pv[:, 0:128], T(hv[:, k, j * 128:(j + 1) * 128], h.res), wv[:, k, :],
                         start=(k == 0), stop=(k == 7))
                blk = tt * 4 + j
                c.copy(T(vav[:, blk, :, 0:64], va.res),
                       T(pv.ap[:, 0:128].rearrange("p (g d) -> p g d", g=2), pv.res), eng="act")
                if grp == 0:
                    st = self.r_st.next().f32()
                    c.copy(st, pv[:, 0:128])
                    tok = tt * 512 + j * 128
                    c.dma("sp", self.O["ga_v"][tok // 256, l, tok % 256:tok % 256 + 128, :], st.ap,
                          reads=[st.res], is_output=True)
        if grp == 1 and "C" not in SKIP:
            for j in range(2):
                st = self.r_st.next().f32()
                c.dma("sp", st.ap, I["cache_ga_k"][l, j * 128:(j + 1) * 128, :], writes=[st.res])
                pt = self.psum("aux")
                c.transpose(pt[:, 0:128], st, self.ident_f)
                c.copy(kTv[:, nt + j * 128: nt + (j + 1) * 128], pt[:, 0:128])
            for j in range(2):
                c.dma("pool", vav[:, 16 + j, :, 0:64],
                      I["cache_ga_v"][l, j * 128:(j + 1) * 128, :].rearrange("p (g d) -> p g d", g=2),
                      writes=[va.res])
        import os
        CUT = int(os.environ.get("CUT", "99"))
        for tt in range(self.ntt if CUT > 0 else 0):
            h = self.hT[tt]
            hv = h.bf().ap.rearrange("p (k t) -> p k t", k=8)
            qt = self.r_q.next()
            qv = qt.bf().ap.rearrange("p (h t) -> p h t", h=4)
            for hl in range(4):
                w = self.wload([W[:, O_GQ + hl * 64:O_GQ + hl * 64 + 64],
                                W[:, O_GQ + (4 + hl) * 64:O_GQ + (4 + hl) * 64 + 64]], 8)
                ps = self.psum()
                for k in range(8):
                    c.mm(ps, w[:, k, :], T(hv[:, k, :], h.res), start=(k == 0), stop=(k == 7))
                self.qk_norm_rope(ps, l, 0, (tt * 512 if grp == 1 else None), T(qv[:, hl, :], qt.res))
            yv = self.yaT[tt].bf().ap.rearrange("p (k t) -> p k t", k=4)
            its = []
            for sub in range(4):
                tok0 = tt * 512 + sub * 128
                if grp == 0:
                    sq_ = tok0 // 256
                    blocks = [2 * sq_, 2 * sq_ + 1]
                else:
                    blocks = list(range(nblk))
                for g in range(2):
                    for bi, kb in enumerate(blocks):
                        its.append((sub, g, bi, kb, len(blocks)))

            def emit_score(it):
                sub, g, bi, kb, nb = it
                s_ps = self.psum()
                c.mm(T(s_ps.ap.rearrange("p (h t) -> p h t", h=4), s_ps.res),
                     kTv[g * 64:(g + 1) * 64, kb * 128:(kb + 1) * 128],
                     T(qv[g * 64:(g + 1) * 64, :, sub * 128:(sub + 1) * 128], qt.res))
                return s_ps
            s_next = emit_score(its[0])
            o_ps = None
            for n_, it in enumerate(its):
                sub, g, bi, kb, nb = it
                s_ps = s_next
                if n_ + 1 < len(its):
                    s_next = emit_score(its[n_ + 1])
                pT = self.r_p.next().bf()
                c.act(pT, s_ps, AF.Exp, scale=0.125)
                if bi == 0:
                    o_ps = self.psum("acc")
                    c.mm(o_ps[:, 0:260], self.zeros_b[:, 0:128], self.zeros_b[:, 0:260], start=True, stop=False,
                         signal=False)
                for hl in range(4):
                    c.mm(o_ps[:, hl * 65:(hl + 1) * 65], pT[:, hl * 128:(hl + 1) * 128],
                         T(vav[:, kb, g, :], va.res), start=False,
                         stop=(bi == nb - 1 and hl == 3), signal=(bi == nb - 1 and hl == 3))
                if bi != nb - 1:
                    continue
                ov = o_ps.ap[:, 0:260].rearrange("p (h e) -> p h e", h=4)
                rec = self.r_small.next().f32()[:, 0:4]
                c.recip(rec, T(ov[:, :, 64], o_ps.res))
                yt = self.r_yt.next()
                ytv = yt.bf().ap.rearrange("p (h e) -> p h e", h=4)
                c.tt(T(ytv, yt.res), T(ov[:, :, 0:64], o_ps.res),
                     T(rec.ap.unsqueeze(2).broadcast_to([128, 4, 64]), rec.res), ALU.mult)
                for j in range(2):
                    tp = self.psum("aux")
                    tpb = T(tp.ap.bitcast(BF16)[:, 0:128], tp.res)
                    c.transpose(tpb, yt.bf()[:, j * 128:(j + 1) * 128], self.ident_b)
                    c.copy(T(yv[:, 2 * g + j, sub * 128:(sub + 1) * 128], self.yaT[tt].res), tpb)
        kT.free()
        va.free()
        self.attn_rings(False)
        if grp == 1:
            self.rope_t.free()

    def mixer_na(self, l):
        c = self.ctx
        A = self.arena
        I = self.I
        grp = self.grp
        nt = self.nt
        nkeys = nt if grp == 0 else nt + 256
        nblk = nkeys // 128
        W = I["w_in"][l]
        self.attn_rings(True)
        if grp == 1:
            zt = self.r_f32.next().f32()
            c.memset(zt, 0.0)
            c.dma("sp", zt.ap[0:120, 48:79], I["na_rpb"][l].rearrange("h r j -> (h r) j"), writes=[zt.res])
            nvp = 8 * 15 * 127
            c.dma("sp", self.vp[0:nvp].rearrange("(a j) -> a j", j=127), zt.ap[0:120, 0:127], reads=[zt.res],
                  writes=[self.vp_res])
            mk = A.alloc(21 * 128 * 2, "na_mask")
            mkv = mk.bf().ap.rearrange("p (c q) -> p c q", c=21)
            c.dma("pool", mkv, I["na_mask"], writes=[mk.res])
            hraw = A.alloc(7 * 2 * 64 * 4, "na_hraw")
            hrv = hraw.f32().ap.rearrange("p (d b c) -> p d b c", d=7, b=2)
            eb = A.alloc(7 * 128 * 2, "na_eb")
            ebv = eb.bf().ap.rearrange("p (d b c) -> p d b c", d=7, b=2)
            etabs = [A.alloc(21 * 128 * 2, "na_etab%d" % i) for i in range(2)]
        for j in range(4):
            kT = A.alloc(nkeys * 2, "na_kT")
            kTv = kT.bf()
            qT = A.alloc(nt * 2, "na_qT")
            qTv = qT.bf()
            va = A.alloc(nblk * 2 * 65 * 2, "na_vaug")
            vab = va.bf(nblk * 2 * 65)
            vav = vab.ap.rearrange("p (b g e) -> p b g e", b=nblk, g=2)
            c.memset(vab, 1.0)
            wq = self.wload(W[:, O_NQ + j * 128:O_NQ + (j + 1) * 128], 8)
            wk = self.wload(W[:, O_NK + j * 128:O_NK + (j + 1) * 128], 8)
            wv = self.wload(W[:, O_NV + j * 128:O_NV + (j + 1) * 128], 8)
            for tt in range(self.ntt):
                h = self.hT[tt]
                hv = h.bf().ap.rearrange("p (k t) -> p k t", k=8)
                ps = self.psum()
                for k in range(8):
                    c.mm(ps, wq[:, k, :], T(hv[:, k, :], h.res), start=(k == 0), stop=(k == 7))
                c.copy(qTv[:, tt * 512:(tt + 1) * 512], ps, eng="act")
                ps = self.psum()
                for k in range(8):
                    c.mm(ps, wk[:, k, :], T(hv[:, k, :], h.res), start=(k == 0), stop=(k == 7))
                if grp == 0:
                    kf = self.r_f32.next().f32()
                    c.copy(kf, ps, eng="act")
                    c.copy(kTv[:, tt * 512:(tt + 1) * 512], kf)
                    for jj in range(4):
                        pt = self.psum("aux")
                        c.transpose(pt[:, 0:128], kf[:, jj * 128:(jj + 1) * 128], self.ident_f)
                        st = self.r_st.next().f32()
                        c.copy(st, pt[:, 0:128])
                        tok = tt * 512 + jj * 128
                        c.dma("sp", self.O["na_k"][tok // 256, l, tok % 256:tok % 256 + 128, j * 128:(j + 1) * 128],
                              st.ap, reads=[st.res], is_output=True)
                else:
                    c.copy(kTv[:, tt * 512:(tt + 1) * 512], ps)
                for jj in range(4):
                    pv = self.psum()
                    for k in range(8):
                        c.mm(pv[:, 0:128], T(hv[:, k, jj * 128:(jj + 1) * 128], h.res), wv[:, k, :],
                             start=(k == 0), stop=(k == 7))
                    blk = tt * 4 + jj
                    c.copy(T(vav[:, blk, :, 0:64], va.res),
                           T(pv.ap[:, 0:128].rearrange("p (g d) -> p g d", g=2), pv.res), eng="act")
                    if grp == 0:
                        st = self.r_st.next().f32()
                        c.copy(st, pv[:, 0:128])
                        tok = tt * 512 + jj * 128
                        c.dma("sp", self.O["na_v"][tok // 256, l, tok % 256:tok % 256 + 128, j * 128:(j + 1) * 128],
                              st.ap, reads=[st.res], is_output=True)
            if grp == 1:
                for b in range(2):
                    st = self.r_st.next().f32()
                    c.dma("sp", st.ap, I["cache_na_k"][l, b * 128:(b + 1) * 128, j * 128:(j + 1) * 128], writes=[st.res])
                    pt = self.psum("aux")
                    c.transpose(pt[:, 0:128], st, self.ident_f)
                    c.copy(kTv[:, nt + b * 128: nt + (b + 1) * 128], pt[:, 0:128])
                    c.dma("pool", vav[:, 16 + b, :, 0:64],
                          I["cache_na_v"][l, b * 128:(b + 1) * 128, j * 128:(j + 1) * 128].rearrange("p (g d) -> p g d", g=2),
                          writes=[va.res])
                for hh in range(2):
                    hd = 2 * j + hh
                    hres = [Res("hraw%d" % q_) for q_ in range(28)]
                    for r_ in hres:
                        r_.w = dict(hraw.res.w)
                        r_.r = dict(hraw.res.r)
                    c.op("dve", lambda e: e.memset(hraw.f32().ap, 0.0), reads=[], writes=hres)
                    for di, dj in enumerate(range(-3, 4)):
                        for a in range(2):
                            for b in range(2):
                                dr = 2 * dj + a - b + 7
                                if 0 <= dr <= 14:
                                    off = (hd * 15 + dr) * 127
                                    src = bass.AP(self.vp.tensor, off, [[1, 64], [1, 64]])
                                    c.dma("sp", hrv[a * 64:(a + 1) * 64, di, b, :], src, reads=[self.vp_res],
                                          writes=[hres[di * 4 + a * 2 + b]])
                    c.op("act", lambda e: e.activation(out=ebv, in_=hrv[:, :, :, ::-1], func=AF.Exp),
                         reads=hres, writes=[eb.res])
                    hraw.res.w = {}
                    hraw.res.r = {}
                    for r_ in hres:
                        for d_, dst_ in ((r_.w, hraw.res.w), (r_.r, hraw.res.r)):
                            for k_, v_ in d_.items():
                                if dst_.get(k_, 0) < v_:
                                    dst_[k_] = v_
                    et = etabs[hh]
                    etv = et.bf().ap.rearrange("p (c q) -> p c q", c=21)
                    ebc = eb.bf().ap.rearrange("p (d q) -> p d q", d=7)
                    for (c0_, n_, d0_) in ((0, 5, 1), (5, 4, 3), (9, 4, 2), (13, 4, 1), (17, 4, 0)):
                        c.tt(T(etv[:, c0_:c0_ + n_, :], et.res), T(ebc[:, d0_:d0_ + n_, :], eb.res),
                             T(mkv[:, c0_:c0_ + n_, :], mk.res), ALU.mult)
            import os
            NST = int(os.environ.get("NA_STAGE", "9"))
            if NST < 2:
                for y in self.ycT:
                    if j == 0:
                        c.memset(y.bf(), 0.0)
            elif grp == 0:
                for seq in range(4):
                    pTs = []
                    for kb in range(2):
                        pT = self.r_p.next().bf()
                        for hh in range(2):
                            s_ps = self.psum()
                            c.mm(s_ps[:, 0:256],
                                 kTv[hh * 64:(hh + 1) * 64, (seq * 2 + kb) * 128:(seq * 2 + kb + 1) * 128],
                                 qTv[hh * 64:(hh + 1) * 64, seq * 256:(seq + 1) * 256])
                            c.act(pT[:, hh * 256:(hh + 1) * 256], s_ps[:, 0:256], AF.Exp, scale=0.125)
                        pTs.append(pT)
                    if NST < 3:
                        if j == 0 and seq == 0:
                            for y in self.ycT:
                                c.memset(y.bf(), 0.0)
                        continue
                    for sub in range(2):
                        o_ps = self.psum("acc")
                        c.mm(o_ps[:, 0:130], self.zeros_b[:, 0:128], self.zeros_b[:, 0:130], start=True, stop=False,
                             signal=False)
                        for kb in range(2):
                            for hh in range(2):
                                last = (kb == 1 and hh == 1)
                                c.mm(o_ps[:, hh * 65:(hh + 1) * 65],
                                     pTs[kb][:, hh * 256 + sub * 128: hh * 256 + (sub + 1) * 128],
                                     T(vav[:, seq * 2 + kb, hh, :], va.res), start=False, stop=last, signal=last)
                        self._na_finish(o_ps, j, seq * 256 + sub * 128)
            else:
                its = []
                for i in range(16):
                    if i == 0:
                        loc, cls0 = [0, 1, 2, 3], 5
                    elif i == 1:
                        loc, cls0 = [0, 1, 2, 3], 9
                    elif i == 14:
                        loc, cls0 = [12, 13, 14, 15], 13
                    elif i == 15:
                        loc, cls0 = [12, 13, 14, 15], 17
                    else:
                        loc, cls0 = list(range(i - 2, i + 3)), 0
                    blocks = loc + [16, 17]
                    for hh in range(2):
                        for part in range(2):
                            its.append((i, hh, part, blocks[0:4] if part == 0 else blocks[4:], cls0, len(loc)))

                def emit_score(it):
                    i, hh, part, bl, cls0, nloc = it
                    s_ps = self.psum()
                    for bi, kb in enumerate(bl):
                        c.mm(s_ps[:, bi * 128:(bi + 1) * 128], kTv[hh * 64:(hh + 1) * 64, kb * 128:(kb + 1) * 128],
                             qTv[hh * 64:(hh + 1) * 64, i * 128:(i + 1) * 128])
                    return s_ps
                LOOK = 2
                pend = [emit_score(its[n_]) for n_ in range(min(LOOK, len(its)))]
                o_ps = None
                for n_, it in enumerate(its):
                    i, hh, part, bl, cls0, nloc = it
                    s_ps = pend.pop(0)
                    if n_ + LOOK < len(its):
                        pend.append(emit_score(its[n_ + LOOK]))
                    etv = etabs[hh].bf().ap.rearrange("p (c q) -> p c q", c=21)
                    pT = self.r_p.next().bf()
                    nb = len(bl)
                    c.act(pT[:, 0:nb * 128], s_ps[:, 0:nb * 128], AF.Exp, scale=0.125)
                    if part == 0:
                        nl = 4
                        c.tt(pT[:, 0:nl * 128], pT[:, 0:nl * 128],
                             T(etv[:, cls0:cls0 + nl, :].rearrange("p c q -> p (c q)"), etabs[hh].res), ALU.mult)
                    elif nloc == 5:
                        c.tt(pT[:, 0:128], pT[:, 0:128], T(etv[:, 4, :], etabs[hh].res), ALU.mult)
                    if hh == 0 and part == 0:
                        o_ps = self.psum("acc")
                        c.mm(o_ps[:, 0:130], self.zeros_b[:, 0:128], self.zeros_b[:, 0:130], start=True, stop=False,
                             signal=False)
                    for bi, kb in enumerate(bl):
                        last = (hh == 1 and part == 1 and bi == nb - 1)
                        c.mm(o_ps[:, hh * 65:(hh + 1) * 65], pT[:, bi * 128:(bi + 1) * 128],
                             T(vav[:, kb, hh, :], va.res), start=False, stop=last, signal=last)
                    if hh == 1 and part == 1:
                        self._na_finish(o_ps, j, i * 128)
            kT.free()
            qT.free()
            va.free()
        if grp == 1:
            for t_ in [mk, hraw, eb] + etabs:
                t_.free()
        self.attn_rings(False)

    def _na_finish(self, o_ps, j, tok0):
        c = self.ctx
        ov = o_ps.ap[:, 0:130].rearrange("p (h e) -> p h e", h=2)
        rec = self.r_small.next().f32()[:, 0:2]
        c.recip(rec, T(ov[:, :, 64], o_ps.res))
        yt = self.r_yt.next()
        ytv = yt.bf(128).ap.rearrange("p (h e) -> p h e", h=2)
        c.tt(T(ytv, yt.res), T(ov[:, :, 0:64], o_ps.res),
             T(rec.ap.unsqueeze(2).broadcast_to([128, 2, 64]), rec.res), ALU.mult)
        tp = self.psum("aux")
        tpb = T(tp.ap.bitcast(BF16)[:, 0:128], tp.res)
        c.transpose(tpb, yt.bf(128), self.ident_b)
        tt = tok0 // 512
        yv = self.ycT[tt].bf().ap.rearrange("p (k t) -> p k t", k=4)
        c.copy(T(yv[:, j, tok0 % 512: tok0 % 512 + 128], self.ycT[tt].res), tpb)

    def _cmul(self, o_re, o_im, a_re, a_im, b_re, b_im, tmp):
        c = self.ctx
        c.tt(o_re, a_re, b_re, ALU.mult)
        c.tt(tmp, a_im, b_im, ALU.mult)
        c.tt(o_re, o_re, tmp, ALU.subtract)
        c.tt(o_im, a_re, b_im, ALU.mult)
        c.tt(tmp, a_im, b_re, ALU.mult)
        c.tt(o_im, o_im, tmp, ALU.add)

    def _trig(self, out, bb, shift, tmp_v, tmp_n):
        c = self.ctx
        PI = math.pi
        c.ts(tmp_v, bb, shift + PI, None, ALU.add)
        c.ts(tmp_n, tmp_v, 2 * PI, None, ALU.is_ge)
        c.stt(tmp_n, tmp_v, 4 * PI, tmp_n, ALU.is_ge, ALU.add)
        c.stt(tmp_n, tmp_v, 6 * PI, tmp_n, ALU.is_ge, ALU.add)
        c.stt(tmp_v, tmp_n, -2 * PI, tmp_v, ALU.mult, ALU.add)
        c.act(out, tmp_v, AF.Sin, bias=self.negpi, scale=1.0)

    def ssm_prep(self, l):
        c = self.ctx
        A = self.arena
        I = self.I
        if not hasattr(self, "negpi"):
            t_ = A.alloc(64, "negpi")
            self.negpi = t_.f32()[:, 0:1]
            c.memset(self.negpi, -math.pi)
        small = A.alloc(4 * 32 * 24, "s5small")
        sm = small.f32().ap.rearrange("p (i g) -> p i g", g=32)

        def S(i):
            return T(sm[:, i, :], small.res)
        pw = A.alloc(4 * 2 * 16 * 32, "s5pw")
        pwv = pw.f32().ap.rearrange("p (r k g) -> p r k g", r=2, k=16)

        def PW(r, k):
            return T(pwv[:, r, k + 7, :], pw.res)
        bt = A.alloc(4 * 2 * 512, "s5B")
        btv = bt.f32().ap.rearrange("p (r g h) -> p r g h", r=2, g=32)
        ct = A.alloc(4 * 2 * 512, "s5C")
        ctv = ct.f32().ap.rearrange("p (r g h) -> p r g h", r=2, g=32)
        bb_t = A.alloc(4 * 2 * 512, "s5Bbar")
        bbv = bb_t.f32().ap.rearrange("p (r g h) -> p r g h", r=2, g=32)
        pr = A.alloc(4 * 4096, "s5pr")
        pi_ = A.alloc(4 * 4096, "s5pi")
        tmp = A.alloc(4 * 4096, "s5tmp")
        xd = A.alloc(2 * 4096, "s5X")
        yd = A.alloc(2 * 4096, "s5Y")
        zt = [A.alloc(2 * 4096, "s5ZT%d" % i) for i in range(2)]
        macc = A.alloc(4 * 4096, "s5M")
        cin = [A.alloc(2 * 4096, "s5Cin%d" % i) for i in range(4)]
        for t_ in cin:
            c.memset(t_.bf(), 0.0)
        msk = A.alloc(4 * 256, "s5mask")
        mskv = msk.f32().ap.rearrange("p (d c) -> p d c", d=2)
        c.dma("sp", mskv[:, 0, :], I["ssm_maskF"], writes=[msk.res])
        c.dma("sp", mskv[:, 1, :], I["ssm_maskB"], writes=[msk.res])
        dcol = A.alloc(4 * 32, "s5dcol")
        for s_ in range(8):
            c.dma("sp", dcol.f32().ap[s_ * 16:(s_ + 1) * 16, :], I["ssm_d"][l].rearrange("(g h) -> h g", h=16),
                  writes=[dcol.res], allow_slow_non_contiguous=True)
        lam8 = A.alloc(4 * 2 * 2 * 32, "lam8_%d" % l)
        l8v = lam8.f32().ap.rearrange("p (a r g) -> p a r g", a=2, r=2)
        self.lam8.append(T(l8v, lam8.res))
        r4 = lambda t_: t_.f32().ap.rearrange("p (g s h) -> p g s h", g=32, s=8)
        r4b = lambda t_: t_.bf().ap.rearrange("p (g s h) -> p g s h", g=32, s=8)
        for d in range(2):
            for ri, nm in ((0, "ssm_lam_re"), (1, "ssm_lam_im")):
                st = self.r_st32.next().f32()
                for hf in range(2):
                    c.dma("sp", st.ap[0:32, hf * 64:(hf + 1) * 64], I[nm][l, d], writes=[st.res])
                pt = self.psum("aux")
                c.transpose(pt[:, 0:32], st[0:32, :], self.ident_f[0:32, 0:32])
                c.copy(S(ri), pt[:, 0:32])
            c.dma("sp", sm[:, 2, :], I["ssm_log_step"][l, d:d + 1, :].partition_broadcast(128)
                  if hasattr(I["ssm_log_step"], "partition_broadcast") else I["ssm_log_step"][l, d:d + 1, :].broadcast_to([128, 32]),
                  writes=[small.res])
            for ri, nm in ((0, "ssm_b_re"), (1, "ssm_b_im")):
                for hf in range(2):
                    c.dma("sp", btv[hf * 64:(hf + 1) * 64, ri, :, :], I[nm][l, d].rearrange("g p h -> p g h"),
                          writes=[bt.res])
            for ri, nm in ((0, "ssm_c_re"), (1, "ssm_c_im")):
                for ch in range(4):
                    st = self.r_st32.next().f32()
                    for hf in range(2):
                        c.dma("sp", st.ap[:, hf * 64:(hf + 1) * 64], I[nm][l, d, ch * 128:(ch + 1) * 128, :],
                              writes=[st.res])
                    pt = self.psum("aux")
                    c.transpose(pt[:, 0:128], st, self.ident_f)
                    c.copy(T(ctv[:, ri, ch * 8:(ch + 1) * 8, :], ct.res),
                           T(pt.ap[:, 0:128].rearrange("p (g h) -> p g h", g=8), pt.res))
            c.act(S(2), S(2), AF.Exp)
            c.tt(S(3), S(0), S(2), ALU.mult)
            c.tt(S(4), S(1), S(2), ALU.mult)
            c.act(S(5), S(3), AF.Exp)
            c.act(S(6), S(3), AF.Exp, scale=-1.0)
            self._trig(S(7), S(4), 0.0, S(9), S(10))
            self._trig(S(8), S(4), math.pi / 2, S(9), S(10))
            c.memset(PW(0, 0), 1.0)
            c.memset(PW(1, 0), 0.0)
            c.tt(PW(0, 1), S(5), S(8), ALU.mult)
            c.tt(PW(1, 1), S(5), S(7), ALU.mult)
            c.tt(PW(0, -1), S(6), S(8), ALU.mult)
            c.tt(S(11), S(6), S(7), ALU.mult)
            c.ts(PW(1, -1), S(11), -1.0, None, ALU.mult)
            for k in range(1, 8):
                self._cmul(PW(0, k + 1), PW(1, k + 1), PW(0, k), PW(1, k), PW(0, 1), PW(1, 1), S(12))
            for k in range(1, 7):
                self._cmul(PW(0, -k - 1), PW(1, -k - 1), PW(0, -k), PW(1, -k), PW(0, -1), PW(1, -1), S(12))
            c.tt(S(15), S(0), S(0), ALU.mult)
            c.tt(S(16), S(1), S(1), ALU.mult)
            c.tt(S(15), S(15), S(16), ALU.add)
            c.recip(S(15), S(15))
            c.ts(S(16), PW(0, 1), -1.0, None, ALU.add)
            c.tt(S(17), S(16), S(0), ALU.mult)
            c.tt(S(18), PW(1, 1), S(1), ALU.mult)
            c.tt(S(17), S(17), S(18), ALU.add)
            c.tt(S(13), S(17), S(15), ALU.mult)
            c.tt(S(17), PW(1, 1), S(0), ALU.mult)
            c.tt(S(18), S(16), S(1), ALU.mult)
            c.tt(S(17), S(17), S(18), ALU.subtract)
            c.tt(S(14), S(17), S(15), ALU.mult)

            def bc_g(t_):
                return T(t_.ap.unsqueeze(2).broadcast_to([128, 32, 16]), t_.res)
            t512 = T(tmp.f32().ap[:, 0:512].rearrange("p (g h) -> p g h", g=32), tmp.res)
            self._cmul(T(bbv[:, 0], bb_t.res), T(bbv[:, 1], bb_t.res), bc_g(S(13)), bc_g(S(14)),
                       T(btv[:, 0], bt.res), T(btv[:, 1], bt.res), t512)
            hs = slice(d * 64, (d + 1) * 64)
            c.copy(T(l8v[hs, 0, 0, :], lam8.res), PW(0, 8)[hs])
            c.copy(T(l8v[hs, 0, 1, :], lam8.res), PW(0, 8)[hs])
            c.copy(T(l8v[hs, 1, 1, :], lam8.res), PW(1, 8)[hs])
            c.ts(T(l8v[hs, 1, 0, :], lam8.res), PW(1, 8)[hs], -1.0, None, ALU.mult)

            def pw_view(k0, step):
                base_re = pwv[:, 0, k0 + 7, :]
                base_im = pwv[:, 1, k0 + 7, :]
                def mk(b_):
                    return T(bass.AP(b_.tensor, b_.offset, [list(b_.ap[0]), [1, 32], [32 * step, 8], [0, 16]]), pw.res)
                return mk(base_re), mk(base_im)

            def mat_view(v4):
                def mk(r):
                    b_ = v4[:, r]
                    return T(bass.AP(b_.tensor, b_.offset, [list(b_.ap[0]), [16, 32], [0, 8], [1, 16]]), None)
                return mk(0), mk(1)
            prv, piv, tmv = T(r4(pr), pr.res), T(r4(pi_), pi_.res), T(r4(tmp), tmp.res)
            a_re, a_im = pw_view(7, -1) if d == 0 else pw_view(0, 1)
            b_re, b_im = mat_view(bbv)
            b_re.res = b_im.res = bb_t.res
            self._cmul(prv, piv, a_re, a_im, b_re, b_im, tmv)
            xv, yv_ = r4b(xd), r4b(yd)
            c.copy(T(xv[0:64], xd.res), prv[0:64])
            c.copy(T(xv[64:128], xd.res), piv[64:128], eng="act")
            c.copy(T(r4b(zt[0])[hs], zt[0].res), prv[hs])
            c.copy(T(r4b(zt[1])[hs], zt[1].res), piv[hs], eng="act")
            a_re, a_im = pw_view(-7, 1) if d == 0 else pw_view(0, -1)
            b_re, b_im = mat_view(ctv)
            b_re.res = b_im.res = ct.res
            self._cmul(prv, piv, a_re, a_im, b_re, b_im, tmv)
            c.copy(T(yv_[0:64], yd.res), prv[0:64])
            c.ts(T(yv_[64:128], yd.res), piv[64:128], -1.0, None, ALU.mult)
            mv = macc.f32().ap.rearrange("p (g c) -> p g c", g=32)
            xf = xd.bf().ap.rearrange("p (g c) -> p g c", g=32)
            yf = yd.bf().ap.rearrange("p (g c) -> p g c", g=32)
            for g in range(32):
                ps = self.psum()
                c.mm(ps[:, 0:128], T(xf[:, g, :], xd.res), T(yf[:, g, :], yd.res))
                if d == 0:
                    c.tt(T(mv[:, g, :], macc.res), ps[:, 0:128], T(mskv[:, 0, :], msk.res), ALU.mult)
                else:
                    t128 = T(tmp.f32().ap[:, 0:128], tmp.res)
                    c.tt(t128, ps[:, 0:128], T(mskv[:, 1, :], msk.res), ALU.mult)
                    c.tt(T(mv[:, g, :], macc.res), T(mv[:, g, :], macc.res), t128, ALU.add)
            a_re, a_im = pw_view(1, 1) if d == 0 else pw_view(8, -1)
            self._cmul(prv, piv, a_re, a_im, b_re, b_im, tmv)
            c.copy(T(r4b(cin[2 * d])[hs], cin[2 * d].res), prv[hs])
            c.ts(T(r4b(cin[2 * d + 1])[hs], cin[2 * d + 1].res), piv[hs], -1.0, None, ALU.mult)
        mv = macc.f32().ap.rearrange("p (g c) -> p g c", g=32)
        mb = A.alloc(2 * 4096, "s5Mb")
        mbv = mb.bf().ap.rearrange("p (g c) -> p g c", g=32)
        for g in range(32):
            c.stt(T(mbv[:, g, :], mb.res), self.ident_f, T(dcol.f32().ap[:, g:g + 1], dcol.res), T(mv[:, g, :], macc.res),
                  ALU.mult, ALU.add)
        c.dma("sp", self.ssmw[l, 0].rearrange("g p c -> p g c"), mbv, reads=[mb.res], writes=[self.ssmw_res])
        for ci in range(4):
            c.dma("sp", self.ssmw[l, 3 + ci].rearrange("g p c -> p g c"),
                  cin[ci].bf().ap.rearrange("p (g c) -> p g c", g=32), reads=[cin[ci].res], writes=[self.ssmw_res])
        for ri in range(2):
            zf = zt[ri].bf().ap.rearrange("p (g c) -> p g c", g=32)
            zo = A.alloc(2 * 4096, "s5Zo")
            zov = zo.bf().ap.rearrange("p (g c) -> p g c", g=32)
            for g in range(32):
                tp = self.psum("aux")
                tpb = T(tp.ap.bitcast(BF16)[:, 0:128], tp.res)
                c.transpose(tpb, T(zf[:, g, :], zt[ri].res), self.ident_b)
                c.copy(T(zov[:, g, :], zo.res), tpb, eng=("act" if g % 2 else "dve"))
            c.dma("sp", self.ssmw[l, 1 + ri].rearrange("g p c -> p g c"), zov, reads=[zo.res], writes=[self.ssmw_res])
            zo.free()
        for t_ in [small, pw, bt, ct, bb_t, pr, pi_, tmp, xd, yd, macc, msk, dcol, mb] + zt + cin:
            t_.free()

    def mixer_s5(self, l):
        c = self.ctx
        A = self.arena
        I = self.I
        grp = self.grp
        nt = self.nt
        N = nt // 8
        nseq = 4 if grp == 0 else 1
        cps = N // nseq
        W = I["w_in"][l]
        wa = A.alloc(8 * 240 * 2, "s5wa")
        wav = wa.bf().ap.rearrange("p (a j) -> p a j", a=8)
        c.dma("pool", wav, I["ssm_wa"], writes=[wa.res])
        zu = A.alloc(4 * nt * 2, "s5zu")
        zuv = zu.bf().ap.rearrange("p (k t) -> p k t", k=4)
        for kc in range(4):
            w = self.wload(W[:, O_U + kc * 128:O_U + (kc + 1) * 128], 8)
            for tt in range(self.ntt):
                h = self.hT[tt]
                hv = h.bf().ap.rearrange("p (k t) -> p k t", k=8)
                ps = self.psum()
                for k in range(8):
                    c.mm(ps, w[:, k, :], T(hv[:, k, :], h.res), start=(k == 0), stop=(k == 7))
                c.copy(T(zuv[:, kc, tt * 512:(tt + 1) * 512], zu.res), ps, eng=("act" if tt % 2 else "dve"))
        U = A.alloc(32 * N * 2, "s5U")
        Uv = U.bf().ap.rearrange("p (g n) -> p g n", g=32)
        for g in range(32):
            kc, gl = g // 8, g % 8
            ps = self.psum()
            zs = zuv[:, kc, :].rearrange("p (n s) -> p s n", s=8)
            for s_ in range(8):
                c.mm(ps[:, 0:N], T(wav[:, gl, 112 - 16 * s_: 240 - 16 * s_], wa.res), T(zs[:, s_, :], zu.res),
                     start=(s_ == 0), stop=(s_ == 7))
            c.copy(T(Uv[:, g, :], U.res), ps[:, 0:N], eng=("act" if g % 2 else "dve"))
        zu.free()
        Z = A.alloc(2 * 32 * nseq * (cps + 2) * 2, "s5Z")
        Zv = Z.bf(2 * 32 * nseq * (cps + 2)).ap.rearrange("p (r g q k) -> p r g q k", r=2, g=32, q=nseq)
        c.memset(Z.bf(), 0.0)
        St = [A.alloc(2 * 32 * nseq * 4, "s5S%d" % i) for i in range(2)]
        Sv = [T(t_.f32(2 * 32 * nseq).ap.rearrange("p (r g q) -> p r g q", r=2, g=32), t_.res) for t_ in St]
        if grp == 0:
            c.memset(Sv[0], 0.0)
        else:
            for ri, nm in ((0, "state_re"), (1, "state_im")):
                for d in range(2):
                    st = self.r_st32.next().f32()
                    for hf in range(2):
                        c.dma("sp", st.ap[0:32, hf * 64:(hf + 1) * 64], I[nm][l, d], writes=[st.res])
                    pt = self.psum("aux")
                    c.transpose(pt[:, 0:32], st[0:32, :], self.ident_f[0:32, 0:32])
                    hs = slice(d * 64, (d + 1) * 64)
                    c.copy(T(Sv[0].ap[hs, ri, :, 0], St[0].res), pt[hs, 0:32])
        c.copy(T(Zv[0:64, :, :, :, 0], Z.res), Sv[0][0:64], eng="act")
        c.copy(T(Zv[64:128, :, :, :, cps + 1], Z.res), Sv[0][64:128], eng="act")
        wst = Ring(A, 3, 7 * 128 * 2, "s5wst")
        def wtile(g):
            t_ = wst.next()
            v = t_.bf().ap.rearrange("p (k c) -> p k c", k=7)
            c.dma("sp", v, self.ssmw[l, :, g].rearrange("k p c -> p k c"), reads=[self.ssmw_res], writes=[t_.res])
            return T(v, t_.res)
        wt_n = wtile(0)
        for g in range(32):
            wt = wt_n
            if g + 1 < 32:
                wt_n = wtile(g + 1)
            for ri in range(2):
                ps = self.psum()
                c.mm(ps[:, 0:N], wt[:, 1 + ri, :], T(Uv[:, g, :], U.res))
                pv = ps.ap[:, 0:N].rearrange("p (q k) -> p q k", q=nseq)
                c.copy(T(Zv[:, ri, g, :, 1:cps + 1], Z.res), T(pv, ps.res), eng=("act" if ri else "dve"))
        l8 = self.lam8[l]
        sh_ = [128, 2, 32, nseq]
        lamA = T(l8.ap[:, 0].unsqueeze(3).broadcast_to(sh_), l8.res)
        lamB = T(l8.ap[:, 1].unsqueeze(3).broadcast_to(sh_), l8.res)
        tA = A.alloc(2 * 32 * nseq * 4, "s5tA")
        tB = A.alloc(2 * 32 * nseq * 4, "s5tB")
        tAv = T(tA.f32(2 * 32 * nseq).ap.rearrange("p (r g q) -> p r g q", r=2, g=32), tA.res)
        tBv = T(tB.f32(2 * 32 * nseq).ap.rearrange("p (r g q) -> p r g q", r=2, g=32), tB.res)
        cur = 0
        for k in range(cps):
            S_, Sn = Sv[cur], Sv[1 - cur]
            c.tt(tAv, S_, lamA, ALU.mult)
            c.tt(tBv, T(S_.ap[:, ::-1], S_.res), lamB, ALU.mult)
            c.tt(tAv, tAv, tBv, ALU.add)
            for hs, col in ((slice(0, 64), k + 1), (slice(64, 128), cps - k)):
                c.tt(Sn[hs], tAv[hs], T(Zv[hs, :, :, :, col], Z.res), ALU.add)
                c.copy(T(Zv[hs, :, :, :, col], Z.res), Sn[hs], eng="act")
            cur = 1 - cur
        if grp == 0:
            fin = Sv[cur]
            for ri, nm in ((0, "ssm_re"), (1, "ssm_im")):
                pt = self.psum("aux")
                c.transpose(pt[:, 0:128], T(fin.ap[:, ri].rearrange("p g q -> p (g q)"), fin.res), self.ident_f)
                st = self.r_st32.next().f32()
                c.copy(st, pt[:, 0:128])
                for q in range(4):
                    c.dma("sp", self.O[nm][q, l].rearrange("d g p -> g d p"),
                          st.ap[q::4, :].rearrange("g (d p) -> g d p", d=2), reads=[st.res], is_output=True)
        for t_ in St + [tA, tB]:
            t_.free()
        Y = A.alloc(32 * N * 2, "s5Y")
        Yv = Y.bf().ap.rearrange("p (g n) -> p g n", g=32)
        wt_n = wtile(0)
        for g in range(32):
            wt = wt_n
            if g + 1 < 32:
                wt_n = wtile(g + 1)
            ps = self.psum()
            c.mm(ps[:, 0:N], wt[:, 0, :], T(Uv[:, g, :], U.res), start=True, stop=False)
            psq = T(ps.ap[:, 0:N].rearrange("p (q k) -> p q k", q=nseq), ps.res)
            for ri in range(2):
                c.mm(psq, wt[:, 3 + ri, :], T(Zv[:, ri, g, :, 0:cps], Z.res), start=False, stop=False)
                c.mm(psq, wt[:, 5 + ri, :], T(Zv[:, ri, g, :, 2:cps + 2], Z.res), start=False, stop=(ri == 1))
            c.copy(T(Yv[:, g, :], Y.res), ps[:, 0:N], eng=("act" if g % 2 else "dve"))
        U.free()
        Z.free()
        wst.free()
        gT = A.alloc(4 * nt * 2, "s5g")
        gv = gT.bf().ap.rearrange("p (k t) -> p k t", k=4)
        for kc in range(4):
            for tp_ in range(8):
                ps = self.psum()
                for gl in range(8):
                    c.mm(ps[:, 0:N], T(wav[:, tp_, 112 - 16 * gl: 240 - 16 * gl], wa.res), T(Yv[:, kc * 8 + gl, :], Y.res),
                         start=(gl == 0), stop=(gl == 7))
                x_ = self.r_f32.next().f32()[:, 0:N]
                c.copy(x_, ps[:, 0:N], eng="act")
                u_ = self.r_f32.next().f32()[:, 0:N]
                c.tt(u_, x_, x_, ALU.mult)
                c.ts(u_, u_, 0.044715, 1.0, ALU.mult, ALU.add)
                c.tt(u_, u_, x_, ALU.mult)
                c.act(u_, u_, AF.Sigmoid, scale=1.5957691216057308)
                dst = gv[:, kc, :].rearrange("p (n s) -> p s n", s=8)[:, tp_, :]
                c.tt(T(dst, gT.res), x_, u_, ALU.mult)
        Y.free()
        wa.free()
        self.ybT = [A.alloc(4 * 512 * 2, "ybT%d" % i) for i in range(self.ntt)]
        for ko in range(4):
            w = self.wload(I["w_glu"][l][:, ko * 128:(ko + 1) * 128], 4)
            for tt in range(self.ntt):
                ps = self.psum()
                for k in range(4):
                    c.mm(ps, w[:, k, :], T(gv[:, k, tt * 512:(tt + 1) * 512], gT.res), start=(k == 0), stop=(k == 3))
                sg = self.r_f32.next().f32()
                c.act(sg, ps, AF.Sigmoid)
                yv = self.ybT[tt].bf().ap.rearrange("p (k t) -> p k t", k=4)
                c.tt(T(yv[:, ko, :], self.ybT[tt].res), T(gv[:, ko, tt * 512:(tt + 1) * 512], gT.res), sg, ALU.mult)
        gT.free()

    def load_x_tile(self, tt, first):
        c = self.ctx
        A = self.arena
        I = self.I
        gt = self.t0 // 512 + tt
        xt = A.alloc(8 * 512 * 4, "xT")
        xv = xt.f32().ap.rearrange("p (k t) -> p k t", k=8)
        import os
        if not first or os.environ.get("NOFP32"):
            c.dma("sp", xv, self.xs[:, :, gt * 512:(gt + 1) * 512].rearrange("k p t -> p k t"),
                  reads=[self.xs_res[k][gt] for k in range(8)], writes=[xt.res])
            return xt
        for k in range(8):
            pass
        tm = [A.alloc(1024 * 4, "xtm%d" % j) for j in range(4)]
        for j in range(4):
            c.dma("sp", tm[j].f32().ap, I["xin"][gt * 512 + j * 128: gt * 512 + (j + 1) * 128, :], writes=[tm[j].res])
        for k in range(8):
            ps = self.psum()
            for j in range(4):
                c.transpose(ps[:, j * 128:(j + 1) * 128], tm[j].f32()[:, k * 128:(k + 1) * 128], self.ident_f,
                            signal=(j == 3))
            c.copy(T(xv[:, k, :], xt.res), ps, eng=("act" if k % 2 else "dve"))
            c.dma("sp", self.xs[k][:, gt * 512:(gt + 1) * 512], xv[:, k, :], reads=[xt.res],
                  writes=[self.xs_res[k][gt]])
        for j in range(4):
            tm[j].free()
        return xt

    def norm_in(self, l, first, which):
        c = self.ctx
        A = self.arena
        n = self.grp
        ia, ib = (0, 1) if which == 0 else (3, 4)
        for tt in range(self.ntt):
            xt = self.load_x_tile(tt, first)
            self.norm_tile(l, xt, self.hT[tt], ia, ib)
            xt.free()

    def norm_tile(self, l, xt, hT, ia, ib):
        c = self.ctx
        A = self.arena
        n = self.grp
        if True:
            xv = xt.f32().ap.rearrange("p (k t) -> p k t", k=8)
            sq = A.alloc(512 * 2 * 2, "sq")
            sqv = sq.bf().ap.rearrange("p (b t) -> p b t", b=2)
            ps = self.psum()
            for k in range(8):
                c.act(T(sqv[:, k % 2, :], sq.res), T(xv[:, k, :], xt.res), AF.Square)
                c.mm(ps, self.ones_b, T(sqv[:, k % 2, :], sq.res), start=(k == 0), stop=(k == 7), signal=True)
            rs = A.alloc(512 * 4, "rstd")
            c.act(rs.f32(), ps, AF.Ln, bias=EPS, scale=1.0)
            c.act(rs.f32(), rs.f32(), AF.Exp, scale=-0.5)
            hv = hT.bf().ap.rearrange("p (k t) -> p k t", k=8)
            for k in range(8):
                c.tt(T(xv[:, k, :], xt.res), T(xv[:, k, :], xt.res), rs.f32(), ALU.mult)
                m = self.modS
                c.act(T(hv[:, k, :], hT.res), T(xv[:, k, :], xt.res), AF.Identity,
                      bias=T(m.ap[:, l, n, ib, k:k + 1], m.res), scale=T(m.ap[:, l, n, ia, k:k + 1], m.res))
            sq.free()
            rs.free()


def _prep_inputs(inp, core):
    f = np.float32
    m = {}
    xp = np.asarray(inp["x_prompt"], f)[4 * core:4 * core + 4].reshape(NP_TOK, D)
    xsam = np.asarray(inp["x_sample"], f)[core]
    m["xin"] = np.ascontiguousarray(np.concatenate([xp, xsam], 0))
    cv = np.stack([np.asarray(inp["c_ctx"], f), np.asarray(inp["c"], f)[core]], 0)
    m["cvecT"] = np.ascontiguousarray(cv.reshape(2, 8, 128).transpose(2, 1, 0))
    m["state_re"] = np.ascontiguousarray(np.asarray(inp["state_ssm_re"], f)[core])
    m["state_im"] = np.ascontiguousarray(np.asarray(inp["state_ssm_im"], f)[core])
    m["cache_na_k"] = np.ascontiguousarray(np.asarray(inp["cache_na_k"], f)[core].reshape(DEPTH, 256, 512))
    m["cache_na_v"] = np.ascontiguousarray(np.asarray(inp["cache_na_v"], f)[core].reshape(DEPTH, 256, 512))
    m["cache_ga_k"] = np.ascontiguousarray(np.asarray(inp["cache_ga_k"], f)[core].reshape(DEPTH, 256, 128))
    m["cache_ga_v"] = np.ascontiguousarray(np.asarray(inp["cache_ga_v"], f)[core].reshape(DEPTH, 256, 128))
    return m


def _shared_inputs(inp):
    f = np.float32
    m = {}
    m["w_mod"] = np.ascontiguousarray(np.asarray(inp["w_mod"], f))
    m["b_modT"] = np.ascontiguousarray(np.asarray(inp["b_mod"], f).reshape(DEPTH, 48, 128).transpose(2, 0, 1))
    m["norm_gT"] = np.ascontiguousarray(np.asarray(inp["norm_g"], f).reshape(DEPTH, 4, 8, 128).transpose(3, 0, 1, 2))
    m["w_in"] = np.ascontiguousarray(np.asarray(inp["w_in"], f))
    m["na_rpb"] = np.ascontiguousarray(np.asarray(inp["na_rpb"], f))
    for nm in ("ssm_lam_re", "ssm_lam_im", "ssm_log_step", "ssm_b_re", "ssm_b_im", "ssm_d", "w_glu"):
        m[nm] = np.ascontiguousarray(np.asarray(inp[nm], f))
    m["ssm_c_re"] = np.ascontiguousarray(np.asarray(inp["ssm_c_re"], f).reshape(DEPTH, 2, 512, 64))
    m["ssm_c_im"] = np.ascontiguousarray(np.asarray(inp["ssm_c_im"], f).reshape(DEPTH, 2, 512, 64))
    for nm in ("w_br_a", "w_br_b", "w_br_c", "w_out", "w_up", "w_down"):
        m[nm] = np.ascontiguousarray(np.asarray(inp[nm], f))
    m["conv_wT"] = np.ascontiguousarray(np.asarray(inp["conv_w"], f).reshape(DEPTH, 3, 44, 128).transpose(3, 0, 1, 2))
    m["conv_bT"] = np.ascontiguousarray(np.asarray(inp["conv_b"], f).reshape(DEPTH, 44, 128).transpose(2, 0, 1))
    qk = np.asarray(inp["qk_norm_g"], f)
    m["qk_gT"] = np.ascontiguousarray(np.concatenate([qk, qk], -1).transpose(2, 0, 1))
    m.update(_host_consts())
    return m


_DEBUG = []


def run(inputs, debug=None):
    dbg = debug if debug is not None else _DEBUG
    dry = Builder(debug=dbg)
    dry.build()
    b = Builder(debug=dbg, wplan=dry.wrec)
    nc = b.build()
    shared = _shared_inputs(inputs)
    in_maps = []
    for core in range(NCORES):
        m = dict(shared)
        m.update(_prep_inputs(inputs, core))
        in_maps.append(m)
    res = run_bass_kernel_spmd(nc, in_maps, core_ids=list(range(NCORES)))
    return res.results, b


def kernel(**inputs):
    results, b = run(inputs)
    f = np.float32
    y_p = np.concatenate([r["y"][:NP_TOK].reshape(4, SEQ, D) for r in results], 0).astype(f)
    y_s = np.stack([r["y"][NP_TOK:] for r in results], 0).astype(f)
    ga_k = np.concatenate([r["ga_k"].reshape(4, DEPTH, SEQ, 2, HD) for r in results], 0).astype(f)
    ga_v = np.concatenate([r["ga_v"].reshape(4, DEPTH, SEQ, 2, HD) for r in results], 0).astype(f)
    na_k = np.concatenate([r["na_k"].reshape(4, DEPTH, SEQ, 8, HD) for r in results], 0).astype(f)
    na_v = np.concatenate([r["na_v"].reshape(4, DEPTH, SEQ, 8, HD) for r in results], 0).astype(f)
    s_re = np.concatenate([r["ssm_re"].reshape(4, DEPTH, 2, 32, 64) for r in results], 0).astype(f)
    s_im = np.concatenate([r["ssm_im"].reshape(4, DEPTH, 2, 32, 64) for r in results], 0).astype(f)
    return (y_p, y_s, ga_k, ga_v, na_k, na_v, s_re, s_im)
```

```python
import math
from contextlib import ExitStack

import numpy as np
import ml_dtypes
import concourse.bass as bass
import concourse.mybir as mybir
from concourse.bass_utils import run_bass_kernel_spmd

F32 = mybir.dt.float32
BF16 = mybir.dt.bfloat16
AF = mybir.ActivationFunctionType
ALU = mybir.AluOpType

D = 1024
DEPTH = 2
NCORES = 8
SEQ = 256
DEC_SEQ = 2048
NP_TOK = 1024
NTOK = NP_TOK + DEC_SEQ
HD = 64
IN_W = 5888
D_FF = 2816
EPS = 1e-6
O_GQ, O_GK, O_GV, O_U, O_NQ, O_NK, O_NV, O_G = 0, 512, 640, 768, 1280, 1792, 2304, 2816


class Res:
    __slots__ = ("w", "r", "name", "psum")

    def __init__(self, name="", psum=False):
        self.w = {}
        self.r = {}
        self.name = name
        self.psum = psum


class T:
    __slots__ = ("ap", "res")

    def __init__(self, ap, res):
        self.ap = ap
        self.res = res

    def __getitem__(self, k):
        return T(self.ap[k], self.res)

    def v(self, ap):
        return T(ap, self.res)


class Eng:
    def __init__(self, name, h, sem):
        self.name = name
        self.h = h
        self.sem = sem
        self.count = 0
        self.waited = {}
        self.n_ops = 0
        self.n_waits = 0
        self.prog = []


def _aps(x):
    return x.ap if isinstance(x, T) else x


class Ctx:
    def __init__(self, nc, es, n_dma_sems=40):
        self.nc = nc
        self.engs = {}
        for name, h in (("pe", nc.tensor), ("act", nc.scalar), ("dve", nc.vector),
                        ("pool", nc.gpsimd), ("sp", nc.sync)):
            sem = es.enter_context(nc.semaphore("sem_" + name))
            self.engs[name] = Eng(name, h, sem)
        self.dma_sems = {}
        self.dma_cnt = {}
        self.dma_i = {}
        for q, n in (("sp", 24), ("pool", 24), ("act", 8)):
            self.dma_sems[q] = [es.enter_context(nc.semaphore("dsem_%s%d" % (q, i))) for i in range(n)]
            self.dma_cnt[q] = [0] * n
            self.dma_i[q] = 0
        self.out_clock = {}
        self.n_inst = 0

    def _wait(self, eng, need):
        for sem, val in need.items():
            if eng.waited.get(sem, 0) < val:
                eng.prog.append(("w", sem, val))
                eng.waited[sem] = val
                eng.n_waits += 1

    def op(self, ename, fn, reads=(), writes=(), signal=True):
        eng = self.engs[ename]
        need = {}
        for r in reads:
            for s, v in r.w.items():
                if s is eng.sem and ename == "pe":
                    continue
                if need.get(s, 0) < v:
                    need[s] = v
            if r.psum:
                for s, v in r.r.items():
                    if s is eng.sem:
                        continue
                    if need.get(s, 0) < v:
                        need[s] = v
        for w in writes:
            for d in (w.w, w.r):
                for s, v in d.items():
                    if s is eng.sem and ename == "pe":
                        continue
                    if need.get(s, 0) < v:
                        need[s] = v
        self._wait(eng, need)
        self.n_inst += 1
        eng.n_ops += 1
        if signal:
            eng.count += 1
            eng.prog.append(("o", fn, eng.sem, 1))
            idx = eng.count
        else:
            eng.prog.append(("o", fn, None, 0))
            idx = eng.count + 1
        inst = None
        for w in writes:
            w.w = {eng.sem: idx}
            w.r = {}
        for r in reads:
            if r.r.get(eng.sem, 0) < idx:
                r.r[eng.sem] = idx
        return inst

    def dma(self, q, out, in_, reads=(), writes=(), is_output=False, **kw):
        eng = self.engs[q]
        need = {}
        for r in reads:
            for s, v in r.w.items():
                if need.get(s, 0) < v:
                    need[s] = v
        for w in writes:
            for d in (w.w, w.r):
                for s, v in d.items():
                    if need.get(s, 0) < v:
                        need[s] = v
        self._wait(eng, need)
        i = self.dma_i[q] % len(self.dma_sems[q])
        self.dma_i[q] += 1
        sem = self.dma_sems[q][i]
        if self.dma_cnt[q][i] > 0:
            self._wait(eng, {sem: self.dma_cnt[q][i]})
        self.dma_cnt[q][i] += 16
        val = self.dma_cnt[q][i]
        eng.prog.append(("o", (lambda e, out=out, in_=in_, kw=kw: e.dma_start(out=out, in_=in_, **kw)), sem, 16))
        self.n_inst += 1
        eng.n_ops += 1
        for w in writes:
            w.w = {sem: val}
            w.r = {}
        for r in reads:
            if r.r.get(sem, 0) < val:
                r.r[sem] = val
        if is_output:
            self.out_clock[sem] = max(self.out_clock.get(sem, 0), val)

    def _rw(self, outs, ins):
        writes = [o.res for o in outs if isinstance(o, T)]
        reads = [i.res for i in ins if isinstance(i, T)]
        return reads, writes

    def mm(self, out, lhsT, rhs, start=True, stop=True, signal=None):
        reads, writes = self._rw([out], [lhsT, rhs])
        if not start:
            reads = reads
        return self.op("pe", lambda e: e.matmul(_aps(out), lhsT=_aps(lhsT), rhs=_aps(rhs), start=start, stop=stop),
                       reads, writes, signal=(stop if signal is None else signal))

    def transpose(self, out, in_, ident, signal=True):
        reads, writes = self._rw([out], [in_, ident])
        return self.op("pe", lambda e: e.transpose(_aps(out), _aps(in_), _aps(ident)), reads, writes, signal=signal)

    def act(self, out, in_, func, bias=None, scale=None, accum_out=None):
        ins = [in_]
        kw = {}
        if bias is not None:
            kw["bias"] = _aps(bias)
            ins.append(bias)
        if scale is not None:
            kw["scale"] = _aps(scale)
            ins.append(scale)
        outs = [out]
        if accum_out is not None:
            kw["accum_out"] = _aps(accum_out)
            outs.append(accum_out)
        reads, writes = self._rw(outs, ins)
        return self.op("act", lambda e: e.activation(out=_aps(out), in_=_aps(in_), func=func, **kw), reads, writes)

    def tt(self, out, in0, in1, op, eng="dve"):
        reads, writes = self._rw([out], [in0, in1])
        return self.op(eng, lambda e: e.tensor_tensor(out=_aps(out), in0=_aps(in0), in1=_aps(in1), op=op), reads, writes)

    def ts(self, out, in0, s1, s2, op0, op1=None, eng="dve"):
        reads, writes = self._rw([out], [in0, s1, s2])
        if op1 is None:
            return self.op(eng, lambda e: e.tensor_single_scalar(out=_aps(out), in_=_aps(in0), scalar=_aps(s1), op=op0),
                           reads, writes)
        return self.op(eng, lambda e: e.tensor_scalar(out=_aps(out), in0=_aps(in0), scalar1=_aps(s1), scalar2=_aps(s2),
                                                      op0=op0, op1=op1), reads, writes)

    def stt(self, out, in0, scalar, in1, op0, op1, eng="dve"):
        reads, writes = self._rw([out], [in0, scalar, in1])
        return self.op(eng, lambda e: e.scalar_tensor_tensor(out=_aps(out), in0=_aps(in0), scalar=_aps(scalar),
                                                             in1=_aps(in1), op0=op0, op1=op1), reads, writes)

    def copy(self, out, in_, eng="dve"):
        reads, writes = self._rw([out], [in_])
        if eng == "act":
            return self.op("act", lambda e: e.activation(out=_aps(out), in_=_aps(in_), func=AF.Copy), reads, writes)
        return self.op(eng, lambda e: e.tensor_copy(out=_aps(out), in_=_aps(in_)), reads, writes)

    def recip(self, out, in_):
        reads, writes = self._rw([out], [in_])
        return self.op("dve", lambda e: e.reciprocal(out=_aps(out), in_=_aps(in_)), reads, writes)

    def memset(self, out, val, eng="dve"):
        reads, writes = self._rw([out], [])
        return self.op(eng, lambda e: e.memset(_aps(out), val), reads, writes)

    def finish(self):
        eng = self.engs["sp"]
        self._wait(eng, self.out_clock)
        need = {}
        for n in ("pe", "act", "dve", "pool"):
            e = self.engs[n]
            if e.count > 0:
                need[e.sem] = e.count
        self._wait(eng, need)
        self.emit()

    def emit(self):
        nc = self.nc

        def replay(eng, h):
            for it in eng.prog:
                if it[0] == "w":
                    h.wait_ge(it[1], it[2])
                else:
                    inst = it[1](h)
                    if it[2] is not None:
                        inst.then_inc(it[2], it[3])

        with nc.Block() as block:
            @block.tensor
            def _(e):
                replay(self.engs["pe"], e)

            @block.scalar
            def _(e):
                replay(self.engs["act"], e)

            @block.vector
            def _(e):
                replay(self.engs["dve"], e)

            @block.gpsimd
            def _(e):
                replay(self.engs["pool"], e)

            @block.sync
            def _(e):
                replay(self.engs["sp"], e)


class Arena:
    def __init__(self, ap_f32, nbytes):
        self.base = ap_f32
        self.size = nbytes
        self.free = [(0, nbytes)]
        self.hist = []

    def alloc(self, nbytes, name=""):
        nbytes = (nbytes + 63) // 64 * 64
        for i, (s, e) in enumerate(self.free):
            if e - s >= nbytes:
                self.free[i] = (s + nbytes, e)
                if self.free[i][0] == self.free[i][1]:
                    del self.free[i]
                break
        else:
            raise RuntimeError("SBUF arena OOM for %s (%d bytes); free=%s" % (name, nbytes, self.free))
        res = Res(name)
        st, en = s, s + nbytes
        keep = []
        for (hs, he, hr) in self.hist:
            if hs < en and st < he:
                for d in (hr.w, hr.r):
                    for k, v in d.items():
                        if res.w.get(k, 0) < v:
                            res.w[k] = v
                if hs >= st and he <= en:
                    continue
            keep.append((hs, he, hr))
        self.hist = keep
        return Tile(self, st, nbytes, res)

    def release(self, tile):
        s, e = tile.start, tile.start + tile.nbytes
        self.hist.append((s, e, tile.res))
        self.free.append((s, e))
        self.free.sort()
        merged = []
        for (a, b) in self.free:
            if merged and merged[-1][1] == a:
                merged[-1] = (merged[-1][0], b)
            else:
                merged.append((a, b))
        self.free = merged


class Tile:
    def __init__(self, arena, start, nbytes, res):
        self.arena = arena
        self.start = start
        self.nbytes = nbytes
        self.res = res

    def f32(self, n=None):
        n = self.nbytes // 4 if n is None else n
        return T(self.arena.base[:, self.start // 4: self.start // 4 + n], self.res)

    def bf(self, n=None):
        n = self.nbytes // 2 if n is None else n
        ap = self.arena.base[:, self.start // 4: self.start // 4 + (n + 1) // 2].bitcast(BF16)
        return T(ap[:, 0:n], self.res)

    def free(self):
        self.arena.release(self)


class Ring:
    def __init__(self, arena, n, nbytes, name):
        self.tiles = [arena.alloc(nbytes, "%s%d" % (name, i)) for i in range(n)]
        self.i = 0

    def next(self):
        t = self.tiles[self.i % len(self.tiles)]
        self.i += 1
        return t

    def free(self):
        for t in self.tiles:
            t.free()


def _host_consts():
    c = {}
    c["ident_f"] = np.eye(128, dtype=np.float32)
    p = np.arange(128)
    d = p % 64
    a = d // 32
    f = d % 16
    t = np.arange(DEC_SEQ)
    inv = (10000.0 ** (-np.arange(16, dtype=np.float32) / 16)).astype(np.float32)
    pos = np.where(a[:, None] == 0, (t // 64)[None, :], (t % 64)[None, :]).astype(np.float32)
    ang = pos * inv[f][:, None]
    c["rope_cos"] = np.cos(ang).astype(np.float32)
    c["rope_sin"] = np.sin(ang).astype(np.float32)
    P = np.zeros((128, 128), np.float32)
    for m in range(128):
        dm = m % 64
        b = (dm % 32) // 16
        if b == 0:
            P[m, m + 16] = -1.0
        else:
            P[m, m - 16] = 1.0
    c["rotT"] = np.ascontiguousarray(P.T)
    bd = np.zeros((128, 128), np.float32)
    bd[:64, :64] = 1.0 / 64
    bd[64:, 64:] = 1.0 / 64
    c["bd64"] = bd
    def valid(i, jb):
        mk = np.zeros((128, 128), np.float32)
        for a in range(2):
            for b in range(2):
                kr = 2 * jb + a
                r = 2 * i + b
                r0 = min(max(r - 4, 0), 24)
                if not (r0 <= kr < r0 + 8):
                    continue
                cc = np.arange(64)
                c0 = np.clip(cc - 8, 0, 48)
                kc = np.arange(64)
                ok = (kc[:, None] >= c0[None, :]) & (kc[:, None] < c0[None, :] + 16)
                mk[a * 64:(a + 1) * 64, b * 64:(b + 1) * 64] = ok
        return mk
    cls = [(6, 6 + d) for d in range(-2, 3)] + [(0, j) for j in range(4)] + [(1, j) for j in range(4)] + \
          [(14, j) for j in range(12, 16)] + [(15, j) for j in range(12, 16)]
    wa = np.zeros((128, 8, 240), np.float32)
    for a in range(8):
        for hh in range(16):
            wa[a * 16 + hh, a, hh + 112] = 1.0
    c["ssm_wa"] = wa
    sh = np.arange(128) // 16
    c["ssm_maskF"] = (sh[:, None] <= sh[None, :]).astype(np.float32)
    c["ssm_maskB"] = (sh[:, None] >= sh[None, :]).astype(np.float32)
    c["na_mask"] = np.ascontiguousarray(np.stack([valid(i, j) for (i, j) in cls], 1))
    return c


class Builder:
    def __init__(self, debug=None, wplan=None):
        self.wplan = wplan
        self.wrec = []
        self.debug = debug or []
        self.nc = bass.Bass("TRN2", target_bir_lowering=False)
        self.dbg_out = {}

    def din(self, name, shape, dt=F32):
        return self.nc.dram_tensor(name, list(shape), dt, kind="ExternalInput").ap()

    def dout(self, name, shape, dt=F32):
        return self.nc.dram_tensor(name, list(shape), dt, kind="ExternalOutput").ap()

    def dscr(self, name, shape, dt=F32):
        return self.nc.dram_tensor(name, list(shape), dt).ap()

    def build(self):
        nc = self.nc
        I = {}
        I["xin"] = self.din("xin", [NTOK, D])
        I["cvecT"] = self.din("cvecT", [128, 8, 2])
        I["w_mod"] = self.din("w_mod", [DEPTH, D, 6 * D])
        I["b_modT"] = self.din("b_modT", [128, DEPTH, 48])
        I["norm_gT"] = self.din("norm_gT", [128, DEPTH, 4, 8])
        I["w_in"] = self.din("w_in", [DEPTH, D, IN_W])
        I["qk_gT"] = self.din("qk_gT", [128, DEPTH, 2])
        for k, shp in (("ident_f", [128, 128]), ("rope_cos", [128, DEC_SEQ]), ("rope_sin", [128, DEC_SEQ]),
                       ("rotT", [128, 128]), ("bd64", [128, 128])):
            I[k] = self.din(k, shp)
        for nm, shp in (("w_br_a", [DEPTH, 512, D]), ("w_br_b", [DEPTH, 512, D]), ("w_br_c", [DEPTH, 512, D]),
                        ("w_out", [DEPTH, D, D]), ("w_up", [DEPTH, D, 2 * D_FF]), ("w_down", [DEPTH, D_FF, D]),
                        ("conv_wT", [128, DEPTH, 3, 44]), ("conv_bT", [128, DEPTH, 44])):
            I[nm] = self.din(nm, shp)
        for nm, shp in (("ssm_wa", [128, 8, 240]), ("ssm_maskF", [128, 128]), ("ssm_maskB", [128, 128]),
                        ("ssm_lam_re", [DEPTH, 2, 32, 64]), ("ssm_lam_im", [DEPTH, 2, 32, 64]),
                        ("ssm_log_step", [DEPTH, 2, 32]), ("ssm_b_re", [DEPTH, 2, 32, 64, 16]),
                        ("ssm_b_im", [DEPTH, 2, 32, 64, 16]), ("ssm_c_re", [DEPTH, 2, 512, 64]),
                        ("ssm_c_im", [DEPTH, 2, 512, 64]), ("ssm_d", [DEPTH, 512]), ("w_glu", [DEPTH, 512, 512]),
                        ("state_re", [DEPTH, 2, 32, 64]), ("state_im", [DEPTH, 2, 32, 64])):
            I[nm] = self.din(nm, shp)
        I["cache_na_k"] = self.din("cache_na_k", [DEPTH, 256, 512])
        I["cache_na_v"] = self.din("cache_na_v", [DEPTH, 256, 512])
        I["na_rpb"] = self.din("na_rpb", [DEPTH, 8, 15, 31])
        I["na_mask"] = self.din("na_mask", [128, 21, 128])
        I["cache_ga_k"] = self.din("cache_ga_k", [DEPTH, 256, 128])
        I["cache_ga_v"] = self.din("cache_ga_v", [DEPTH, 256, 128])
        self.I = I
        O = {}
        O["y"] = self.dout("y", [NTOK, D])
        O["ga_k"] = self.dout("ga_k", [4, DEPTH, 256, 128])
        O["ga_v"] = self.dout("ga_v", [4, DEPTH, 256, 128])
        O["na_k"] = self.dout("na_k", [4, DEPTH, 256, 512])
        O["na_v"] = self.dout("na_v", [4, DEPTH, 256, 512])
        O["ssm_re"] = self.dout("ssm_re", [4, DEPTH, 2, 32, 64])
        O["ssm_im"] = self.dout("ssm_im", [4, DEPTH, 2, 32, 64])
        self.ssmw = self.dscr("ssmw", [DEPTH, 7, 32, 128, 128], BF16)
        self.ssmw_res = Res("ssmw")
        self.vp = self.dscr("na_vp", [8 * 15 * 127 + 128])
        self.vp_res = Res("na_vp")
        self.O = O
        for name, shape in self.debug:
            self.dbg_out[name] = self.dout("dbg_" + name, shape)
        self.xs = self.dscr("xs", [8, 128, NTOK])
        self.xs_res = [[Res("xs%d_%d" % (k, t)) for t in range(NTOK // 512)] for k in range(8)]

        with ExitStack() as es:
            arena_t = es.enter_context(nc.sbuf_tensor("arena", [128, 52000], F32))
            ps_t = es.enter_context(nc.psum_tensor("ps", [128, 4096], F32))
            self.ctx = Ctx(nc, es)
            self.arena = Arena(arena_t[:, :], 52000 * 4)
            self.ps_banks = [T(ps_t[:, b * 512:(b + 1) * 512], Res("ps%d" % b, psum=True)) for b in range(8)]
            self.ps_pi = {}
            self.program()
            self.ctx.finish()
        return nc

    PS_POOLS = {"mm": (0, 1, 2, 3), "acc": (4, 5), "aux": (6, 7), "all8": (0, 1, 2, 3, 4, 5, 6, 7)}

    def psum(self, pool="mm"):
        banks = self.PS_POOLS[pool]
        i = self.ps_pi.get(pool, 0)
        self.ps_pi[pool] = i + 1
        return self.ps_banks[banks[i % len(banks)]]

    W_SLOTS = 14
    W_AHEAD = 7

    def wring_init(self, nslots=None, slot_bytes=2048):
        nslots = nslots or self.W_SLOTS
        self.wslots = [self.arena.alloc(slot_bytes, "wslot%d" % i) for i in range(nslots)]
        self.w_i = 0
        self.w_issued = 0
        self.w_tiles = {}

    def _wissue(self, i, srcs, kchunks):
        slot = self.wslots[i % len(self.wslots)]
        ncols = sum(a.shape[1] for a in srcs)
        assert kchunks * ncols * 2 <= slot.nbytes
        t = slot.bf(kchunks * ncols)
        dst = t.ap.rearrange("p (k c) -> p k c", k=kchunks)
        o = 0
        for a in srcs:
            n = a.shape[1]
            self.ctx.dma("pool", dst[:, :, o:o + n], a.rearrange("(k p) c -> p k c", p=128), writes=[t.res])
            o += n
        self.w_tiles[i] = T(dst, t.res)

    def wload(self, src_ap, kchunks, ncols=None):
        srcs = list(src_ap) if isinstance(src_ap, (list, tuple)) else [src_ap]
        idx = self.w_i
        self.w_i += 1
        if self.wplan is None:
            self.wrec.append(([(a.tensor.name, a.offset, [list(x) for x in a.ap]) for a in srcs], kchunks))
            self._wissue(idx, srcs, kchunks)
            return self.w_tiles.pop(idx)
        rec = self.wplan[idx]
        assert rec[1] == kchunks and rec[0][0][1] == srcs[0].offset and rec[0][0][0] == srcs[0].tensor.name
        hi = min(idx + self.W_AHEAD, len(self.wplan) - 1)
        while self.w_issued <= hi:
            r_srcs, r_k = self.wplan[self.w_issued]
            aps = [bass.AP(self.I[nm].tensor, off, apl) for (nm, off, apl) in r_srcs]
            self._wissue(self.w_issued, aps, r_k)
            self.w_issued += 1
        return self.w_tiles.pop(idx)

    def program(self):
        c = self.ctx
        A = self.arena
        I = self.I
        cst = A.alloc(4 * (128 + 64 + 64 + 64 + 64 + 64), "consts")
        base = cst.f32()
        self.ident_f = base[:, 0:128]
        o = 128
        self.ident_b = T(base.ap[:, o:o + 64].bitcast(BF16), cst.res); o += 64
        self.ones_b = T(base.ap[:, o:o + 64].bitcast(BF16), cst.res); o += 64
        self.bd64_b = T(base.ap[:, o:o + 64].bitcast(BF16), cst.res); o += 64
        self.rot_b = T(base.ap[:, o:o + 64].bitcast(BF16), cst.res); o += 64
        c.dma("sp", self.ident_f.ap, I["ident_f"], writes=[cst.res])
        c.dma("pool", self.ident_b.ap, I["ident_f"], writes=[cst.res])
        c.dma("pool", self.bd64_b.ap, I["bd64"], writes=[cst.res])
        c.dma("pool", self.rot_b.ap, I["rotT"], writes=[cst.res])
        c.memset(self.ones_b, 1.0 / 1024)
        zt = A.alloc(512 * 2, "zeros")
        self.zeros_b = zt.bf()
        c.memset(self.zeros_b, 0.0)
        qkg = A.alloc(4 * DEPTH * 2, "qkg")
        self.qkg = T(qkg.f32(DEPTH * 2).ap.rearrange("p (l j) -> p l j", l=DEPTH), qkg.res)
        c.dma("sp", qkg.f32(DEPTH * 2).ap, I["qk_gT"].rearrange("p l j -> p (l j)"), writes=[qkg.res])
        cw = A.alloc(4 * DEPTH * 4 * 44, "convw")
        cwv = cw.f32(DEPTH * 4 * 44).ap.rearrange("p (l j c) -> p l j c", l=DEPTH, j=4)
        self.convw = T(cwv, cw.res)
        for l in range(DEPTH):
            c.dma("sp", cwv[:, l, 0:3, :], I["conv_wT"][:, l, :, :], writes=[cw.res])
            c.dma("sp", cwv[:, l, 3, :], I["conv_bT"][:, l, :], writes=[cw.res])
        self.wring_init()
        self.r_st32 = Ring(A, 3, 128 * 4, "rst32")
        self.lam8 = []
        import os
        for l in range(DEPTH if not os.environ.get("NOS5") else 0):
            self.ssm_prep(l)
        self.adaln()
        import os
        for grp in ((0, 1) if not os.environ.get("G0") else (0,)):
            self.group_pass(grp)

    def adaln(self):
        c = self.ctx
        A = self.arena
        I = self.I
        mod_t = A.alloc(4 * DEPTH * 2 * 6 * 8, "modS")
        self.modS = T(mod_t.f32().ap.rearrange("p (l n i k) -> p l n i k", l=DEPTH, n=2, i=6), mod_t.res)
        tmp = A.alloc(4 * 288, "adaln_tmp")
        tb = tmp.f32()
        cT = tb[:, 0:16]
        bm = tb[:, 16:112]
        raw = tb[:, 112:208]
        ng = tb[:, 208:272]
        scb = T(tb.ap[:, 272:280].bitcast(BF16), tmp.res)
        c.dma("sp", cT.ap, I["cvecT"].rearrange("p k n -> p (k n)"), writes=[tmp.res])
        c.dma("sp", bm.ap, I["b_modT"].rearrange("p l c -> p (l c)"), writes=[tmp.res])
        c.dma("sp", ng.ap, I["norm_gT"].rearrange("p l i k -> p (l i k)"), writes=[tmp.res])
        c.act(scb, cT, AF.Silu)
        scb3 = scb.ap.rearrange("p (k n) -> p k n", n=2)
        ngv = ng.ap.rearrange("p (l i k) -> p l i k", l=DEPTH, i=4)
        for l in range(DEPTH):
            ps = self.psum()
            loads = []
            for cc in range(48):
                loads.append(lambda cc=cc: self.wload(I["w_mod"][l][:, cc * 128:(cc + 1) * 128], 8, 128))
            pend = []
            DEPTHQ = 4
            for cc in range(48 + DEPTHQ):
                if cc < 48:
                    pend.append(loads[cc]())
                if cc >= DEPTHQ:
                    j = cc - DEPTHQ
                    w = pend[j]
                    for k in range(8):
                        c.mm(ps[:, 2 * j:2 * j + 2], w[:, k, :], T(scb3[:, k, :], scb.res), start=(k == 0), stop=(k == 7))
            rawv = raw.ap.rearrange("p (c n) -> p c n", n=2)
            bmv = bm.ap.rearrange("p (l c) -> p l c", l=DEPTH)[:, l, :]
            for n in range(2):
                c.tt(T(rawv[:, :, n], tmp.res), T(ps.ap[:, 0:96].rearrange("p (c n) -> p c n", n=2)[:, :, n], ps.res),
                     T(bmv, tmp.res), ALU.add)
            for n in range(2):
                def R(i):
                    return T(rawv[:, i * 8:(i + 1) * 8, n], tmp.res)
                m = self.modS
                c.stt(T(m.ap[:, l, n, 0, :], m.res), R(1), 1.0, T(ngv[:, l, 0, :], tmp.res), ALU.add, ALU.mult)
                c.copy(T(m.ap[:, l, n, 1, :], m.res), R(0))
                c.tt(T(m.ap[:, l, n, 2, :], m.res), R(2), T(ngv[:, l, 1, :], tmp.res), ALU.mult)
                c.stt(T(m.ap[:, l, n, 3, :], m.res), R(4), 1.0, T(ngv[:, l, 2, :], tmp.res), ALU.add, ALU.mult)
                c.copy(T(m.ap[:, l, n, 4, :], m.res), R(3))
                c.tt(T(m.ap[:, l, n, 5, :], m.res), R(5), T(ngv[:, l, 3, :], tmp.res), ALU.mult)
        if "modS" in self.dbg_out:
            c.dma("sp", self.dbg_out["modS"], mod_t.f32().ap, reads=[mod_t.res], is_output=True)
        tmp.free()

    def group_pass(self, grp):
        self.grp = grp
        self.t0 = 0 if grp == 0 else NP_TOK
        self.nt = NP_TOK if grp == 0 else DEC_SEQ
        self.ntt = self.nt // 512
        A = self.arena
        c = self.ctx
        self.r_sq = Ring(A, 3, 512 * 2, "rsq")
        self.r_f32 = Ring(A, 6, 512 * 4, "rf32")
        self.r_small = Ring(A, 4, 64, "rsmall")
        self._pass_t0 = None
        for l in range(DEPTH):
            self.hT = [A.alloc(8 * 512 * 2, "hT%d" % i) for i in range(self.ntt)]
            self.norm_in(l, first=(l == 0), which=0)
            if l == 0 and ("hT%d" % grp) in self.dbg_out:
                for i, h in enumerate(self.hT):
                    self.ctx.dma("pool", self.dbg_out["hT%d" % grp][:, :, i * 512:(i + 1) * 512],
                                 h.bf().ap.rearrange("p (k t) -> p k t", k=8), reads=[h.res], is_output=True)
            self.mixer(l)
            self.ffn(l)
            for h in self.h2T:
                h.free()
        for r in (self.r_sq, self.r_f32, self.r_small):
            r.free()

    def qk_norm_rope(self, ps, l, which, rope_off, out_bf, f32_out=None):
        c = self.ctx
        sq = self.r_sq.next().bf()
        c.act(sq, ps, AF.Square)
        ps2 = self.psum()
        c.mm(ps2, self.bd64_b, sq)
        rs = self.r_f32.next().f32()
        c.act(rs, ps2, AF.Ln, bias=EPS, scale=1.0)
        c.act(rs, rs, AF.Exp, scale=-0.5)
        qn = self.r_f32.next().f32()
        c.tt(qn, ps, rs, ALU.mult)
        g = T(self.qkg.ap[:, l, which:which + 1], self.qkg.res)
        if rope_off is None:
            if f32_out is not None:
                c.act(f32_out, qn, AF.Identity, scale=g)
                c.copy(out_bf, f32_out)
            else:
                c.act(out_bf, qn, AF.Identity, scale=g)
            return
        qg = self.r_sq.next().bf()
        c.act(qg, qn, AF.Identity, scale=g)
        ps3 = self.psum()
        c.mm(ps3, self.rot_b, qg)
        t1 = self.r_f32.next().f32()
        c.tt(t1, qg, self.rope_cos[:, rope_off:rope_off + 512], ALU.mult)
        t2 = self.r_f32.next().f32()
        c.tt(t2, ps3, self.rope_sin[:, rope_off:rope_off + 512], ALU.mult)
        c.tt(out_bf, t1, t2, ALU.add)

    def mixer(self, l):
        A = self.arena
        self.yaT = [A.alloc(4 * 512 * 2, "yaT%d" % i) for i in range(self.ntt)]
        self.mixer_ga(l)
        if ("ya%d" % self.grp) in self.dbg_out and l == 0:
            for i, y in enumerate(self.yaT):
                self.ctx.dma("pool", self.dbg_out["ya%d" % self.grp][:, :, i * 512:(i + 1) * 512],
                             y.bf().ap.rearrange("p (k t) -> p k t", k=4), reads=[y.res], is_output=True)
        import os
        if os.environ.get("NOS5"):
            self.ybT = [A.alloc(4 * 512 * 2, "ybT%d" % i) for i in range(self.ntt)]
            for y in self.ybT:
                self.ctx.memset(y.bf(), 0.0)
        else:
            self.mixer_s5(l)
        self.ycT = [A.alloc(4 * 512 * 2, "ycT%d" % i) for i in range(self.ntt)]
        if os.environ.get("NONA"):
            for y in self.ycT:
                self.ctx.memset(y.bf(), 0.0)
        else:
            self.mixer_na(l)
        self.merge(l)
        for y in self.yaT + self.ybT + self.ycT + self.hT:
            y.free()
        self.h2T = [A.alloc(8 * 512 * 2, "h2T%d" % i) for i in range(self.ntt)]
        self.proj_out_residual(l, self.mT, "w_out", 8, 2, self.h2T, last=False)
        for m_ in self.mT:
            m_.free()

    def merge(self, l):
        c = self.ctx
        A = self.arena
        I = self.I
        self.mT = [A.alloc(8 * 512 * 2, "mT%d" % i) for i in range(self.ntt)]
        ys = (self.yaT, self.ybT, self.ycT)
        wn = ("w_br_a", "w_br_b", "w_br_c")
        for fo in range(8):
            wb = [self.wload(I[wn[b]][l][:, fo * 128:(fo + 1) * 128], 4) for b in range(3)]
            wg = [self.wload(I["w_in"][l][:, O_G + b * 1024 + fo * 128: O_G + b * 1024 + (fo + 1) * 128], 8)
                  for b in range(3)]
            for tt in range(self.ntt):
                h = self.hT[tt]
                hv = h.bf().ap.rearrange("p (k t) -> p k t", k=8)
                acc = self.r_f32.next().f32()
                for b in range(3):
                    y = ys[b][tt]
                    yv = y.bf().ap.rearrange("p (k t) -> p k t", k=4)
                    pp = self.psum("all8")
                    for k in range(4):
                        c.mm(pp, wb[b][:, k, :], T(yv[:, k, :], y.res), start=(k == 0), stop=(k == 3))
                    pg = self.psum("all8")
                    for k in range(8):
                        c.mm(pg, wg[b][:, k, :], T(hv[:, k, :], h.res), start=(k == 0), stop=(k == 7))
                    sg = self.r_f32.next().f32()
                    c.act(sg, pg, AF.Sigmoid)
                    if b == 0:
                        c.tt(acc, sg, pp, ALU.mult)
                    else:
                        c.tt(sg, sg, pp, ALU.mult)
                        if b == 1:
                            c.tt(acc, acc, sg, ALU.add)
                        else:
                            mv = self.mT[tt].bf().ap.rearrange("p (k t) -> p k t", k=8)
                            c.tt(T(mv[:, fo, :], self.mT[tt].res), acc, sg, ALU.add)

    def proj_out_residual(self, l, inT, wname, nk, ig, h2T, last):
        c = self.ctx
        A = self.arena
        I = self.I
        n = self.grp
        m = self.modS
        for tt in range(len(inT)):
            gt = self.t0 // 512 + tt if not hasattr(self, "_pass_t0") or self._pass_t0 is None else self._pass_t0 // 512 + tt
            src = inT[tt]
            sv = src.bf().ap.rearrange("p (k t) -> p k t", k=nk)
            ot = A.alloc(8 * 512 * 4, "oT")
            ov = ot.f32().ap.rearrange("p (k t) -> p k t", k=8)
            xt = A.alloc(8 * 512 * 4, "xT")
            xv = xt.f32().ap.rearrange("p (k t) -> p k t", k=8)
            c.dma("sp", xv, self.xs[:, :, gt * 512:(gt + 1) * 512].rearrange("k p t -> p k t"),
                  reads=[self.xs_res[k][gt] for k in range(8)], writes=[xt.res])
            msq = self.psum("acc")
            for fo in range(8):
                ps = self.psum()
                k0 = 0
                first = True
                while k0 < nk:
                    kn = min(8, nk - k0)
                    w = self.wload(I[wname][l][k0 * 128:(k0 + kn) * 128, fo * 128:(fo + 1) * 128], kn)
                    for k in range(kn):
                        c.mm(ps, w[:, k, :], T(sv[:, k0 + k, :], src.res), start=first, stop=(k0 + k == nk - 1))
                        first = False
                    k0 += kn
                c.copy(T(ov[:, fo, :], ot.res), ps, eng="act")
                sq = self.r_sq.next().bf()
                c.tt(sq, T(ov[:, fo, :], ot.res), T(ov[:, fo, :], ot.res), ALU.mult)
                c.mm(msq, self.ones_b, sq, start=(fo == 0), stop=(fo == 7), signal=True)
            rs = self.r_f32.next().f32()
            c.act(rs, msq, AF.Ln, bias=EPS, scale=1.0)
            c.act(rs, rs, AF.Exp, scale=-0.5)
            for k in range(8):
                c.tt(T(ov[:, k, :], ot.res), T(ov[:, k, :], ot.res), rs, ALU.mult)
                c.stt(T(xv[:, k, :], xt.res), T(ov[:, k, :], ot.res), T(m.ap[:, l, n, ig, k:k + 1], m.res),
                      T(xv[:, k, :], xt.res), ALU.mult, ALU.add)
                if not (last and l == DEPTH - 1):
                    c.dma("sp", self.xs[k][:, gt * 512:(gt + 1) * 512], xv[:, k, :], reads=[xt.res],
                          writes=[self.xs_res[k][gt]])
            ot.free()
            if not last:
                self.norm_tile(l, xt, h2T[tt], 3, 4)
            elif l == DEPTH - 1:
                for j in range(4):
                    orow = A.alloc(1024 * 4, "orow")
                    for k in range(8):
                        pt = self.psum("aux")
                        c.transpose(pt[:, 0:128], T(xv[:, k, j * 128:(j + 1) * 128], xt.res), self.ident_f)
                        c.copy(orow.f32()[:, k * 128:(k + 1) * 128], pt[:, 0:128], eng=("act" if k % 2 else "dve"))
                    c.dma("sp", self.O["y"][gt * 512 + j * 128: gt * 512 + (j + 1) * 128, :], orow.f32().ap,
                          reads=[orow.res], is_output=True)
                    orow.free()
            xt.free()

    def ffn(self, l):
        c = self.ctx
        A = self.arena
        I = self.I
        cwv = self.convw
        self.r_ub = Ring(A, 2, 1026 * 4 + 56, "rub")
        self.r_v = Ring(A, 3, 1024 * 4, "rv")
        for p0 in range(0, self.nt, 1024):
            tts = [p0 // 512, p0 // 512 + 1]
            act = A.alloc(22 * 1024 * 2, "ffn_act")
            av = act.bf().ap.rearrange("p (j t) -> p j t", j=22)
            nseg = 4 if self.grp == 0 else 1
            L = 1024 // nseg
            for j in range(22):
                wa = self.wload(I["w_up"][l][:, j * 128:(j + 1) * 128], 8)
                wg = self.wload(I["w_up"][l][:, D_FF + j * 128: D_FF + (j + 1) * 128], 8)
                vs = []
                for which, w in ((0, wa), (1, wg)):
                    cc = j + 22 * which
                    ub = self.r_ub.next()
                    ubv = ub.f32(1026)
                    for ti, tt in enumerate(tts):
                        h = self.h2T[tt]
                        hv = h.bf().ap.rearrange("p (k t) -> p k t", k=8)
                        ps = self.psum("all8")
                        for k in range(8):
                            c.mm(ps, w[:, k, :], T(hv[:, k, :], h.res), start=(k == 0), stop=(k == 7))
                        c.copy(ubv[:, 1 + ti * 512: 1 + (ti + 1) * 512], ps, eng="act")
                    for side, col in ((0, 0), (1, 1025)):
                        tok = p0 - 1 if side == 0 else p0 + 1024
                        if self.grp == 0 or tok < 0 or tok >= self.nt:
                            c.memset(ubv[:, col:col + 1], 0.0)
                        else:
                            h = self.h2T[tok // 512]
                            hv = h.bf().ap.rearrange("p (k t) -> p k t", k=8)
                            ps = self.psum("all8")
                            for k in range(8):
                                c.mm(ps[:, 0:1], w[:, k, :], T(hv[:, k, tok % 512: tok % 512 + 1], h.res),
                                     start=(k == 0), stop=(k == 7))
                            c.copy(ubv[:, col:col + 1], ps[:, 0:1])
                    v = self.r_v.next().f32(1024)
                    c.act(v, ubv[:, 1:1025], AF.Identity, bias=T(cwv.ap[:, l, 3, cc:cc + 1], cwv.res),
                          scale=T(cwv.ap[:, l, 1, cc:cc + 1], cwv.res))
                    v3 = T(v.ap.rearrange("p (s t) -> p s t", s=nseg), v.res)
                    ul = T(ubv.ap[:, 1:1025].rearrange("p (s t) -> p s t", s=nseg), ub.res)
                    if nseg == 1:
                        c.stt(v, ubv[:, 0:1024], T(cwv.ap[:, l, 0, cc:cc + 1], cwv.res), v, ALU.mult, ALU.add)
                        c.stt(v, ubv[:, 2:1026], T(cwv.ap[:, l, 2, cc:cc + 1], cwv.res), v, ALU.mult, ALU.add)
                    else:
                        c.stt(v3[:, :, 1:L], ul[:, :, 0:L - 1], T(cwv.ap[:, l, 0, cc:cc + 1], cwv.res), v3[:, :, 1:L],
                              ALU.mult, ALU.add)
                        c.stt(v3[:, :, 0:L - 1], ul[:, :, 1:L], T(cwv.ap[:, l, 2, cc:cc + 1], cwv.res),
                              v3[:, :, 0:L - 1], ALU.mult, ALU.add)
                    vs.append(v)
                c.act(vs[1], vs[1], AF.Silu)
                c.tt(T(av[:, j, :], act.res), vs[0], vs[1], ALU.mult)
            self._ffn_down(l, av, act, tts)
            act.free()
        self.r_ub.free()
        self.r_v.free()

    def _ffn_down(self, l, av, act, tts):
        c = self.ctx

        class V:
            def __init__(s_, ap, res):
                s_.ap_ = ap
                s_.res = res

            def bf(s_):
                return s_

            @property
            def ap(s_):
                return s_

            def rearrange(s_, *a, **k):
                return s_.ap_
        inT = [V(av[:, :, ti * 512:(ti + 1) * 512], act.res) for ti in range(2)]
        self._pass_t0 = self.t0 + tts[0] * 512
        self.proj_out_residual(l, inT, "w_down", 22, 5, None, last=True)
        self._pass_t0 = None

    def attn_rings(self, on):
        A = self.arena
        if on:
            self.r_p = Ring(A, 4, 512 * 2, "rp")
            self.r_q = Ring(A, 2, 4 * 512 * 2, "rq")
            self.r_yt = Ring(A, 3, 4 * 64 * 2, "ryt")
            self.r_st = Ring(A, 4, 128 * 4, "rst")
        else:
            for r in (self.r_p, self.r_q, self.r_yt, self.r_st):
                r.free()

    def mixer_ga(self, l):
        c = self.ctx
        A = self.arena
        I = self.I
        grp = self.grp
        nt = self.nt
        self.attn_rings(True)
        if grp == 1:
            self.rope_t = A.alloc(2 * DEC_SEQ * 4, "rope")
            rv = self.rope_t.f32().ap.rearrange("p (j t) -> p j t", j=2)
            self.rope_cos = T(rv[:, 0, :], self.rope_t.res)
            self.rope_sin = T(rv[:, 1, :], self.rope_t.res)
            c.dma("sp", rv[:, 0, :], self.I["rope_cos"], writes=[self.rope_t.res])
            c.dma("sp", rv[:, 1, :], self.I["rope_sin"], writes=[self.rope_t.res])
        nkeys = nt if grp == 0 else nt + 256
        nblk = nkeys // 128
        W = I["w_in"][l]
        kT = A.alloc(nkeys * 2, "ga_kT")
        kTv = kT.bf()
        va = A.alloc(nblk * 2 * 65 * 2, "ga_vaug")
        vab = va.bf(nblk * 2 * 65)
        vav = vab.ap.rearrange("p (b g e) -> p b g e", b=nblk, g=2)
        c.memset(vab, 1.0)
        wk = self.wload(W[:, O_GK:O_GK + 128], 8)
        wv = self.wload(W[:, O_GV:O_GV + 128], 8)
        import os
        SKIP = os.environ.get("SKIP", "")
        LIM = int(os.environ.get("LIM", "99"))
        for tt in range(min(self.ntt, LIM)):
            h = self.hT[tt]
            hv = h.bf().ap.rearrange("p (k t) -> p k t", k=8)
            if "K" not in SKIP:
                ps = self.psum()
                for k in range(8):
                    c.mm(ps, wk[:, k, :], T(hv[:, k, :], h.res), start=(k == 0), stop=(k == 7))
            if "K" in SKIP:
                pass
            elif "N" in SKIP:
                if "X" not in SKIP:
                    c.copy(kTv[:, tt * 512:(tt + 1) * 512], ps)
            elif grp == 0:
                kf = self.r_f32.next().f32()
                self.qk_norm_rope(ps, l, 1, None, kTv[:, tt * 512:(tt + 1) * 512], f32_out=kf)
                for j in range(4):
                    pt = self.psum("aux")
                    c.transpose(pt[:, 0:128], kf[:, j * 128:(j + 1) * 128], self.ident_f)
                    st = self.r_st.next().f32()
                    c.copy(st, pt[:, 0:128])
                    tok = tt * 512 + j * 128
                    c.dma("sp", self.O["ga_k"][tok // 256, l, tok % 256:tok % 256 + 128, :], st.ap,
                          reads=[st.res], is_output=True)
            else:
                self.qk_norm_rope(ps, l, 1, tt * 512, kTv[:, tt * 512:(tt + 1) * 512])
            for j in range(4 if "V" not in SKIP else 0):
                pv = self.psum("aux" if "A" in SKIP else "mm")
                for k in range(8):
                    c.mm(pv[:, 0:128], T(hv[:, k, j * 128:(j + 1) * 128], h.res), wv[:, k, :],
                         start=(k == 0), stop=(k == 7))
                blk = tt * 4 + j
                c.copy(T(vav[:, blk, :, 0:64], va.res),
                       T(pv.ap[:, 0:128].rearrange("p (g d) -> p g d", g=2), pv.res), eng="act")
                if grp == 0:
                    st = self.r_st.next().f32()
                    c.copy(st, pv[:, 0:128])
                    tok = tt * 512 + j * 128
                    c.dma("sp", self.O["ga_v"][tok // 256, l, tok % 256:tok % 256 + 128, :], st.ap,
                          reads=[st.res], is_output=True)
        if grp == 1 and "C" not in SKIP:
            for j in range(2):
                st = self.r_st.next().f32()
                c.dma("sp", st.ap, I["cache_ga_k"][l, j * 128:(j + 1) * 128, :], writes=[st.res])
                pt = self.psum("aux")
                c.transpose(pt[:, 0:128], st, self.ident_f)
                c.copy(kTv[:, nt + j * 128: nt + (j + 1) * 128], pt[:, 0:128])
            for j in range(2):
                c.dma("pool", vav[:, 16 + j, :, 0:64],
                      I["cache_ga_v"][l, j * 128:(j + 1) * 128, :].rearrange("p (g d) -> p g d", g=2),
                      writes=[va.res])
        import os
        CUT = int(os.environ.get("CUT", "99"))
        def emit_q(tt_):
            h_ = self.hT[tt_]
            hv_ = h_.bf().ap.rearrange("p (k t) -> p k t", k=8)
            qt_ = self.r_q.next()
            qv_ = qt_.bf().ap.rearrange("p (h t) -> p h t", h=4)
            for hl in range(4):
                w = self.wload([W[:, O_GQ + hl * 64:O_GQ + hl * 64 + 64],
                                W[:, O_GQ + (4 + hl) * 64:O_GQ + (4 + hl) * 64 + 64]], 8)
                ps = self.psum("aux")
                for k in range(8):
                    c.mm(ps, w[:, k, :], T(hv_[:, k, :], h_.res), start=(k == 0), stop=(k == 7))
                self.qk_norm_rope(ps, l, 0, (tt_ * 512 if grp == 1 else None), T(qv_[:, hl, :], qt_.res))
            return qt_, qv_
        q_next = emit_q(0)
        for tt in range(self.ntt if CUT > 0 else 0):
            qt, qv = q_next
            if tt + 1 < self.ntt:
                q_next = emit_q(tt + 1)
            yv = self.yaT[tt].bf().ap.rearrange("p (k t) -> p k t", k=4)
            its = []
            for sub in range(4):
                tok0 = tt * 512 + sub * 128
                if grp == 0:
                    sq_ = tok0 // 256
                    blocks = [2 * sq_, 2 * sq_ + 1]
                else:
                    blocks = list(range(nblk))
                for g in range(2):
                    for bi, kb in enumerate(blocks):
                        its.append((sub, g, bi, kb, len(blocks)))

            def emit_score(it):
                sub, g, bi, kb, nb = it
                s_ps = self.psum()
                c.mm(T(s_ps.ap.rearrange("p (h t) -> p h t", h=4), s_ps.res),
                     kTv[g * 64:(g + 1) * 64, kb * 128:(kb + 1) * 128],
                     T(qv[g * 64:(g + 1) * 64, :, sub * 128:(sub + 1) * 128], qt.res))
                return s_ps
            s_next = emit_score(its[0])
            o_ps = None
            for n_, it in enumerate(its):
                sub, g, bi, kb, nb = it
                s_ps = s_next
                if n_ + 1 < len(its):
                    s_next = emit_score(its[n_ + 1])
                pT = self.r_p.next().bf()
                c.act(pT, s_ps, AF.Exp, scale=0.125)
                if bi == 0:
                    o_ps = self.psum("acc")
                    c.mm(o_ps[:, 0:260], self.zeros_b[:, 0:128], self.zeros_b[:, 0:260], start=True, stop=False,
                         signal=False)
                for hl in range(4):
                    c.mm(o_ps[:, hl * 65:(hl + 1) * 65], pT[:, hl * 128:(hl + 1) * 128],
                         T(vav[:, kb, g, :], va.res), start=False,
                         stop=(bi == nb - 1 and hl == 3), signal=(bi == nb - 1 and hl == 3))
                if bi != nb - 1:
                    continue
                ov = o_ps.ap[:, 0:260].rearrange("p (h e) -> p h e", h=4)
                rec = self.r_small.next().f32()[:, 0:4]
                c.recip(rec, T(ov[:, :, 64], o_ps.res))
                yt = self.r_yt.next()
                ytv = yt.bf().ap.rearrange("p (h e) -> p h e", h=4)
                c.tt(T(ytv, yt.res), T(ov[:, :, 0:64], o_ps.res),
                     T(rec.ap.unsqueeze(2).broadcast_to([128, 4, 64]), rec.res), ALU.mult)
                for j in range(2):
                    tp = self.psum("aux")
                    tpb = T(tp.ap.bitcast(BF16)[:, 0:128], tp.res)
                    c.transpose(tpb, yt.bf()[:, j * 128:(j + 1) * 128], self.ident_b)
                    c.copy(T(yv[:, 2 * g + j, sub * 128:(sub + 1) * 128], self.yaT[tt].res), tpb)
        kT.free()
        va.free()
        self.attn_rings(False)
        if grp == 1:
            self.rope_t.free()

    def mixer_na(self, l):
        c = self.ctx
        A = self.arena
        I = self.I
        grp = self.grp
        nt = self.nt
        nkeys = nt if grp == 0 else nt + 256
        nblk = nkeys // 128
        W = I["w_in"][l]
        self.attn_rings(True)
        if grp == 1:
            zt = self.r_f32.next().f32()
            c.memset(zt, 0.0)
            c.dma("sp", zt.ap[0:120, 48:79], I["na_rpb"][l].rearrange("h r j -> (h r) j"), writes=[zt.res])
            nvp = 8 * 15 * 127
            c.dma("sp", self.vp[0:nvp].rearrange("(a j) -> a j", j=127), zt.ap[0:120, 0:127], reads=[zt.res],
                  writes=[self.vp_res])
            mk = A.alloc(21 * 128 * 2, "na_mask")
            mkv = mk.bf().ap.rearrange("p (c q) -> p c q", c=21)
            c.dma("pool", mkv, I["na_mask"], writes=[mk.res])
            hraw = A.alloc(7 * 2 * 64 * 4, "na_hraw")
            hrv = hraw.f32().ap.rearrange("p (d b c) -> p d b c", d=7, b=2)
            eb = A.alloc(7 * 128 * 2, "na_eb")
            ebv = eb.bf().ap.rearrange("p (d b c) -> p d b c", d=7, b=2)
            etabs = [A.alloc(21 * 128 * 2, "na_etab%d" % i) for i in range(2)]
        for j in range(4):
            kT = A.alloc(nkeys * 2, "na_kT")
            kTv = kT.bf()
            qT = A.alloc(nt * 2, "na_qT")
            qTv = qT.bf()
            va = A.alloc(nblk * 2 * 65 * 2, "na_vaug")
            vab = va.bf(nblk * 2 * 65)
            vav = vab.ap.rearrange("p (b g e) -> p b g e", b=nblk, g=2)
            c.memset(vab, 1.0)
            wq = self.wload(W[:, O_NQ + j * 128:O_NQ + (j + 1) * 128], 8)
            wk = self.wload(W[:, O_NK + j * 128:O_NK + (j + 1) * 128], 8)
            wv = self.wload(W[:, O_NV + j * 128:O_NV + (j + 1) * 128], 8)
            for tt in range(self.ntt):
                h = self.hT[tt]
                hv = h.bf().ap.rearrange("p (k t) -> p k t", k=8)
                ps = self.psum()
                for k in range(8):
                    c.mm(ps, wq[:, k, :], T(hv[:, k, :], h.res), start=(k == 0), stop=(k == 7))
                c.copy(qTv[:, tt * 512:(tt + 1) * 512], ps, eng="act")
                ps = self.psum()
                for k in range(8):
                    c.mm(ps, wk[:, k, :], T(hv[:, k, :], h.res), start=(k == 0), stop=(k == 7))
                if grp == 0:
                    kf = self.r_f32.next().f32()
                    c.copy(kf, ps, eng="act")
                    c.copy(kTv[:, tt * 512:(tt + 1) * 512], kf)
                    for jj in range(4):
                        pt = self.psum("aux")
                        c.transpose(pt[:, 0:128], kf[:, jj * 128:(jj + 1) * 128], self.ident_f)
                        st = self.r_st.next().f32()
                        c.copy(st, pt[:, 0:128])
                        tok = tt * 512 + jj * 128
                        c.dma("sp", self.O["na_k"][tok // 256, l, tok % 256:tok % 256 + 128, j * 128:(j + 1) * 128],
                              st.ap, reads=[st.res], is_output=True)
                else:
                    c.copy(kTv[:, tt * 512:(tt + 1) * 512], ps)
                for jj in range(4):
                    pv = self.psum()
                    for k in range(8):
                        c.mm(pv[:, 0:128], T(hv[:, k, jj * 128:(jj + 1) * 128], h.res), wv[:, k, :],
                             start=(k == 0), stop=(k == 7))
                    blk = tt * 4 + jj
                    c.copy(T(vav[:, blk, :, 0:64], va.res),
                           T(pv.ap[:, 0:128].rearrange("p (g d) -> p g d", g=2), pv.res), eng="act")
                    if grp == 0:
                        st = self.r_st.next().f32()
                        c.copy(st, pv[:, 0:128])
                        tok = tt * 512 + jj * 128
                        c.dma("sp", self.O["na_v"][tok // 256, l, tok % 256:tok % 256 + 128, j * 128:(j + 1) * 128],
                              st.ap, reads=[st.res], is_output=True)
            if grp == 1:
                for b in range(2):
                    st = self.r_st.next().f32()
                    c.dma("sp", st.ap, I["cache_na_k"][l, b * 128:(b + 1) * 128, j * 128:(j + 1) * 128], writes=[st.res])
                    pt = self.psum("aux")
                    c.transpose(pt[:, 0:128], st, self.ident_f)
                    c.copy(kTv[:, nt + b * 128: nt + (b + 1) * 128], pt[:, 0:128])
                    c.dma("pool", vav[:, 16 + b, :, 0:64],
                          I["cache_na_v"][l, b * 128:(b + 1) * 128, j * 128:(j + 1) * 128].rearrange("p (g d) -> p g d", g=2),
                          writes=[va.res])
                for hh in range(2):
                    hd = 2 * j + hh
                    hres = [Res("hraw%d" % q_) for q_ in range(28)]
                    for r_ in hres:
                        r_.w = dict(hraw.res.w)
                        r_.r = dict(hraw.res.r)
                    c.op("dve", lambda e: e.memset(hraw.f32().ap, 0.0), reads=[], writes=hres)
                    for di, dj in enumerate(range(-3, 4)):
                        for a in range(2):
                            for b in range(2):
                                dr = 2 * dj + a - b + 7
                                if 0 <= dr <= 14:
                                    off = (hd * 15 + dr) * 127
                                    src = bass.AP(self.vp.tensor, off, [[1, 64], [1, 64]])
                                    c.dma("sp", hrv[a * 64:(a + 1) * 64, di, b, :], src, reads=[self.vp_res],
                                          writes=[hres[di * 4 + a * 2 + b]])
                    c.op("act", lambda e: e.activation(out=ebv, in_=hrv[:, :, :, ::-1], func=AF.Exp),
                         reads=hres, writes=[eb.res])
                    hraw.res.w = {}
                    hraw.res.r = {}
                    for r_ in hres:
                        for d_, dst_ in ((r_.w, hraw.res.w), (r_.r, hraw.res.r)):
                            for k_, v_ in d_.items():
                                if dst_.get(k_, 0) < v_:
                                    dst_[k_] = v_
                    et = etabs[hh]
                    etv = et.bf().ap.rearrange("p (c q) -> p c q", c=21)
                    ebc = eb.bf().ap.rearrange("p (d q) -> p d q", d=7)
                    for (c0_, n_, d0_) in ((0, 5, 1), (5, 4, 3), (9, 4, 2), (13, 4, 1), (17, 4, 0)):
                        c.tt(T(etv[:, c0_:c0_ + n_, :], et.res), T(ebc[:, d0_:d0_ + n_, :], eb.res),
                             T(mkv[:, c0_:c0_ + n_, :], mk.res), ALU.mult)
            import os
            NST = int(os.environ.get("NA_STAGE", "9"))
            if NST < 2:
                for y in self.ycT:
                    if j == 0:
                        c.memset(y.bf(), 0.0)
            elif grp == 0:
                for seq in range(4):
                    pTs = []
                    for kb in range(2):
                        pT = self.r_p.next().bf()
                        for hh in range(2):
                            s_ps = self.psum()
                            c.mm(s_ps[:, 0:256],
                                 kTv[hh * 64:(hh + 1) * 64, (seq * 2 + kb) * 128:(seq * 2 + kb + 1) * 128],
                                 qTv[hh * 64:(hh + 1) * 64, seq * 256:(seq + 1) * 256])
                            c.act(pT[:, hh * 256:(hh + 1) * 256], s_ps[:, 0:256], AF.Exp, scale=0.125)
                        pTs.append(pT)
                    if NST < 3:
                        if j == 0 and seq == 0:
                            for y in self.ycT:
                                c.memset(y.bf(), 0.0)
                        continue
                    for sub in range(2):
                        o_ps = self.psum("acc")
                        c.mm(o_ps[:, 0:130], self.zeros_b[:, 0:128], self.zeros_b[:, 0:130], start=True, stop=False,
                             signal=False)
                        for kb in range(2):
                            for hh in range(2):
                                last = (kb == 1 and hh == 1)
                                c.mm(o_ps[:, hh * 65:(hh + 1) * 65],
                                     pTs[kb][:, hh * 256 + sub * 128: hh * 256 + (sub + 1) * 128],
                                     T(vav[:, seq * 2 + kb, hh, :], va.res), start=False, stop=last, signal=last)
                        self._na_finish(o_ps, j, seq * 256 + sub * 128)
            else:
                its = []
                for i in range(16):
                    if i == 0:
                        loc, cls0 = [0, 1, 2, 3], 5
                    elif i == 1:
                        loc, cls0 = [0, 1, 2, 3], 9
                    elif i == 14:
                        loc, cls0 = [12, 13, 14, 15], 13
                    elif i == 15:
                        loc, cls0 = [12, 13, 14, 15], 17
                    else:
                        loc, cls0 = list(range(i - 2, i + 3)), 0
                    blocks = loc + [16, 17]
                    for hh in range(2):
                        for part in range(2):
                            its.append((i, hh, part, blocks[0:4] if part == 0 else blocks[4:], cls0, len(loc)))

                def emit_score(it):
                    i, hh, part, bl, cls0, nloc = it
                    s_ps = self.psum()
                    for bi, kb in enumerate(bl):
                        c.mm(s_ps[:, bi * 128:(bi + 1) * 128], kTv[hh * 64:(hh + 1) * 64, kb * 128:(kb + 1) * 128],
                             qTv[hh * 64:(hh + 1) * 64, i * 128:(i + 1) * 128])
                    return s_ps
                LOOK = 2
                pend = [emit_score(its[n_]) for n_ in range(min(LOOK, len(its)))]
                o_ps = None
                for n_, it in enumerate(its):
                    i, hh, part, bl, cls0, nloc = it
                    s_ps = pend.pop(0)
                    if n_ + LOOK < len(its):
                        pend.append(emit_score(its[n_ + LOOK]))
                    etv = etabs[hh].bf().ap.rearrange("p (c q) -> p c q", c=21)
                    pT = self.r_p.next().bf()
                    nb = len(bl)
                    c.act(pT[:, 0:nb * 128], s_ps[:, 0:nb * 128], AF.Exp, scale=0.125)
                    if part == 0:
                        nl = 4
                        c.tt(pT[:, 0:nl * 128], pT[:, 0:nl * 128],
                             T(etv[:, cls0:cls0 + nl, :].rearrange("p c q -> p (c q)"), etabs[hh].res), ALU.mult)
                    elif nloc == 5:
                        c.tt(pT[:, 0:128], pT[:, 0:128], T(etv[:, 4, :], etabs[hh].res), ALU.mult)
                    if hh == 0 and part == 0:
                        o_ps = self.psum("acc")
                        c.mm(o_ps[:, 0:130], self.zeros_b[:, 0:128], self.zeros_b[:, 0:130], start=True, stop=False,
                             signal=False)
                    for bi, kb in enumerate(bl):
                        last = (hh == 1 and part == 1 and bi == nb - 1)
                        c.mm(o_ps[:, hh * 65:(hh + 1) * 65], pT[:, bi * 128:(bi + 1) * 128],
                             T(vav[:, kb, hh, :], va.res), start=False, stop=last, signal=last)
                    if hh == 1 and part == 1:
                        self._na_finish(o_ps, j, i * 128)
            kT.free()
            qT.free()
            va.free()
        if grp == 1:
            for t_ in [mk, hraw, eb] + etabs:
                t_.free()
        self.attn_rings(False)

    def _na_finish(self, o_ps, j, tok0):
        c = self.ctx
        ov = o_ps.ap[:, 0:130].rearrange("p (h e) -> p h e", h=2)
        rec = self.r_small.next().f32()[:, 0:2]
        c.recip(rec, T(ov[:, :, 64], o_ps.res))
        yt = self.r_yt.next()
        ytv = yt.bf(128).ap.rearrange("p (h e) -> p h e", h=2)
        c.tt(T(ytv, yt.res), T(ov[:, :, 0:64], o_ps.res),
             T(rec.ap.unsqueeze(2).broadcast_to([128, 2, 64]), rec.res), ALU.mult)
        tp = self.psum("aux")
        tpb = T(tp.ap.bitcast(BF16)[:, 0:128], tp.res)
        c.transpose(tpb, yt.bf(128), self.ident_b)
        tt = tok0 // 512
        yv = self.ycT[tt].bf().ap.rearrange("p (k t) -> p k t", k=4)
        c.copy(T(yv[:, j, tok0 % 512: tok0 % 512 + 128], self.ycT[tt].res), tpb)

    def _cmul(self, o_re, o_im, a_re, a_im, b_re, b_im, tmp):
        c = self.ctx
        c.tt(o_re, a_re, b_re, ALU.mult)
        c.tt(tmp, a_im, b_im, ALU.mult)
        c.tt(o_re, o_re, tmp, ALU.subtract)
        c.tt(o_im, a_re, b_im, ALU.mult)
        c.tt(tmp, a_im, b_re, ALU.mult)
        c.tt(o_im, o_im, tmp, ALU.add)

    def _trig(self, out, bb, shift, tmp_v, tmp_n):
        c = self.ctx
        PI = math.pi
        c.ts(tmp_v, bb, shift + PI, None, ALU.add)
        c.ts(tmp_n, tmp_v, 2 * PI, None, ALU.is_ge)
        c.stt(tmp_n, tmp_v, 4 * PI, tmp_n, ALU.is_ge, ALU.add)
        c.stt(tmp_n, tmp_v, 6 * PI, tmp_n, ALU.is_ge, ALU.add)
        c.stt(tmp_v, tmp_n, -2 * PI, tmp_v, ALU.mult, ALU.add)
        c.act(out, tmp_v, AF.Sin, bias=self.negpi, scale=1.0)

    def ssm_prep(self, l):
        c = self.ctx
        A = self.arena
        I = self.I
        if not hasattr(self, "negpi"):
            t_ = A.alloc(64, "negpi")
            self.negpi = t_.f32()[:, 0:1]
            c.memset(self.negpi, -math.pi)
        small = A.alloc(4 * 32 * 24, "s5small")
        sm = small.f32().ap.rearrange("p (i g) -> p i g", g=32)

        def S(i):
            return T(sm[:, i, :], small.res)
        pw = A.alloc(4 * 2 * 16 * 32, "s5pw")
        pwv = pw.f32().ap.rearrange("p (r k g) -> p r k g", r=2, k=16)

        def PW(r, k):
            return T(pwv[:, r, k + 7, :], pw.res)
        bt = A.alloc(4 * 2 * 512, "s5B")
        btv = bt.f32().ap.rearrange("p (r g h) -> p r g h", r=2, g=32)
        ct = A.alloc(4 * 2 * 512, "s5C")
        ctv = ct.f32().ap.rearrange("p (r g h) -> p r g h", r=2, g=32)
        bb_t = A.alloc(4 * 2 * 512, "s5Bbar")
        bbv = bb_t.f32().ap.rearrange("p (r g h) -> p r g h", r=2, g=32)
        pr = A.alloc(4 * 4096, "s5pr")
        pi_ = A.alloc(4 * 4096, "s5pi")
        tmp = A.alloc(4 * 4096, "s5tmp")
        xd = A.alloc(2 * 4096, "s5X")
        yd = A.alloc(2 * 4096, "s5Y")
        zt = [A.alloc(2 * 4096, "s5ZT%d" % i) for i in range(2)]
        macc = A.alloc(4 * 4096, "s5M")
        cin = [A.alloc(2 * 4096, "s5Cin%d" % i) for i in range(4)]
        for t_ in cin:
            c.memset(t_.bf(), 0.0)
        msk = A.alloc(4 * 256, "s5mask")
        mskv = msk.f32().ap.rearrange("p (d c) -> p d c", d=2)
        c.dma("sp", mskv[:, 0, :], I["ssm_maskF"], writes=[msk.res])
        c.dma("sp", mskv[:, 1, :], I["ssm_maskB"], writes=[msk.res])
        dcol = A.alloc(4 * 32, "s5dcol")
        for s_ in range(8):
            c.dma("sp", dcol.f32().ap[s_ * 16:(s_ + 1) * 16, :], I["ssm_d"][l].rearrange("(g h) -> h g", h=16),
                  writes=[dcol.res], allow_slow_non_contiguous=True)
        lam8 = A.alloc(4 * 2 * 2 * 32, "lam8_%d" % l)
        l8v = lam8.f32().ap.rearrange("p (a r g) -> p a r g", a=2, r=2)
        self.lam8.append(T(l8v, lam8.res))
        r4 = lambda t_: t_.f32().ap.rearrange("p (g s h) -> p g s h", g=32, s=8)
        r4b = lambda t_: t_.bf().ap.rearrange("p (g s h) -> p g s h", g=32, s=8)
        for d in range(2):
            for ri, nm in ((0, "ssm_lam_re"), (1, "ssm_lam_im")):
                st = self.r_st32.next().f32()
                for hf in range(2):
                    c.dma("sp", st.ap[0:32, hf * 64:(hf + 1) * 64], I[nm][l, d], writes=[st.res])
                pt = self.psum("aux")
                c.transpose(pt[:, 0:32], st[0:32, :], self.ident_f[0:32, 0:32])
                c.copy(S(ri), pt[:, 0:32])
            c.dma("sp", sm[:, 2, :], I["ssm_log_step"][l, d:d + 1, :].partition_broadcast(128)
                  if hasattr(I["ssm_log_step"], "partition_broadcast") else I["ssm_log_step"][l, d:d + 1, :].broadcast_to([128, 32]),
                  writes=[small.res])
            for ri, nm in ((0, "ssm_b_re"), (1, "ssm_b_im")):
                for hf in range(2):
                    c.dma("sp", btv[hf * 64:(hf + 1) * 64, ri, :, :], I[nm][l, d].rearrange("g p h -> p g h"),
                          writes=[bt.res])
            for ri, nm in ((0, "ssm_c_re"), (1, "ssm_c_im")):
                for ch in range(4):
                    st = self.r_st32.next().f32()
                    for hf in range(2):
                        c.dma("sp", st.ap[:, hf * 64:(hf + 1) * 64], I[nm][l, d, ch * 128:(ch + 1) * 128, :],
                              writes=[st.res])
                    pt = self.psum("aux")
                    c.transpose(pt[:, 0:128], st, self.ident_f)
                    c.copy(T(ctv[:, ri, ch * 8:(ch + 1) * 8, :], ct.res),
                           T(pt.ap[:, 0:128].rearrange("p (g h) -> p g h", g=8), pt.res))
            c.act(S(2), S(2), AF.Exp)
            c.tt(S(3), S(0), S(2), ALU.mult)
            c.tt(S(4), S(1), S(2), ALU.mult)
            c.act(S(5), S(3), AF.Exp)
            c.act(S(6), S(3), AF.Exp, scale=-1.0)
            self._trig(S(7), S(4), 0.0, S(9), S(10))
            self._trig(S(8), S(4), math.pi / 2, S(9), S(10))
            c.memset(PW(0, 0), 1.0)
            c.memset(PW(1, 0), 0.0)
            c.tt(PW(0, 1), S(5), S(8), ALU.mult)
            c.tt(PW(1, 1), S(5), S(7), ALU.mult)
            c.tt(PW(0, -1), S(6), S(8), ALU.mult)
            c.tt(S(11), S(6), S(7), ALU.mult)
            c.ts(PW(1, -1), S(11), -1.0, None, ALU.mult)
            for k in range(1, 8):
                self._cmul(PW(0, k + 1), PW(1, k + 1), PW(0, k), PW(1, k), PW(0, 1), PW(1, 1), S(12))
            for k in range(1, 7):
                self._cmul(PW(0, -k - 1), PW(1, -k - 1), PW(0, -k), PW(1, -k), PW(0, -1), PW(1, -1), S(12))
            c.tt(S(15), S(0), S(0), ALU.mult)
            c.tt(S(16), S(1), S(1), ALU.mult)
            c.tt(S(15), S(15), S(16), ALU.add)
            c.recip(S(15), S(15))
            c.ts(S(16), PW(0, 1), -1.0, None, ALU.add)
            c.tt(S(17), S(16), S(0), ALU.mult)
            c.tt(S(18), PW(1, 1), S(1), ALU.mult)
            c.tt(S(17), S(17), S(18), ALU.add)
            c.tt(S(13), S(17), S(15), ALU.mult)
            c.tt(S(17), PW(1, 1), S(0), ALU.mult)
            c.tt(S(18), S(16), S(1), ALU.mult)
            c.tt(S(17), S(17), S(18), ALU.subtract)
            c.tt(S(14), S(17), S(15), ALU.mult)

            def bc_g(t_):
                return T(t_.ap.unsqueeze(2).broadcast_to([128, 32, 16]), t_.res)
            t512 = T(tmp.f32().ap[:, 0:512].rearrange("p (g h) -> p g h", g=32), tmp.res)
            self._cmul(T(bbv[:, 0], bb_t.res), T(bbv[:, 1], bb_t.res), bc_g(S(13)), bc_g(S(14)),
                       T(btv[:, 0], bt.res), T(btv[:, 1], bt.res), t512)
            hs = slice(d * 64, (d + 1) * 64)
            c.copy(T(l8v[hs, 0, 0, :], lam8.res), PW(0, 8)[hs])
            c.copy(T(l8v[hs, 0, 1, :], lam8.res), PW(0, 8)[hs])
            c.copy(T(l8v[hs, 1, 1, :], lam8.res), PW(1, 8)[hs])
            c.ts(T(l8v[hs, 1, 0, :], lam8.res), PW(1, 8)[hs], -1.0, None, ALU.mult)

            def pw_view(k0, step):
                base_re = pwv[:, 0, k0 + 7, :]
                base_im = pwv[:, 1, k0 + 7, :]
                def mk(b_):
                    return T(bass.AP(b_.tensor, b_.offset, [list(b_.ap[0]), [1, 32], [32 * step, 8], [0, 16]]), pw.res)
                return mk(base_re), mk(base_im)

            def mat_view(v4):
                def mk(r):
                    b_ = v4[:, r]
                    return T(bass.AP(b_.tensor, b_.offset, [list(b_.ap[0]), [16, 32], [0, 8], [1, 16]]), None)
                return mk(0), mk(1)
            prv, piv, tmv = T(r4(pr), pr.res), T(r4(pi_), pi_.res), T(r4(tmp), tmp.res)
            a_re, a_im = pw_view(7, -1) if d == 0 else pw_view(0, 1)
            b_re, b_im = mat_view(bbv)
            b_re.res = b_im.res = bb_t.res
            self._cmul(prv, piv, a_re, a_im, b_re, b_im, tmv)
            xv, yv_ = r4b(xd), r4b(yd)
            c.copy(T(xv[0:64], xd.res), prv[0:64])
            c.copy(T(xv[64:128], xd.res), piv[64:128], eng="act")
            c.copy(T(r4b(zt[0])[hs], zt[0].res), prv[hs])
            c.copy(T(r4b(zt[1])[hs], zt[1].res), piv[hs], eng="act")
            a_re, a_im = pw_view(-7, 1) if d == 0 else pw_view(0, -1)
            b_re, b_im = mat_view(ctv)
            b_re.res = b_im.res = ct.res
            self._cmul(prv, piv, a_re, a_im, b_re, b_im, tmv)
            c.copy(T(yv_[0:64], yd.res), prv[0:64])
            c.ts(T(yv_[64:128], yd.res), piv[64:128], -1.0, None, ALU.mult)
            mv = macc.f32().ap.rearrange("p (g c) -> p g c", g=32)
            xf = xd.bf().ap.rearrange("p (g c) -> p g c", g=32)
            yf = yd.bf().ap.rearrange("p (g c) -> p g c", g=32)
            for g in range(32):
                ps = self.psum()
                c.mm(ps[:, 0:128], T(xf[:, g, :], xd.res), T(yf[:, g, :], yd.res))
                if d == 0:
                    c.tt(T(mv[:, g, :], macc.res), ps[:, 0:128], T(mskv[:, 0, :], msk.res), ALU.mult)
                else:
                    t128 = T(tmp.f32().ap[:, 0:128], tmp.res)
                    c.tt(t128, ps[:, 0:128], T(mskv[:, 1, :], msk.res), ALU.mult)
                    c.tt(T(mv[:, g, :], macc.res), T(mv[:, g, :], macc.res), t128, ALU.add)
            a_re, a_im = pw_view(1, 1) if d == 0 else pw_view(8, -1)
            self._cmul(prv, piv, a_re, a_im, b_re, b_im, tmv)
            c.copy(T(r4b(cin[2 * d])[hs], cin[2 * d].res), prv[hs])
            c.ts(T(r4b(cin[2 * d + 1])[hs], cin[2 * d + 1].res), piv[hs], -1.0, None, ALU.mult)
        mv = macc.f32().ap.rearrange("p (g c) -> p g c", g=32)
        mb = A.alloc(2 * 4096, "s5Mb")
        mbv = mb.bf().ap.rearrange("p (g c) -> p g c", g=32)
        for g in range(32):
            c.stt(T(mbv[:, g, :], mb.res), self.ident_f, T(dcol.f32().ap[:, g:g + 1], dcol.res), T(mv[:, g, :], macc.res),
                  ALU.mult, ALU.add)
        c.dma("sp", self.ssmw[l, 0].rearrange("g p c -> p g c"), mbv, reads=[mb.res], writes=[self.ssmw_res])
        for ci in range(4):
            c.dma("sp", self.ssmw[l, 3 + ci].rearrange("g p c -> p g c"),
                  cin[ci].bf().ap.rearrange("p (g c) -> p g c", g=32), reads=[cin[ci].res], writes=[self.ssmw_res])
        for ri in range(2):
            zf = zt[ri].bf().ap.rearrange("p (g c) -> p g c", g=32)
            zo = A.alloc(2 * 4096, "s5Zo")
            zov = zo.bf().ap.rearrange("p (g c) -> p g c", g=32)
            for g in range(32):
                tp = self.psum("aux")
                tpb = T(tp.ap.bitcast(BF16)[:, 0:128], tp.res)
                c.transpose(tpb, T(zf[:, g, :], zt[ri].res), self.ident_b)
                c.copy(T(zov[:, g, :], zo.res), tpb, eng=("act" if g % 2 else "dve"))
            c.dma("sp", self.ssmw[l, 1 + ri].rearrange("g p c -> p g c"), zov, reads=[zo.res], writes=[self.ssmw_res])
            zo.free()
        for t_ in [small, pw, bt, ct, bb_t, pr, pi_, tmp, xd, yd, macc, msk, dcol, mb] + zt + cin:
            t_.free()

    def mixer_s5(self, l):
        c = self.ctx
        A = self.arena
        I = self.I
        grp = self.grp
        nt = self.nt
        N = nt // 8
        nseq = 4 if grp == 0 else 1
        cps = N // nseq
        W = I["w_in"][l]
        wa = A.alloc(8 * 240 * 2, "s5wa")
        wav = wa.bf().ap.rearrange("p (a j) -> p a j", a=8)
        c.dma("pool", wav, I["ssm_wa"], writes=[wa.res])
        zu = A.alloc(4 * nt * 2, "s5zu")
        zuv = zu.bf().ap.rearrange("p (k t) -> p k t", k=4)
        for kc in range(4):
            w = self.wload(W[:, O_U + kc * 128:O_U + (kc + 1) * 128], 8)
            for tt in range(self.ntt):
                h = self.hT[tt]
                hv = h.bf().ap.rearrange("p (k t) -> p k t", k=8)
                ps = self.psum()
                for k in range(8):
                    c.mm(ps, w[:, k, :], T(hv[:, k, :], h.res), start=(k == 0), stop=(k == 7))
                c.copy(T(zuv[:, kc, tt * 512:(tt + 1) * 512], zu.res), ps, eng=("act" if tt % 2 else "dve"))
        U = A.alloc(32 * N * 2, "s5U")
        Uv = U.bf().ap.rearrange("p (g n) -> p g n", g=32)
        for g in range(32):
            kc, gl = g // 8, g % 8
            ps = self.psum()
            zs = zuv[:, kc, :].rearrange("p (n s) -> p s n", s=8)
            for s_ in range(8):
                c.mm(ps[:, 0:N], T(wav[:, gl, 112 - 16 * s_: 240 - 16 * s_], wa.res), T(zs[:, s_, :], zu.res),
                     start=(s_ == 0), stop=(s_ == 7))
            c.copy(T(Uv[:, g, :], U.res), ps[:, 0:N], eng=("act" if g % 2 else "dve"))
        zu.free()
        Z = A.alloc(2 * 32 * nseq * (cps + 2) * 2, "s5Z")
        Zv = Z.bf(2 * 32 * nseq * (cps + 2)).ap.rearrange("p (r g q k) -> p r g q k", r=2, g=32, q=nseq)
        c.memset(Z.bf(), 0.0)
        St = [A.alloc(2 * 32 * nseq * 4, "s5S%d" % i) for i in range(2)]
        Sv = [T(t_.f32(2 * 32 * nseq).ap.rearrange("p (r g q) -> p r g q", r=2, g=32), t_.res) for t_ in St]
        if grp == 0:
            c.memset(Sv[0], 0.0)
        else:
            for ri, nm in ((0, "state_re"), (1, "state_im")):
                for d in range(2):
                    st = self.r_st32.next().f32()
                    for hf in range(2):
                        c.dma("sp", st.ap[0:32, hf * 64:(hf + 1) * 64], I[nm][l, d], writes=[st.res])
                    pt = self.psum("aux")
                    c.transpose(pt[:, 0:32], st[0:32, :], self.ident_f[0:32, 0:32])
                    hs = slice(d * 64, (d + 1) * 64)
                    c.copy(T(Sv[0].ap[hs, ri, :, 0], St[0].res), pt[hs, 0:32])
        c.copy(T(Zv[0:64, :, :, :, 0], Z.res), Sv[0][0:64], eng="act")
        c.copy(T(Zv[64:128, :, :, :, cps + 1], Z.res), Sv[0][64:128], eng="act")
        wst = Ring(A, 3, 7 * 128 * 2, "s5wst")
        def wtile(g):
            t_ = wst.next()
            v = t_.bf().ap.rearrange("p (k c) -> p k c", k=7)
            c.dma("sp", v, self.ssmw[l, :, g].rearrange("k p c -> p k c"), reads=[self.ssmw_res], writes=[t_.res])
            return T(v, t_.res)
        wt_n = wtile(0)
        for g in range(32):
            wt = wt_n
            if g + 1 < 32:
                wt_n = wtile(g + 1)
            for ri in range(2):
                ps = self.psum()
                c.mm(ps[:, 0:N], wt[:, 1 + ri, :], T(Uv[:, g, :], U.res))
                pv = ps.ap[:, 0:N].rearrange("p (q k) -> p q k", q=nseq)
                c.copy(T(Zv[:, ri, g, :, 1:cps + 1], Z.res), T(pv, ps.res), eng=("act" if ri else "dve"))
        l8 = self.lam8[l]
        sh_ = [128, 2, 32, nseq]
        lamA = T(l8.ap[:, 0].unsqueeze(3).broadcast_to(sh_), l8.res)
        lamB = T(l8.ap[:, 1].unsqueeze(3).broadcast_to(sh_), l8.res)
        tA = A.alloc(2 * 32 * nseq * 4, "s5tA")
        tB = A.alloc(2 * 32 * nseq * 4, "s5tB")
        tAv = T(tA.f32(2 * 32 * nseq).ap.rearrange("p (r g q) -> p r g q", r=2, g=32), tA.res)
        tBv = T(tB.f32(2 * 32 * nseq).ap.rearrange("p (r g q) -> p r g q", r=2, g=32), tB.res)
        cur = 0
        for k in range(cps):
            S_, Sn = Sv[cur], Sv[1 - cur]
            c.tt(tAv, S_, lamA, ALU.mult)
            c.tt(tBv, T(S_.ap[:, ::-1], S_.res), lamB, ALU.mult)
            c.tt(tAv, tAv, tBv, ALU.add)
            for hs, col in ((slice(0, 64), k + 1), (slice(64, 128), cps - k)):
                c.tt(Sn[hs], tAv[hs], T(Zv[hs, :, :, :, col], Z.res), ALU.add)
                c.copy(T(Zv[hs, :, :, :, col], Z.res), Sn[hs], eng="act")
            cur = 1 - cur
        if grp == 0:
            fin = Sv[cur]
            for ri, nm in ((0, "ssm_re"), (1, "ssm_im")):
                pt = self.psum("aux")
                c.transpose(pt[:, 0:128], T(fin.ap[:, ri].rearrange("p g q -> p (g q)"), fin.res), self.ident_f)
                st = self.r_st32.next().f32()
                c.copy(st, pt[:, 0:128])
                for q in range(4):
                    c.dma("sp", self.O[nm][q, l].rearrange("d g p -> g d p"),
                          st.ap[q::4, :].rearrange("g (d p) -> g d p", d=2), reads=[st.res], is_output=True)
        for t_ in St + [tA, tB]:
            t_.free()
        Y = A.alloc(32 * N * 2, "s5Y")
        Yv = Y.bf().ap.rearrange("p (g n) -> p g n", g=32)
        wt_n = wtile(0)
        for g in range(32):
            wt = wt_n
            if g + 1 < 32:
                wt_n = wtile(g + 1)
            ps = self.psum()
            c.mm(ps[:, 0:N], wt[:, 0, :], T(Uv[:, g, :], U.res), start=True, stop=False)
            psq = T(ps.ap[:, 0:N].rearrange("p (q k) -> p q k", q=nseq), ps.res)
            for ri in range(2):
                c.mm(psq, wt[:, 3 + ri, :], T(Zv[:, ri, g, :, 0:cps], Z.res), start=False, stop=False)
                c.mm(psq, wt[:, 5 + ri, :], T(Zv[:, ri, g, :, 2:cps + 2], Z.res), start=False, stop=(ri == 1))
            c.copy(T(Yv[:, g, :], Y.res), ps[:, 0:N], eng=("act" if g % 2 else "dve"))
        U.free()
        Z.free()
        wst.free()
        gT = A.alloc(4 * nt * 2, "s5g")
        gv = gT.bf().ap.rearrange("p (k t) -> p k t", k=4)
        for kc in range(4):
            for tp_ in range(8):
                ps = self.psum()
                for gl in range(8):
                    c.mm(ps[:, 0:N], T(wav[:, tp_, 112 - 16 * gl: 240 - 16 * gl], wa.res), T(Yv[:, kc * 8 + gl, :], Y.res),
                         start=(gl == 0), stop=(gl == 7))
                x_ = self.r_f32.next().f32()[:, 0:N]
                c.copy(x_, ps[:, 0:N], eng="act")
                u_ = self.r_f32.next().f32()[:, 0:N]
                c.tt(u_, x_, x_, ALU.mult)
                c.ts(u_, u_, 0.044715, 1.0, ALU.mult, ALU.add)
                c.tt(u_, u_, x_, ALU.mult)
                c.act(u_, u_, AF.Sigmoid, scale=1.5957691216057308)
                dst = gv[:, kc, :].rearrange("p (n s) -> p s n", s=8)[:, tp_, :]
                c.tt(T(dst, gT.res), x_, u_, ALU.mult)
        Y.free()
        wa.free()
        self.ybT = [A.alloc(4 * 512 * 2, "ybT%d" % i) for i in range(self.ntt)]
        for ko in range(4):
            w = self.wload(I["w_glu"][l][:, ko * 128:(ko + 1) * 128], 4)
            for tt in range(self.ntt):
                ps = self.psum()
                for k in range(4):
                    c.mm(ps, w[:, k, :], T(gv[:, k, tt * 512:(tt + 1) * 512], gT.res), start=(k == 0), stop=(k == 3))
                sg = self.r_f32.next().f32()
                c.act(sg, ps, AF.Sigmoid)
                yv = self.ybT[tt].bf().ap.rearrange("p (k t) -> p k t", k=4)
                c.tt(T(yv[:, ko, :], self.ybT[tt].res), T(gv[:, ko, tt * 512:(tt + 1) * 512], gT.res), sg, ALU.mult)
        gT.free()

    def load_x_tile(self, tt, first):
        c = self.ctx
        A = self.arena
        I = self.I
        gt = self.t0 // 512 + tt
        xt = A.alloc(8 * 512 * 4, "xT")
        xv = xt.f32().ap.rearrange("p (k t) -> p k t", k=8)
        import os
        if not first or os.environ.get("NOFP32"):
            c.dma("sp", xv, self.xs[:, :, gt * 512:(gt + 1) * 512].rearrange("k p t -> p k t"),
                  reads=[self.xs_res[k][gt] for k in range(8)], writes=[xt.res])
            return xt
        for k in range(8):
            pass
        tm = [A.alloc(1024 * 4, "xtm%d" % j) for j in range(4)]
        for j in range(4):
            c.dma("sp", tm[j].f32().ap, I["xin"][gt * 512 + j * 128: gt * 512 + (j + 1) * 128, :], writes=[tm[j].res])
        for k in range(8):
            ps = self.psum()
            for j in range(4):
                c.transpose(ps[:, j * 128:(j + 1) * 128], tm[j].f32()[:, k * 128:(k + 1) * 128], self.ident_f,
                            signal=(j == 3))
            c.copy(T(xv[:, k, :], xt.res), ps, eng=("act" if k % 2 else "dve"))
            c.dma("sp", self.xs[k][:, gt * 512:(gt + 1) * 512], xv[:, k, :], reads=[xt.res],
                  writes=[self.xs_res[k][gt]])
        for j in range(4):
            tm[j].free()
        return xt

    def norm_in(self, l, first, which):
        c = self.ctx
        A = self.arena
        n = self.grp
        ia, ib = (0, 1) if which == 0 else (3, 4)
        for tt in range(self.ntt):
            xt = self.load_x_tile(tt, first)
            self.norm_tile(l, xt, self.hT[tt], ia, ib)
            xt.free()

    def norm_tile(self, l, xt, hT, ia, ib):
        c = self.ctx
        A = self.arena
        n = self.grp
        if True:
            xv = xt.f32().ap.rearrange("p (k t) -> p k t", k=8)
            sq = A.alloc(512 * 2 * 2, "sq")
            sqv = sq.bf().ap.rearrange("p (b t) -> p b t", b=2)
            ps = self.psum()
            for k in range(8):
                c.act(T(sqv[:, k % 2, :], sq.res), T(xv[:, k, :], xt.res), AF.Square)
                c.mm(ps, self.ones_b, T(sqv[:, k % 2, :], sq.res), start=(k == 0), stop=(k == 7), signal=True)
            rs = A.alloc(512 * 4, "rstd")
            c.act(rs.f32(), ps, AF.Ln, bias=EPS, scale=1.0)
            c.act(rs.f32(), rs.f32(), AF.Exp, scale=-0.5)
            hv = hT.bf().ap.rearrange("p (k t) -> p k t", k=8)
            for k in range(8):
                c.tt(T(xv[:, k, :], xt.res), T(xv[:, k, :], xt.res), rs.f32(), ALU.mult)
                m = self.modS
                c.act(T(hv[:, k, :], hT.res), T(xv[:, k, :], xt.res), AF.Identity,
                      bias=T(m.ap[:, l, n, ib, k:k + 1], m.res), scale=T(m.ap[:, l, n, ia, k:k + 1], m.res))
            sq.free()
            rs.free()


def _prep_inputs(inp, core):
    f = np.float32
    m = {}
    xp = np.asarray(inp["x_prompt"], f)[4 * core:4 * core + 4].reshape(NP_TOK, D)
    xsam = np.asarray(inp["x_sample"], f)[core]
    m["xin"] = np.ascontiguousarray(np.concatenate([xp, xsam], 0))
    cv = np.stack([np.asarray(inp["c_ctx"], f), np.asarray(inp["c"], f)[core]], 0)
    m["cvecT"] = np.ascontiguousarray(cv.reshape(2, 8, 128).transpose(2, 1, 0))
    m["state_re"] = np.ascontiguousarray(np.asarray(inp["state_ssm_re"], f)[core])
    m["state_im"] = np.ascontiguousarray(np.asarray(inp["state_ssm_im"], f)[core])
    m["cache_na_k"] = np.ascontiguousarray(np.asarray(inp["cache_na_k"], f)[core].reshape(DEPTH, 256, 512))
    m["cache_na_v"] = np.ascontiguousarray(np.asarray(inp["cache_na_v"], f)[core].reshape(DEPTH, 256, 512))
    m["cache_ga_k"] = np.ascontiguousarray(np.asarray(inp["cache_ga_k"], f)[core].reshape(DEPTH, 256, 128))
    m["cache_ga_v"] = np.ascontiguousarray(np.asarray(inp["cache_ga_v"], f)[core].reshape(DEPTH, 256, 128))
    return m


def _shared_inputs(inp):
    f = np.float32
    m = {}
    m["w_mod"] = np.ascontiguousarray(np.asarray(inp["w_mod"], f))
    m["b_modT"] = np.ascontiguousarray(np.asarray(inp["b_mod"], f).reshape(DEPTH, 48, 128).transpose(2, 0, 1))
    m["norm_gT"] = np.ascontiguousarray(np.asarray(inp["norm_g"], f).reshape(DEPTH, 4, 8, 128).transpose(3, 0, 1, 2))
    m["w_in"] = np.ascontiguousarray(np.asarray(inp["w_in"], f))
    m["na_rpb"] = np.ascontiguousarray(np.asarray(inp["na_rpb"], f))
    for nm in ("ssm_lam_re", "ssm_lam_im", "ssm_log_step", "ssm_b_re", "ssm_b_im", "ssm_d", "w_glu"):
        m[nm] = np.ascontiguousarray(np.asarray(inp[nm], f))
    m["ssm_c_re"] = np.ascontiguousarray(np.asarray(inp["ssm_c_re"], f).reshape(DEPTH, 2, 512, 64))
    m["ssm_c_im"] = np.ascontiguousarray(np.asarray(inp["ssm_c_im"], f).reshape(DEPTH, 2, 512, 64))
    for nm in ("w_br_a", "w_br_b", "w_br_c", "w_out", "w_up", "w_down"):
        m[nm] = np.ascontiguousarray(np.asarray(inp[nm], f))
    m["conv_wT"] = np.ascontiguousarray(np.asarray(inp["conv_w"], f).reshape(DEPTH, 3, 44, 128).transpose(3, 0, 1, 2))
    m["conv_bT"] = np.ascontiguousarray(np.asarray(inp["conv_b"], f).reshape(DEPTH, 44, 128).transpose(2, 0, 1))
    qk = np.asarray(inp["qk_norm_g"], f)
    m["qk_gT"] = np.ascontiguousarray(np.concatenate([qk, qk], -1).transpose(2, 0, 1))
    m.update(_host_consts())
    return m


_DEBUG = []


def run(inputs, debug=None):
    dbg = debug if debug is not None else _DEBUG
    dry = Builder(debug=dbg)
    dry.build()
    b = Builder(debug=dbg, wplan=dry.wrec)
    nc = b.build()
    shared = _shared_inputs(inputs)
    in_maps = []
    for core in range(NCORES):
        m = dict(shared)
        m.update(_prep_inputs(inputs, core))
        in_maps.append(m)
    res = run_bass_kernel_spmd(nc, in_maps, core_ids=list(range(NCORES)))
    return res.results, b


def kernel(**inputs):
    results, b = run(inputs)
    f = np.float32
    y_p = np.concatenate([r["y"][:NP_TOK].reshape(4, SEQ, D) for r in results], 0).astype(f)
    y_s = np.stack([r["y"][NP_TOK:] for r in results], 0).astype(f)
    ga_k = np.concatenate([r["ga_k"].reshape(4, DEPTH, SEQ, 2, HD) for r in results], 0).astype(f)
    ga_v = np.concatenate([r["ga_v"].reshape(4, DEPTH, SEQ, 2, HD) for r in results], 0).astype(f)
    na_k = np.concatenate([r["na_k"].reshape(4, DEPTH, SEQ, 8, HD) for r in results], 0).astype(f)
    na_v = np.concatenate([r["na_v"].reshape(4, DEPTH, SEQ, 8, HD) for r in results], 0).astype(f)
    s_re = np.concatenate([r["ssm_re"].reshape(4, DEPTH, 2, 32, 64) for r in results], 0).astype(f)
    s_im = np.concatenate([r["ssm_im"].reshape(4, DEPTH, 2, 32, 64) for r in results], 0).astype(f)
    return (y_p, y_s, ga_k, ga_v, na_k, na_v, s_re, s_im)
```

```python
import math
from contextlib import ExitStack

import numpy as np
import ml_dtypes
import concourse.bass as bass
import concourse.mybir as mybir
from concourse.bass_utils import run_bass_kernel_spmd

F32 = mybir.dt.float32
BF16 = mybir.dt.bfloat16
AF = mybir.ActivationFunctionType
ALU = mybir.AluOpType

D = 1024
DEPTH = 2
NCORES = 8
SEQ = 256
DEC_SEQ = 2048
NP_TOK = 1024
NTOK = NP_TOK + DEC_SEQ
HD = 64
IN_W = 5888
D_FF = 2816
EPS = 1e-6
O_GQ, O_GK, O_GV, O_U, O_NQ, O_NK, O_NV, O_G = 0, 512, 640, 768, 1280, 1792, 2304, 2816


class Res:
    __slots__ = ("w", "r", "name", "psum")

    def __init__(self, name="", psum=False):
        self.w = {}
        self.r = {}
        self.name = name
        self.psum = psum


class T:
    __slots__ = ("ap", "res")

    def __init__(self, ap, res):
        self.ap = ap
        self.res = res

    def __getitem__(self, k):
        return T(self.ap[k], self.res)

    def v(self, ap):
        return T(ap, self.res)


class Eng:
    def __init__(self, name, h, sem):
        self.name = name
        self.h = h
        self.sem = sem
        self.count = 0
        self.waited = {}
        self.n_ops = 0
        self.n_waits = 0
        self.prog = []


def _aps(x):
    return x.ap if isinstance(x, T) else x


class Ctx:
    def __init__(self, nc, es, n_dma_sems=40):
        self.nc = nc
        self.engs = {}
        for name, h in (("pe", nc.tensor), ("act", nc.scalar), ("dve", nc.vector),
                        ("pool", nc.gpsimd), ("sp", nc.sync)):
            sem = es.enter_context(nc.semaphore("sem_" + name))
            self.engs[name] = Eng(name, h, sem)
        self.dma_sems = {}
        self.dma_cnt = {}
        self.dma_i = {}
        for q, n in (("sp", 24), ("pool", 24), ("act", 8)):
            self.dma_sems[q] = [es.enter_context(nc.semaphore("dsem_%s%d" % (q, i))) for i in range(n)]
            self.dma_cnt[q] = [0] * n
            self.dma_i[q] = 0
        self.out_clock = {}
        self.n_inst = 0

    def _wait(self, eng, need):
        for sem, val in need.items():
            if eng.waited.get(sem, 0) < val:
                eng.prog.append(("w", sem, val))
                eng.waited[sem] = val
                eng.n_waits += 1

    def op(self, ename, fn, reads=(), writes=(), signal=True):
        eng = self.engs[ename]
        need = {}
        for r in reads:
            for s, v in r.w.items():
                if s is eng.sem and ename == "pe":
                    continue
                if need.get(s, 0) < v:
                    need[s] = v
            if r.psum:
                for s, v in r.r.items():
                    if s is eng.sem:
                        continue
                    if need.get(s, 0) < v:
                        need[s] = v
        for w in writes:
            for d in (w.w, w.r):
                for s, v in d.items():
                    if s is eng.sem and ename == "pe":
                        continue
                    if need.get(s, 0) < v:
                        need[s] = v
        self._wait(eng, need)
        self.n_inst += 1
        eng.n_ops += 1
        if signal:
            eng.count += 1
            eng.prog.append(("o", fn, eng.sem, 1))
            idx = eng.count
        else:
            eng.prog.append(("o", fn, None, 0))
            idx = eng.count + 1
        inst = None
        for w in writes:
            w.w = {eng.sem: idx}
            w.r = {}
        for r in reads:
            if r.r.get(eng.sem, 0) < idx:
                r.r[eng.sem] = idx
        return inst

    def dma(self, q, out, in_, reads=(), writes=(), is_output=False, **kw):
        eng = self.engs[q]
        need = {}
        for r in reads:
            for s, v in r.w.items():
                if need.get(s, 0) < v:
                    need[s] = v
        for w in writes:
            for d in (w.w, w.r):
                for s, v in d.items():
                    if need.get(s, 0) < v:
                        need[s] = v
        self._wait(eng, need)
        i = self.dma_i[q] % len(self.dma_sems[q])
        self.dma_i[q] += 1
        sem = self.dma_sems[q][i]
        if self.dma_cnt[q][i] > 0:
            self._wait(eng, {sem: self.dma_cnt[q][i]})
        self.dma_cnt[q][i] += 16
        val = self.dma_cnt[q][i]
        eng.prog.append(("o", (lambda e, out=out, in_=in_, kw=kw: e.dma_start(out=out, in_=in_, **kw)), sem, 16))
        self.n_inst += 1
        eng.n_ops += 1
        for w in writes:
            w.w = {sem: val}
            w.r = {}
        for r in reads:
            if r.r.get(sem, 0) < val:
                r.r[sem] = val
        if is_output:
            self.out_clock[sem] = max(self.out_clock.get(sem, 0), val)

    def _rw(self, outs, ins):
        writes = [o.res for o in outs if isinstance(o, T)]
        reads = [i.res for i in ins if isinstance(i, T)]
        return reads, writes

    def mm(self, out, lhsT, rhs, start=True, stop=True, signal=None):
        reads, writes = self._rw([out], [lhsT, rhs])
        if not start:
            reads = reads
        return self.op("pe", lambda e: e.matmul(_aps(out), lhsT=_aps(lhsT), rhs=_aps(rhs), start=start, stop=stop),
                       reads, writes, signal=(stop if signal is None else signal))

    def transpose(self, out, in_, ident, signal=True):
        reads, writes = self._rw([out], [in_, ident])
        return self.op("pe", lambda e: e.transpose(_aps(out), _aps(in_), _aps(ident)), reads, writes, signal=signal)

    def act(self, out, in_, func, bias=None, scale=None, accum_out=None):
        ins = [in_]
        kw = {}
        if bias is not None:
            kw["bias"] = _aps(bias)
            ins.append(bias)
        if scale is not None:
            kw["scale"] = _aps(scale)
            ins.append(scale)
        outs = [out]
        if accum_out is not None:
            kw["accum_out"] = _aps(accum_out)
            outs.append(accum_out)
        reads, writes = self._rw(outs, ins)
        return self.op("act", lambda e: e.activation(out=_aps(out), in_=_aps(in_), func=func, **kw), reads, writes)

    def tt(self, out, in0, in1, op, eng="dve"):
        reads, writes = self._rw([out], [in0, in1])
        return self.op(eng, lambda e: e.tensor_tensor(out=_aps(out), in0=_aps(in0), in1=_aps(in1), op=op), reads, writes)

    def ts(self, out, in0, s1, s2, op0, op1=None, eng="dve"):
        reads, writes = self._rw([out], [in0, s1, s2])
        if op1 is None:
            return self.op(eng, lambda e: e.tensor_single_scalar(out=_aps(out), in_=_aps(in0), scalar=_aps(s1), op=op0),
                           reads, writes)
        return self.op(eng, lambda e: e.tensor_scalar(out=_aps(out), in0=_aps(in0), scalar1=_aps(s1), scalar2=_aps(s2),
                                                      op0=op0, op1=op1), reads, writes)

    def stt(self, out, in0, scalar, in1, op0, op1, eng="dve"):
        reads, writes = self._rw([out], [in0, scalar, in1])
        return self.op(eng, lambda e: e.scalar_tensor_tensor(out=_aps(out), in0=_aps(in0), scalar=_aps(scalar),
                                                             in1=_aps(in1), op0=op0, op1=op1), reads, writes)

    def copy(self, out, in_, eng="dve"):
        reads, writes = self._rw([out], [in_])
        if eng == "act":
            return self.op("act", lambda e: e.activation(out=_aps(out), in_=_aps(in_), func=AF.Copy), reads, writes)
        return self.op(eng, lambda e: e.tensor_copy(out=_aps(out), in_=_aps(in_)), reads, writes)

    def recip(self, out, in_):
        reads, writes = self._rw([out], [in_])
        return self.op("dve", lambda e: e.reciprocal(out=_aps(out), in_=_aps(in_)), reads, writes)

    def memset(self, out, val, eng="dve"):
        reads, writes = self._rw([out], [])
        return self.op(eng, lambda e: e.memset(_aps(out), val), reads, writes)

    def finish(self):
        eng = self.engs["sp"]
        self._wait(eng, self.out_clock)
        need = {}
        for n in ("pe", "act", "dve", "pool"):
            e = self.engs[n]
            if e.count > 0:
                need[e.sem] = e.count
        self._wait(eng, need)
        self.emit()

    def emit(self):
        nc = self.nc

        def replay(eng, h):
            for it in eng.prog:
                if it[0] == "w":
                    h.wait_ge(it[1], it[2])
                else:
                    inst = it[1](h)
                    if it[2] is not None:
                        inst.then_inc(it[2], it[3])

        with nc.Block() as block:
            @block.tensor
            def _(e):
                replay(self.engs["pe"], e)

            @block.scalar
            def _(e):
                replay(self.engs["act"], e)

            @block.vector
            def _(e):
                replay(self.engs["dve"], e)

            @block.gpsimd
            def _(e):
                replay(self.engs["pool"], e)

            @block.sync
            def _(e):
                replay(self.engs["sp"], e)


class Arena:
    def __init__(self, ap_f32, nbytes):
        self.base = ap_f32
        self.size = nbytes
        self.free = [(0, nbytes)]
        self.hist = []

    def alloc(self, nbytes, name=""):
        nbytes = (nbytes + 63) // 64 * 64
        for i, (s, e) in enumerate(self.free):
            if e - s >= nbytes:
                self.free[i] = (s + nbytes, e)
                if self.free[i][0] == self.free[i][1]:
                    del self.free[i]
                break
        else:
            raise RuntimeError("SBUF arena OOM for %s (%d bytes); free=%s" % (name, nbytes, self.free))
        res = Res(name)
        st, en = s, s + nbytes
        keep = []
        for (hs, he, hr) in self.hist:
            if hs < en and st < he:
                for d in (hr.w, hr.r):
                    for k, v in d.items():
                        if res.w.get(k, 0) < v:
                            res.w[k] = v
                if hs >= st and he <= en:
                    continue
            keep.append((hs, he, hr))
        self.hist = keep
        return Tile(self, st, nbytes, res)

    def release(self, tile):
        s, e = tile.start, tile.start + tile.nbytes
        self.hist.append((s, e, tile.res))
        self.free.append((s, e))
        self.free.sort()
        merged = []
        for (a, b) in self.free:
            if merged and merged[-1][1] == a:
                merged[-1] = (merged[-1][0], b)
            else:
                merged.append((a, b))
        self.free = merged


class Tile:
    def __init__(self, arena, start, nbytes, res):
        self.arena = arena
        self.start = start
        self.nbytes = nbytes
        self.res = res

    def f32(self, n=None):
        n = self.nbytes // 4 if n is None else n
        return T(self.arena.base[:, self.start // 4: self.start // 4 + n], self.res)

    def bf(self, n=None):
        n = self.nbytes // 2 if n is None else n
        ap = self.arena.base[:, self.start // 4: self.start // 4 + (n + 1) // 2].bitcast(BF16)
        return T(ap[:, 0:n], self.res)

    def free(self):
        self.arena.release(self)


class Ring:
    def __init__(self, arena, n, nbytes, name):
        self.tiles = [arena.alloc(nbytes, "%s%d" % (name, i)) for i in range(n)]
        self.i = 0

    def next(self):
        t = self.tiles[self.i % len(self.tiles)]
        self.i += 1
        return t

    def free(self):
        for t in self.tiles:
            t.free()


def _host_consts():
    c = {}
    c["ident_f"] = np.eye(128, dtype=np.float32)
    p = np.arange(128)
    d = p % 64
    a = d // 32
    f = d % 16
    t = np.arange(DEC_SEQ)
    inv = (10000.0 ** (-np.arange(16, dtype=np.float32) / 16)).astype(np.float32)
    pos = np.where(a[:, None] == 0, (t // 64)[None, :], (t % 64)[None, :]).astype(np.float32)
    ang = pos * inv[f][:, None]
    c["rope_cos"] = np.cos(ang).astype(np.float32)
    c["rope_sin"] = np.sin(ang).astype(np.float32)
    P = np.zeros((128, 128), np.float32)
    for m in range(128):
        dm = m % 64
        b = (dm % 32) // 16
        if b == 0:
            P[m, m + 16] = -1.0
        else:
            P[m, m - 16] = 1.0
    c["rotT"] = np.ascontiguousarray(P.T)
    bd = np.zeros((128, 128), np.float32)
    bd[:64, :64] = 1.0 / 64
    bd[64:, 64:] = 1.0 / 64
    c["bd64"] = bd
    def valid(i, jb):
        mk = np.zeros((128, 128), np.float32)
        for a in range(2):
            for b in range(2):
                kr = 2 * jb + a
                r = 2 * i + b
                r0 = min(max(r - 4, 0), 24)
                if not (r0 <= kr < r0 + 8):
                    continue
                cc = np.arange(64)
                c0 = np.clip(cc - 8, 0, 48)
                kc = np.arange(64)
                ok = (kc[:, None] >= c0[None, :]) & (kc[:, None] < c0[None, :] + 16)
                mk[a * 64:(a + 1) * 64, b * 64:(b + 1) * 64] = ok
        return mk
    cls = [(6, 6 + d) for d in range(-2, 3)] + [(0, j) for j in range(4)] + [(1, j) for j in range(4)] + \
          [(14, j) for j in range(12, 16)] + [(15, j) for j in range(12, 16)]
    wa = np.zeros((128, 8, 240), np.float32)
    for a in range(8):
        for hh in range(16):
            wa[a * 16 + hh, a, hh + 112] = 1.0
    c["ssm_wa"] = wa
    sh = np.arange(128) // 16
    c["ssm_maskF"] = (sh[:, None] <= sh[None, :]).astype(np.float32)
    c["ssm_maskB"] = (sh[:, None] >= sh[None, :]).astype(np.float32)
    c["na_mask"] = np.ascontiguousarray(np.stack([valid(i, j) for (i, j) in cls], 1))
    return c


class Builder:
    def __init__(self, debug=None, wplan=None):
        self.wplan = wplan
        self.wrec = []
        self.debug = debug or []
        self.nc = bass.Bass("TRN2", target_bir_lowering=False)
        self.dbg_out = {}

    def din(self, name, shape, dt=F32):
        return self.nc.dram_tensor(name, list(shape), dt, kind="ExternalInput").ap()

    def dout(self, name, shape, dt=F32):
        return self.nc.dram_tensor(name, list(shape), dt, kind="ExternalOutput").ap()

    def dscr(self, name, shape, dt=F32):
        return self.nc.dram_tensor(name, list(shape), dt).ap()

    def build(self):
        nc = self.nc
        I = {}
        I["xin"] = self.din("xin", [NTOK, D])
        I["cvecT"] = self.din("cvecT", [128, 8, 2])
        I["w_mod"] = self.din("w_mod", [DEPTH, D, 6 * D])
        I["b_modT"] = self.din("b_modT", [128, DEPTH, 48])
        I["norm_gT"] = self.din("norm_gT", [128, DEPTH, 4, 8])
        I["w_in"] = self.din("w_in", [DEPTH, D, IN_W])
        I["qk_gT"] = self.din("qk_gT", [128, DEPTH, 2])
        for k, shp in (("ident_f", [128, 128]), ("rope_cos", [128, DEC_SEQ]), ("rope_sin", [128, DEC_SEQ]),
                       ("rotT", [128, 128]), ("bd64", [128, 128])):
            I[k] = self.din(k, shp)
        for nm, shp in (("w_br_a", [DEPTH, 512, D]), ("w_br_b", [DEPTH, 512, D]), ("w_br_c", [DEPTH, 512, D]),
                        ("w_out", [DEPTH, D, D]), ("w_up", [DEPTH, D, 2 * D_FF]), ("w_down", [DEPTH, D_FF, D]),
                        ("conv_wT", [128, DEPTH, 3, 44]), ("conv_bT", [128, DEPTH, 44])):
            I[nm] = self.din(nm, shp)
        for nm, shp in (("ssm_wa", [128, 8, 240]), ("ssm_maskF", [128, 128]), ("ssm_maskB", [128, 128]),
                        ("ssm_lam_re", [DEPTH, 2, 32, 64]), ("ssm_lam_im", [DEPTH, 2, 32, 64]),
                        ("ssm_log_step", [DEPTH, 2, 32]), ("ssm_b_re", [DEPTH, 2, 32, 64, 16]),
                        ("ssm_b_im", [DEPTH, 2, 32, 64, 16]), ("ssm_c_re", [DEPTH, 2, 512, 64]),
                        ("ssm_c_im", [DEPTH, 2, 512, 64]), ("ssm_d", [DEPTH, 512]), ("w_glu", [DEPTH, 512, 512]),
                        ("state_re", [DEPTH, 2, 32, 64]), ("state_im", [DEPTH, 2, 32, 64])):
            I[nm] = self.din(nm, shp)
        I["cache_na_k"] = self.din("cache_na_k", [DEPTH, 256, 512])
        I["cache_na_v"] = self.din("cache_na_v", [DEPTH, 256, 512])
        I["na_rpb"] = self.din("na_rpb", [DEPTH, 8, 15, 31])
        I["na_mask"] = self.din("na_mask", [128, 21, 128])
        I["cache_ga_k"] = self.din("cache_ga_k", [DEPTH, 256, 128])
        I["cache_ga_v"] = self.din("cache_ga_v", [DEPTH, 256, 128])
        self.I = I
        O = {}
        O["y"] = self.dout("y", [NTOK, D])
        O["ga_k"] = self.dout("ga_k", [4, DEPTH, 256, 128])
        O["ga_v"] = self.dout("ga_v", [4, DEPTH, 256, 128])
        O["na_k"] = self.dout("na_k", [4, DEPTH, 256, 512])
        O["na_v"] = self.dout("na_v", [4, DEPTH, 256, 512])
        O["ssm_re"] = self.dout("ssm_re", [4, DEPTH, 2, 32, 64])
        O["ssm_im"] = self.dout("ssm_im", [4, DEPTH, 2, 32, 64])
        self.ssmw = self.dscr("ssmw", [DEPTH, 7, 32, 128, 128], BF16)
        self.ssmw_res = Res("ssmw")
        self.vp = self.dscr("na_vp", [8 * 15 * 127 + 128])
        self.vp_res = Res("na_vp")
        self.O = O
        for name, shape in self.debug:
            self.dbg_out[name] = self.dout("dbg_" + name, shape)
        self.xs = self.dscr("xs", [8, 128, NTOK])
        self.xs_res = [[Res("xs%d_%d" % (k, t)) for t in range(NTOK // 512)] for k in range(8)]

        with ExitStack() as es:
            arena_t = es.enter_context(nc.sbuf_tensor("arena", [128, 52000], F32))
            ps_t = es.enter_context(nc.psum_tensor("ps", [128, 4096], F32))
            self.ctx = Ctx(nc, es)
            self.arena = Arena(arena_t[:, :], 52000 * 4)
            self.ps_banks = [T(ps_t[:, b * 512:(b + 1) * 512], Res("ps%d" % b, psum=True)) for b in range(8)]
            self.ps_pi = {}
            self.program()
            self.ctx.finish()
        return nc

    PS_POOLS = {"mm": (0, 1, 2, 3), "acc": (4, 5), "aux": (6, 7), "all8": (0, 1, 2, 3, 4, 5, 6, 7)}

    def psum(self, pool="mm"):
        banks = self.PS_POOLS[pool]
        i = self.ps_pi.get(pool, 0)
        self.ps_pi[pool] = i + 1
        return self.ps_banks[banks[i % len(banks)]]

    W_SLOTS = 16
    W_AHEAD = 9

    def wring_init(self, nslots=None, slot_bytes=2048):
        nslots = nslots or self.W_SLOTS
        self.wslots = [self.arena.alloc(slot_bytes, "wslot%d" % i) for i in range(nslots)]
        self.w_i = 0
        self.w_issued = 0
        self.w_tiles = {}

    def _wissue(self, i, srcs, kchunks):
        slot = self.wslots[i % len(self.wslots)]
        ncols = sum(a.shape[1] for a in srcs)
        assert kchunks * ncols * 2 <= slot.nbytes
        t = slot.bf(kchunks * ncols)
        dst = t.ap.rearrange("p (k c) -> p k c", k=kchunks)
        o = 0
        for a in srcs:
            n = a.shape[1]
            self.ctx.dma("pool", dst[:, :, o:o + n], a.rearrange("(k p) c -> p k c", p=128), writes=[t.res])
            o += n
        self.w_tiles[i] = T(dst, t.res)

    def wload(self, src_ap, kchunks, ncols=None):
        srcs = list(src_ap) if isinstance(src_ap, (list, tuple)) else [src_ap]
        idx = self.w_i
        self.w_i += 1
        if self.wplan is None:
            self.wrec.append(([(a.tensor.name, a.offset, [list(x) for x in a.ap]) for a in srcs], kchunks))
            self._wissue(idx, srcs, kchunks)
            return self.w_tiles.pop(idx)
        rec = self.wplan[idx]
        assert rec[1] == kchunks and rec[0][0][1] == srcs[0].offset and rec[0][0][0] == srcs[0].tensor.name
        hi = min(idx + self.W_AHEAD, len(self.wplan) - 1)
        while self.w_issued <= hi:
            r_srcs, r_k = self.wplan[self.w_issued]
            aps = [bass.AP(self.I[nm].tensor, off, apl) for (nm, off, apl) in r_srcs]
            self._wissue(self.w_issued, aps, r_k)
            self.w_issued += 1
        return self.w_tiles.pop(idx)

    def program(self):
        c = self.ctx
        A = self.arena
        I = self.I
        cst = A.alloc(4 * (128 + 64 + 64 + 64 + 64 + 64), "consts")
        base = cst.f32()
        self.ident_f = base[:, 0:128]
        o = 128
        self.ident_b = T(base.ap[:, o:o + 64].bitcast(BF16), cst.res); o += 64
        self.ones_b = T(base.ap[:, o:o + 64].bitcast(BF16), cst.res); o += 64
        self.bd64_b = T(base.ap[:, o:o + 64].bitcast(BF16), cst.res); o += 64
        self.rot_b = T(base.ap[:, o:o + 64].bitcast(BF16), cst.res); o += 64
        c.dma("sp", self.ident_f.ap, I["ident_f"], writes=[cst.res])
        c.dma("pool", self.ident_b.ap, I["ident_f"], writes=[cst.res])
        c.dma("pool", self.bd64_b.ap, I["bd64"], writes=[cst.res])
        c.dma("pool", self.rot_b.ap, I["rotT"], writes=[cst.res])
        c.memset(self.ones_b, 1.0 / 1024)
        zt = A.alloc(512 * 2, "zeros")
        self.zeros_b = zt.bf()
        c.memset(self.zeros_b, 0.0)
        qkg = A.alloc(4 * DEPTH * 2, "qkg")
        self.qkg = T(qkg.f32(DEPTH * 2).ap.rearrange("p (l j) -> p l j", l=DEPTH), qkg.res)
        c.dma("sp", qkg.f32(DEPTH * 2).ap, I["qk_gT"].rearrange("p l j -> p (l j)"), writes=[qkg.res])
        cw = A.alloc(4 * DEPTH * 4 * 44, "convw")
        cwv = cw.f32(DEPTH * 4 * 44).ap.rearrange("p (l j c) -> p l j c", l=DEPTH, j=4)
        self.convw = T(cwv, cw.res)
        for l in range(DEPTH):
            c.dma("sp", cwv[:, l, 0:3, :], I["conv_wT"][:, l, :, :], writes=[cw.res])
            c.dma("sp", cwv[:, l, 3, :], I["conv_bT"][:, l, :], writes=[cw.res])
        self.wring_init()
        self.r_st32 = Ring(A, 3, 128 * 4, "rst32")
        self.lam8 = []
        import os
        for l in range(DEPTH if not os.environ.get("NOS5") else 0):
            self.ssm_prep(l)
        self.adaln()
        import os
        for grp in ((0, 1) if not os.environ.get("G0") else (0,)):
            self.group_pass(grp)

    def adaln(self):
        c = self.ctx
        A = self.arena
        I = self.I
        mod_t = A.alloc(4 * DEPTH * 2 * 6 * 8, "modS")
        self.modS = T(mod_t.f32().ap.rearrange("p (l n i k) -> p l n i k", l=DEPTH, n=2, i=6), mod_t.res)
        tmp = A.alloc(4 * 288, "adaln_tmp")
        tb = tmp.f32()
        cT = tb[:, 0:16]
        bm = tb[:, 16:112]
        raw = tb[:, 112:208]
        ng = tb[:, 208:272]
        scb = T(tb.ap[:, 272:280].bitcast(BF16), tmp.res)
        c.dma("sp", cT.ap, I["cvecT"].rearrange("p k n -> p (k n)"), writes=[tmp.res])
        c.dma("sp", bm.ap, I["b_modT"].rearrange("p l c -> p (l c)"), writes=[tmp.res])
        c.dma("sp", ng.ap, I["norm_gT"].rearrange("p l i k -> p (l i k)"), writes=[tmp.res])
        c.act(scb, cT, AF.Silu)
        scb3 = scb.ap.rearrange("p (k n) -> p k n", n=2)
        ngv = ng.ap.rearrange("p (l i k) -> p l i k", l=DEPTH, i=4)
        for l in range(DEPTH):
            ps = self.psum()
            loads = []
            for cc in range(48):
                loads.append(lambda cc=cc: self.wload(I["w_mod"][l][:, cc * 128:(cc + 1) * 128], 8, 128))
            pend = []
            DEPTHQ = 4
            for cc in range(48 + DEPTHQ):
                if cc < 48:
                    pend.append(loads[cc]())
                if cc >= DEPTHQ:
                    j = cc - DEPTHQ
                    w = pend[j]
                    for k in range(8):
                        c.mm(ps[:, 2 * j:2 * j + 2], w[:, k, :], T(scb3[:, k, :], scb.res), start=(k == 0), stop=(k == 7))
            rawv = raw.ap.rearrange("p (c n) -> p c n", n=2)
            bmv = bm.ap.rearrange("p (l c) -> p l c", l=DEPTH)[:, l, :]
            for n in range(2):
                c.tt(T(rawv[:, :, n], tmp.res), T(ps.ap[:, 0:96].rearrange("p (c n) -> p c n", n=2)[:, :, n], ps.res),
                     T(bmv, tmp.res), ALU.add)
            for n in range(2):
                def R(i):
                    return T(rawv[:, i * 8:(i + 1) * 8, n], tmp.res)
                m = self.modS
                c.stt(T(m.ap[:, l, n, 0, :], m.res), R(1), 1.0, T(ngv[:, l, 0, :], tmp.res), ALU.add, ALU.mult)
                c.copy(T(m.ap[:, l, n, 1, :], m.res), R(0))
                c.tt(T(m.ap[:, l, n, 2, :], m.res), R(2), T(ngv[:, l, 1, :], tmp.res), ALU.mult)
                c.stt(T(m.ap[:, l, n, 3, :], m.res), R(4), 1.0, T(ngv[:, l, 2, :], tmp.res), ALU.add, ALU.mult)
                c.copy(T(m.ap[:, l, n, 4, :], m.res), R(3))
                c.tt(T(m.ap[:, l, n, 5, :], m.res), R(5), T(ngv[:, l, 3, :], tmp.res), ALU.mult)
        if "modS" in self.dbg_out:
            c.dma("sp", self.dbg_out["modS"], mod_t.f32().ap, reads=[mod_t.res], is_output=True)
        tmp.free()

    def group_pass(self, grp):
        self.grp = grp
        self.t0 = 0 if grp == 0 else NP_TOK
        self.nt = NP_TOK if grp == 0 else DEC_SEQ
        self.ntt = self.nt // 512
        A = self.arena
        c = self.ctx
        self.r_sq = Ring(A, 3, 512 * 2, "rsq")
        self.r_f32 = Ring(A, 6, 512 * 4, "rf32")
        self.r_small = Ring(A, 4, 64, "rsmall")
        self._pass_t0 = None
        for l in range(DEPTH):
            self.hT = [A.alloc(8 * 512 * 2, "hT%d" % i) for i in range(self.ntt)]
            self.norm_in(l, first=(l == 0), which=0)
            if l == 0 and ("hT%d" % grp) in self.dbg_out:
                for i, h in enumerate(self.hT):
                    self.ctx.dma("pool", self.dbg_out["hT%d" % grp][:, :, i * 512:(i + 1) * 512],
                                 h.bf().ap.rearrange("p (k t) -> p k t", k=8), reads=[h.res], is_output=True)
            self.mixer(l)
            self.ffn(l)
            for h in self.h2T:
                h.free()
        for r in (self.r_sq, self.r_f32, self.r_small):
            r.free()

    def qk_norm_rope(self, ps, l, which, rope_off, out_bf, f32_out=None):
        c = self.ctx
        sq = self.r_sq.next().bf()
        c.act(sq, ps, AF.Square)
        ps2 = self.psum()
        c.mm(ps2, self.bd64_b, sq)
        rs = self.r_f32.next().f32()
        c.act(rs, ps2, AF.Ln, bias=EPS, scale=1.0)
        c.act(rs, rs, AF.Exp, scale=-0.5)
        qn = self.r_f32.next().f32()
        c.tt(qn, ps, rs, ALU.mult)
        g = T(self.qkg.ap[:, l, which:which + 1], self.qkg.res)
        if rope_off is None:
            if f32_out is not None:
                c.act(f32_out, qn, AF.Identity, scale=g)
                c.copy(out_bf, f32_out)
            else:
                c.act(out_bf, qn, AF.Identity, scale=g)
            return
        qg = self.r_sq.next().bf()
        c.act(qg, qn, AF.Identity, scale=g)
        ps3 = self.psum()
        c.mm(ps3, self.rot_b, qg)
        t1 = self.r_f32.next().f32()
        c.tt(t1, qg, self.rope_cos[:, rope_off:rope_off + 512], ALU.mult)
        t2 = self.r_f32.next().f32()
        c.tt(t2, ps3, self.rope_sin[:, rope_off:rope_off + 512], ALU.mult)
        c.tt(out_bf, t1, t2, ALU.add)

    def mixer(self, l):
        A = self.arena
        self.yaT = [A.alloc(4 * 512 * 2, "yaT%d" % i) for i in range(self.ntt)]
        self.mixer_ga(l)
        if ("ya%d" % self.grp) in self.dbg_out and l == 0:
            for i, y in enumerate(self.yaT):
                self.ctx.dma("pool", self.dbg_out["ya%d" % self.grp][:, :, i * 512:(i + 1) * 512],
                             y.bf().ap.rearrange("p (k t) -> p k t", k=4), reads=[y.res], is_output=True)
        import os
        if os.environ.get("NOS5"):
            self.ybT = [A.alloc(4 * 512 * 2, "ybT%d" % i) for i in range(self.ntt)]
            for y in self.ybT:
                self.ctx.memset(y.bf(), 0.0)
        else:
            self.mixer_s5(l)
        self.ycT = [A.alloc(4 * 512 * 2, "ycT%d" % i) for i in range(self.ntt)]
        if os.environ.get("NONA"):
            for y in self.ycT:
                self.ctx.memset(y.bf(), 0.0)
        else:
            self.mixer_na(l)
        self.merge(l)
        for y in self.yaT + self.ybT + self.ycT + self.hT:
            y.free()
        self.h2T = [A.alloc(8 * 512 * 2, "h2T%d" % i) for i in range(self.ntt)]
        self.proj_out_residual(l, self.mT, "w_out", 8, 2, self.h2T, last=False)
        for m_ in self.mT:
            m_.free()

    def merge(self, l):
        c = self.ctx
        A = self.arena
        I = self.I
        self.mT = [A.alloc(8 * 512 * 2, "mT%d" % i) for i in range(self.ntt)]
        ys = (self.yaT, self.ybT, self.ycT)
        wn = ("w_br_a", "w_br_b", "w_br_c")
        for fo in range(8):
            wb = [self.wload(I[wn[b]][l][:, fo * 128:(fo + 1) * 128], 4) for b in range(3)]
            wg = [self.wload(I["w_in"][l][:, O_G + b * 1024 + fo * 128: O_G + b * 1024 + (fo + 1) * 128], 8)
                  for b in range(3)]
            for tt in range(self.ntt):
                h = self.hT[tt]
                hv = h.bf().ap.rearrange("p (k t) -> p k t", k=8)
                acc = self.r_f32.next().f32()
                for b in range(3):
                    y = ys[b][tt]
                    yv = y.bf().ap.rearrange("p (k t) -> p k t", k=4)
                    pp = self.psum("all8")
                    for k in range(4):
                        c.mm(pp, wb[b][:, k, :], T(yv[:, k, :], y.res), start=(k == 0), stop=(k == 3))
                    pg = self.psum("all8")
                    for k in range(8):
                        c.mm(pg, wg[b][:, k, :], T(hv[:, k, :], h.res), start=(k == 0), stop=(k == 7))
                    sg = self.r_f32.next().f32()
                    c.act(sg, pg, AF.Sigmoid)
                    if b == 0:
                        c.tt(acc, sg, pp, ALU.mult)
                    else:
                        c.tt(sg, sg, pp, ALU.mult)
                        if b == 1:
                            c.tt(acc, acc, sg, ALU.add)
                        else:
                            mv = self.mT[tt].bf().ap.rearrange("p (k t) -> p k t", k=8)
                            c.tt(T(mv[:, fo, :], self.mT[tt].res), acc, sg, ALU.add)

    def proj_out_residual(self, l, inT, wname, nk, ig, h2T, last):
        c = self.ctx
        A = self.arena
        I = self.I
        n = self.grp
        m = self.modS
        for tt in range(len(inT)):
            gt = self.t0 // 512 + tt if not hasattr(self, "_pass_t0") or self._pass_t0 is None else self._pass_t0 // 512 + tt
            src = inT[tt]
            sv = src.bf().ap.rearrange("p (k t) -> p k t", k=nk)
            ot = A.alloc(8 * 512 * 4, "oT")
            ov = ot.f32().ap.rearrange("p (k t) -> p k t", k=8)
            xt = A.alloc(8 * 512 * 4, "xT")
            xv = xt.f32().ap.rearrange("p (k t) -> p k t", k=8)
            c.dma("sp", xv, self.xs[:, :, gt * 512:(gt + 1) * 512].rearrange("k p t -> p k t"),
                  reads=[self.xs_res[k][gt] for k in range(8)], writes=[xt.res])
            msq = self.psum("acc")
            for fo in range(8):
                ps = self.psum()
                k0 = 0
                first = True
                while k0 < nk:
                    kn = min(8, nk - k0)
                    w = self.wload(I[wname][l][k0 * 128:(k0 + kn) * 128, fo * 128:(fo + 1) * 128], kn)
                    for k in range(kn):
                        c.mm(ps, w[:, k, :], T(sv[:, k0 + k, :], src.res), start=first, stop=(k0 + k == nk - 1))
                        first = False
                    k0 += kn
                c.copy(T(ov[:, fo, :], ot.res), ps, eng="act")
                sq = self.r_sq.next().bf()
                c.tt(sq, T(ov[:, fo, :], ot.res), T(ov[:, fo, :], ot.res), ALU.mult)
                c.mm(msq, self.ones_b, sq, start=(fo == 0), stop=(fo == 7), signal=True)
            rs = self.r_f32.next().f32()
            c.act(rs, msq, AF.Ln, bias=EPS, scale=1.0)
            c.act(rs, rs, AF.Exp, scale=-0.5)
            for k in range(8):
                c.tt(T(ov[:, k, :], ot.res), T(ov[:, k, :], ot.res), rs, ALU.mult)
                c.stt(T(xv[:, k, :], xt.res), T(ov[:, k, :], ot.res), T(m.ap[:, l, n, ig, k:k + 1], m.res),
                      T(xv[:, k, :], xt.res), ALU.mult, ALU.add)
                if not (last and l == DEPTH - 1):
                    c.dma("sp", self.xs[k][:, gt * 512:(gt + 1) * 512], xv[:, k, :], reads=[xt.res],
                          writes=[self.xs_res[k][gt]])
            ot.free()
            if not last:
                self.norm_tile(l, xt, h2T[tt], 3, 4)
            elif l == DEPTH - 1:
                for j in range(4):
                    orow = A.alloc(1024 * 4, "orow")
                    for k in range(8):
                        pt = self.psum("aux")
                        c.transpose(pt[:, 0:128], T(xv[:, k, j * 128:(j + 1) * 128], xt.res), self.ident_f)
                        c.copy(orow.f32()[:, k * 128:(k + 1) * 128], pt[:, 0:128], eng=("act" if k % 2 else "dve"))
                    c.dma("sp", self.O["y"][gt * 512 + j * 128: gt * 512 + (j + 1) * 128, :], orow.f32().ap,
                          reads=[orow.res], is_output=True)
                    orow.free()
            xt.free()

    def ffn(self, l):
        c = self.ctx
        A = self.arena
        I = self.I
        cwv = self.convw
        self.r_ub = Ring(A, 2, 1026 * 4 + 56, "rub")
        self.r_v = Ring(A, 3, 1024 * 4, "rv")
        for p0 in range(0, self.nt, 1024):
            tts = [p0 // 512, p0 // 512 + 1]
            act = A.alloc(22 * 1024 * 2, "ffn_act")
            av = act.bf().ap.rearrange("p (j t) -> p j t", j=22)
            nseg = 4 if self.grp == 0 else 1
            L = 1024 // nseg
            for j in range(22):
                wa = self.wload(I["w_up"][l][:, j * 128:(j + 1) * 128], 8)
                wg = self.wload(I["w_up"][l][:, D_FF + j * 128: D_FF + (j + 1) * 128], 8)
                vs = []
                for which, w in ((0, wa), (1, wg)):
                    cc = j + 22 * which
                    ub = self.r_ub.next()
                    ubv = ub.f32(1026)
                    for ti, tt in enumerate(tts):
                        h = self.h2T[tt]
                        hv = h.bf().ap.rearrange("p (k t) -> p k t", k=8)
                        ps = self.psum("all8")
                        for k in range(8):
                            c.mm(ps, w[:, k, :], T(hv[:, k, :], h.res), start=(k == 0), stop=(k == 7))
                        c.copy(ubv[:, 1 + ti * 512: 1 + (ti + 1) * 512], ps, eng="act")
                    for side, col in ((0, 0), (1, 1025)):
                        tok = p0 - 1 if side == 0 else p0 + 1024
                        if self.grp == 0 or tok < 0 or tok >= self.nt:
                            c.memset(ubv[:, col:col + 1], 0.0)
                        else:
                            h = self.h2T[tok // 512]
                            hv = h.bf().ap.rearrange("p (k t) -> p k t", k=8)
                            ps = self.psum("all8")
                            for k in range(8):
                                c.mm(ps[:, 0:1], w[:, k, :], T(hv[:, k, tok % 512: tok % 512 + 1], h.res),
                                     start=(k == 0), stop=(k == 7))
                            c.copy(ubv[:, col:col + 1], ps[:, 0:1])
                    v = self.r_v.next().f32(1024)
                    c.act(v, ubv[:, 1:1025], AF.Identity, bias=T(cwv.ap[:, l, 3, cc:cc + 1], cwv.res),
                          scale=T(cwv.ap[:, l, 1, cc:cc + 1], cwv.res))
                    v3 = T(v.ap.rearrange("p (s t) -> p s t", s=nseg), v.res)
                    ul = T(ubv.ap[:, 1:1025].rearrange("p (s t) -> p s t", s=nseg), ub.res)
                    if nseg == 1:
                        c.stt(v, ubv[:, 0:1024], T(cwv.ap[:, l, 0, cc:cc + 1], cwv.res), v, ALU.mult, ALU.add)
                        c.stt(v, ubv[:, 2:1026], T(cwv.ap[:, l, 2, cc:cc + 1], cwv.res), v, ALU.mult, ALU.add)
                    else:
                        c.stt(v3[:, :, 1:L], ul[:, :, 0:L - 1], T(cwv.ap[:, l, 0, cc:cc + 1], cwv.res), v3[:, :, 1:L],
                              ALU.mult, ALU.add)
                        c.stt(v3[:, :, 0:L - 1], ul[:, :, 1:L], T(cwv.ap[:, l, 2, cc:cc + 1], cwv.res),
                              v3[:, :, 0:L - 1], ALU.mult, ALU.add)
                    vs.append(v)
                c.act(vs[1], vs[1], AF.Silu)
                c.tt(T(av[:, j, :], act.res), vs[0], vs[1], ALU.mult)
            self._ffn_down(l, av, act, tts)
            act.free()
        self.r_ub.free()
        self.r_v.free()

    def _ffn_down(self, l, av, act, tts):
        c = self.ctx

        class V:
            def __init__(s_, ap, res):
                s_.ap_ = ap
                s_.res = res

            def bf(s_):
                return s_

            @property
            def ap(s_):
                return s_

            def rearrange(s_, *a, **k):
                return s_.ap_
        inT = [V(av[:, :, ti * 512:(ti + 1) * 512], act.res) for ti in range(2)]
        self._pass_t0 = self.t0 + tts[0] * 512
        self.proj_out_residual(l, inT, "w_down", 22, 5, None, last=True)
        self._pass_t0 = None

    def attn_rings(self, on):
        A = self.arena
        if on:
            self.r_p = Ring(A, 4, 512 * 2, "rp")
            self.r_q = Ring(A, 2, 4 * 512 * 2, "rq")
            self.r_yt = Ring(A, 3, 4 * 64 * 2, "ryt")
            self.r_st = Ring(A, 4, 128 * 4, "rst")
        else:
            for r in (self.r_p, self.r_q, self.r_yt, self.r_st):
                r.free()

    def mixer_ga(self, l):
        c = self.ctx
        A = self.arena
        I = self.I
        grp = self.grp
        nt = self.nt
        self.attn_rings(True)
        if grp == 1:
            self.rope_t = A.alloc(2 * DEC_SEQ * 4, "rope")
            rv = self.rope_t.f32().ap.rearrange("p (j t) -> p j t", j=2)
            self.rope_cos = T(rv[:, 0, :], self.rope_t.res)
            self.rope_sin = T(rv[:, 1, :], self.rope_t.res)
            c.dma("sp", rv[:, 0, :], self.I["rope_cos"], writes=[self.rope_t.res])
            c.dma("sp", rv[:, 1, :], self.I["rope_sin"], writes=[self.rope_t.res])
        nkeys = nt if grp == 0 else nt + 256
        nblk = nkeys // 128
        W = I["w_in"][l]
        kT = A.alloc(nkeys * 2, "ga_kT")
        kTv = kT.bf()
        va = A.alloc(nblk * 2 * 65 * 2, "ga_vaug")
        vab = va.bf(nblk * 2 * 65)
        vav = vab.ap.rearrange("p (b g e) -> p b g e", b=nblk, g=2)
        c.memset(vab, 1.0)
        wk = self.wload(W[:, O_GK:O_GK + 128], 8)
        wv = self.wload(W[:, O_GV:O_GV + 128], 8)
        import os
        SKIP = os.environ.get("SKIP", "")
        LIM = int(os.environ.get("LIM", "99"))
        for tt in range(min(self.ntt, LIM)):
            h = self.hT[tt]
            hv = h.bf().ap.rearrange("p (k t) -> p k t", k=8)
            if "K" not in SKIP:
                ps = self.psum()
                for k in range(8):
                    c.mm(ps, wk[:, k, :], T(hv[:, k, :], h.res), start=(k == 0), stop=(k == 7))
            if "K" in SKIP:
                pass
            elif "N" in SKIP:
                if "X" not in SKIP:
                    c.copy(kTv[:, tt * 512:(tt + 1) * 512], ps)
            elif grp == 0:
                kf = self.r_f32.next().f32()
                self.qk_norm_rope(ps, l, 1, None, kTv[:, tt * 512:(tt + 1) * 512], f32_out=kf)
                for j in range(4):
                    pt = self.psum("aux")
                    c.transpose(pt[:, 0:128], kf[:, j * 128:(j + 1) * 128], self.ident_f)
                    st = self.r_st.next().f32()
                    c.copy(st, pt[:, 0:128])
                    tok = tt * 512 + j * 128
                    c.dma("sp", self.O["ga_k"][tok // 256, l, tok % 256:tok % 256 + 128, :], st.ap,
                          reads=[st.res], is_output=True)
            else:
                self.qk_norm_rope(ps, l, 1, tt * 512, kTv[:, tt * 512:(tt + 1) * 512])
            for j in range(4 if "V" not in SKIP else 0):
                pv = self.psum("aux" if "A" in SKIP else "mm")
                for k in range(8):
                    c.mm(pv[:, 0:128], T(hv[:, k, j * 128:(j + 1) * 128], h.res), wv[:, k, :],
                         start=(k == 0), stop=(k == 7))
                blk = tt * 4 + j
                c.copy(T(vav[:, blk, :, 0:64], va.res),
                       T(pv.ap[:, 0:128].rearrange("p (g d) -> p g d", g=2), pv.res), eng="act")
                if grp == 0:
                    st = self.r_st.next().f32()
                    c.copy(st, pv[:, 0:128])
                    tok = tt * 512 + j * 128
                    c.dma("sp", self.O["ga_v"][tok // 256, l, tok % 256:tok % 256 + 128, :], st.ap,
                          reads=[st.res], is_output=True)
        if grp == 1 and "C" not in SKIP:
            for j in range(2):
                st = self.r_st.next().f32()
                c.dma("sp", st.ap, I["cache_ga_k"][l, j * 128:(j + 1) * 128, :], writes=[st.res])
                pt = self.psum("aux")
                c.transpose(pt[:, 0:128], st, self.ident_f)
                c.copy(kTv[:, nt + j * 128: nt + (j + 1) * 128], pt[:, 0:128])
            for j in range(2):
                c.dma("pool", vav[:, 16 + j, :, 0:64],
                      I["cache_ga_v"][l, j * 128:(j + 1) * 128, :].rearrange("p (g d) -> p g d", g=2),
                      writes=[va.res])
        import os
        CUT = int(os.environ.get("CUT", "99"))
        for tt in range(self.ntt if CUT > 0 else 0):
            h = self.hT[tt]
            hv = h.bf().ap.rearrange("p (k t) -> p k t", k=8)
            qt = self.r_q.next()
            qv = qt.bf().ap.rearrange("p (h t) -> p h t", h=4)
            for hl in range(4):
                w = self.wload([W[:, O_GQ + hl * 64:O_GQ + hl * 64 + 64],
                                W[:, O_GQ + (4 + hl) * 64:O_GQ + (4 + hl) * 64 + 64]], 8)
                ps = self.psum()
                for k in range(8):
                    c.mm(ps, w[:, k, :], T(hv[:, k, :], h.res), start=(k == 0), stop=(k == 7))
                self.qk_norm_rope(ps, l, 0, (tt * 512 if grp == 1 else None), T(qv[:, hl, :], qt.res))
            yv = self.yaT[tt].bf().ap.rearrange("p (k t) -> p k t", k=4)
            its = []
            for sub in range(4):
                tok0 = tt * 512 + sub * 128
                if grp == 0:
                    sq_ = tok0 // 256
                    blocks = [2 * sq_, 2 * sq_ + 1]
                else:
                    blocks = list(range(nblk))
                for g in range(2):
                    for bi, kb in enumerate(blocks):
                        its.append((sub, g, bi, kb, len(blocks)))

            def emit_score(it):
                sub, g, bi, kb, nb = it
                s_ps = self.psum()
                c.mm(T(s_ps.ap.rearrange("p (h t) -> p h t", h=4), s_ps.res),
                     kTv[g * 64:(g + 1) * 64, kb * 128:(kb + 1) * 128],
                     T(qv[g * 64:(g + 1) * 64, :, sub * 128:(sub + 1) * 128], qt.res))
                return s_ps
            s_next = emit_score(its[0])
            o_ps = None
            for n_, it in enumerate(its):
                sub, g, bi, kb, nb = it
                s_ps = s_next
                if n_ + 1 < len(its):
                    s_next = emit_score(its[n_ + 1])
                pT = self.r_p.next().bf()
                c.act(pT, s_ps, AF.Exp, scale=0.125)
                if bi == 0:
                    o_ps = self.psum("acc")
                    c.mm(o_ps[:, 0:260], self.zeros_b[:, 0:128], self.zeros_b[:, 0:260], start=True, stop=False,
                         signal=False)
                for hl in range(4):
                    c.mm(o_ps[:, hl * 65:(hl + 1) * 65], pT[:, hl * 128:(hl + 1) * 128],
                         T(vav[:, kb, g, :], va.res), start=False,
                         stop=(bi == nb - 1 and hl == 3), signal=(bi == nb - 1 and hl == 3))
                if bi != nb - 1:
                    continue
                ov = o_ps.ap[:, 0:260].rearrange("p (h e) -> p h e", h=4)
                rec = self.r_small.next().f32()[:, 0:4]
                c.recip(rec, T(ov[:, :, 64], o_ps.res))
                yt = self.r_yt.next()
                ytv = yt.bf().ap.rearrange("p (h e) -> p h e", h=4)
                c.tt(T(ytv, yt.res), T(ov[:, :, 0:64], o_ps.res),
                     T(rec.ap.unsqueeze(2).broadcast_to([128, 4, 64]), rec.res), ALU.mult)
                for j in range(2):
                    tp = self.psum("aux")
                    tpb = T(tp.ap.bitcast(BF16)[:, 0:128], tp.res)
                    c.transpose(tpb, yt.bf()[:, j * 128:(j + 1) * 128], self.ident_b)
                    c.copy(T(yv[:, 2 * g + j, sub * 128:(sub + 1) * 128], self.yaT[tt].res), tpb)
        kT.free()
        va.free()
        self.attn_rings(False)
        if grp == 1:
            self.rope_t.free()

    def mixer_na(self, l):
        c = self.ctx
        A = self.arena
        I = self.I
        grp = self.grp
        nt = self.nt
        nkeys = nt if grp == 0 else nt + 256
        nblk = nkeys // 128
        W = I["w_in"][l]
        self.attn_rings(True)
        if grp == 1:
            zt = self.r_f32.next().f32()
            c.memset(zt, 0.0)
            c.dma("sp", zt.ap[0:120, 48:79], I["na_rpb"][l].rearrange("h r j -> (h r) j"), writes=[zt.res])
            nvp = 8 * 15 * 127
            c.dma("sp", self.vp[0:nvp].rearrange("(a j) -> a j", j=127), zt.ap[0:120, 0:127], reads=[zt.res],
                  writes=[self.vp_res])
            mk = A.alloc(21 * 128 * 2, "na_mask")
            mkv = mk.bf().ap.rearrange("p (c q) -> p c q", c=21)
            c.dma("pool", mkv, I["na_mask"], writes=[mk.res])
            hraw = A.alloc(7 * 2 * 64 * 4, "na_hraw")
            hrv = hraw.f32().ap.rearrange("p (d b c) -> p d b c", d=7, b=2)
            eb = A.alloc(7 * 128 * 2, "na_eb")
            ebv = eb.bf().ap.rearrange("p (d b c) -> p d b c", d=7, b=2)
            etabs = [A.alloc(21 * 128 * 2, "na_etab%d" % i) for i in range(2)]
        for j in range(4):
            kT = A.alloc(nkeys * 2, "na_kT")
            kTv = kT.bf()
            qT = A.alloc(nt * 2, "na_qT")
            qTv = qT.bf()
            va = A.alloc(nblk * 2 * 65 * 2, "na_vaug")
            vab = va.bf(nblk * 2 * 65)
            vav = vab.ap.rearrange("p (b g e) -> p b g e", b=nblk, g=2)
            c.memset(vab, 1.0)
            wq = self.wload(W[:, O_NQ + j * 128:O_NQ + (j + 1) * 128], 8)
            wk = self.wload(W[:, O_NK + j * 128:O_NK + (j + 1) * 128], 8)
            wv = self.wload(W[:, O_NV + j * 128:O_NV + (j + 1) * 128], 8)
            for tt in range(self.ntt):
                h = self.hT[tt]
                hv = h.bf().ap.rearrange("p (k t) -> p k t", k=8)
                ps = self.psum()
                for k in range(8):
                    c.mm(ps, wq[:, k, :], T(hv[:, k, :], h.res), start=(k == 0), stop=(k == 7))
                c.copy(qTv[:, tt * 512:(tt + 1) * 512], ps, eng="act")
                ps = self.psum()
                for k in range(8):
                    c.mm(ps, wk[:, k, :], T(hv[:, k, :], h.res), start=(k == 0), stop=(k == 7))
                if grp == 0:
                    kf = self.r_f32.next().f32()
                    c.copy(kf, ps, eng="act")
                    c.copy(kTv[:, tt * 512:(tt + 1) * 512], kf)
                    for jj in range(4):
                        pt = self.psum("aux")
                        c.transpose(pt[:, 0:128], kf[:, jj * 128:(jj + 1) * 128], self.ident_f)
                        st = self.r_st.next().f32()
                        c.copy(st, pt[:, 0:128])
                        tok = tt * 512 + jj * 128
                        c.dma("sp", self.O["na_k"][tok // 256, l, tok % 256:tok % 256 + 128, j * 128:(j + 1) * 128],
                              st.ap, reads=[st.res], is_output=True)
                else:
                    c.copy(kTv[:, tt * 512:(tt + 1) * 512], ps)
                for jj in range(4):
                    pv = self.psum()
                    for k in range(8):
                        c.mm(pv[:, 0:128], T(hv[:, k, jj * 128:(jj + 1) * 128], h.res), wv[:, k, :],
                             start=(k == 0), stop=(k == 7))
                    blk = tt * 4 + jj
                    c.copy(T(vav[:, blk, :, 0:64], va.res),
                           T(pv.ap[:, 0:128].rearrange("p (g d) -> p g d", g=2), pv.res), eng="act")
                    if grp == 0:
                        st = self.r_st.next().f32()
                        c.copy(st, pv[:, 0:128])
                        tok = tt * 512 + jj * 128
                        c.dma("sp", self.O["na_v"][tok // 256, l, tok % 256:tok % 256 + 128, j * 128:(j + 1) * 128],
                              st.ap, reads=[st.res], is_output=True)
            if grp == 1:
                for b in range(2):
                    st = self.r_st.next().f32()
                    c.dma("sp", st.ap, I["cache_na_k"][l, b * 128:(b + 1) * 128, j * 128:(j + 1) * 128], writes=[st.res])
                    pt = self.psum("aux")
                    c.transpose(pt[:, 0:128], st, self.ident_f)
                    c.copy(kTv[:, nt + b * 128: nt + (b + 1) * 128], pt[:, 0:128])
                    c.dma("pool", vav[:, 16 + b, :, 0:64],
                          I["cache_na_v"][l, b * 128:(b + 1) * 128, j * 128:(j + 1) * 128].rearrange("p (g d) -> p g d", g=2),
                          writes=[va.res])
                for hh in range(2):
                    hd = 2 * j + hh
                    hres = [Res("hraw%d" % q_) for q_ in range(28)]
                    for r_ in hres:
                        r_.w = dict(hraw.res.w)
                        r_.r = dict(hraw.res.r)
                    c.op("dve", lambda e: e.memset(hraw.f32().ap, 0.0), reads=[], writes=hres)
                    for di, dj in enumerate(range(-3, 4)):
                        for a in range(2):
                            for b in range(2):
                                dr = 2 * dj + a - b + 7
                                if 0 <= dr <= 14:
                                    off = (hd * 15 + dr) * 127
                                    src = bass.AP(self.vp.tensor, off, [[1, 64], [1, 64]])
                                    c.dma("sp", hrv[a * 64:(a + 1) * 64, di, b, :], src, reads=[self.vp_res],
                                          writes=[hres[di * 4 + a * 2 + b]])
                    c.op("act", lambda e: e.activation(out=ebv, in_=hrv[:, :, :, ::-1], func=AF.Exp),
                         reads=hres, writes=[eb.res])
                    hraw.res.w = {}
                    hraw.res.r = {}
                    for r_ in hres:
                        for d_, dst_ in ((r_.w, hraw.res.w), (r_.r, hraw.res.r)):
                            for k_, v_ in d_.items():
                                if dst_.get(k_, 0) < v_:
                                    dst_[k_] = v_
                    et = etabs[hh]
                    etv = et.bf().ap.rearrange("p (c q) -> p c q", c=21)
                    ebc = eb.bf().ap.rearrange("p (d q) -> p d q", d=7)
                    for (c0_, n_, d0_) in ((0, 5, 1), (5, 4, 3), (9, 4, 2), (13, 4, 1), (17, 4, 0)):
                        c.tt(T(etv[:, c0_:c0_ + n_, :], et.res), T(ebc[:, d0_:d0_ + n_, :], eb.res),
                             T(mkv[:, c0_:c0_ + n_, :], mk.res), ALU.mult)
            import os
            NST = int(os.environ.get("NA_STAGE", "9"))
            if NST < 2:
                for y in self.ycT:
                    if j == 0:
                        c.memset(y.bf(), 0.0)
            elif grp == 0:
                for seq in range(4):
                    pTs = []
                    for kb in range(2):
                        pT = self.r_p.next().bf()
                        for hh in range(2):
                            s_ps = self.psum()
                            c.mm(s_ps[:, 0:256],
                                 kTv[hh * 64:(hh + 1) * 64, (seq * 2 + kb) * 128:(seq * 2 + kb + 1) * 128],
                                 qTv[hh * 64:(hh + 1) * 64, seq * 256:(seq + 1) * 256])
                            c.act(pT[:, hh * 256:(hh + 1) * 256], s_ps[:, 0:256], AF.Exp, scale=0.125)
                        pTs.append(pT)
                    if NST < 3:
                        if j == 0 and seq == 0:
                            for y in self.ycT:
                                c.memset(y.bf(), 0.0)
                        continue
                    for sub in range(2):
                        o_ps = self.psum("acc")
                        c.mm(o_ps[:, 0:130], self.zeros_b[:, 0:128], self.zeros_b[:, 0:130], start=True, stop=False,
                             signal=False)
                        for kb in range(2):
                            for hh in range(2):
                                last = (kb == 1 and hh == 1)
                                c.mm(o_ps[:, hh * 65:(hh + 1) * 65],
                                     pTs[kb][:, hh * 256 + sub * 128: hh * 256 + (sub + 1) * 128],
                                     T(vav[:, seq * 2 + kb, hh, :], va.res), start=False, stop=last, signal=last)
                        self._na_finish(o_ps, j, seq * 256 + sub * 128)
            else:
                its = []
                for i in range(16):
                    if i == 0:
                        loc, cls0 = [0, 1, 2, 3], 5
                    elif i == 1:
                        loc, cls0 = [0, 1, 2, 3], 9
                    elif i == 14:
                        loc, cls0 = [12, 13, 14, 15], 13
                    elif i == 15:
                        loc, cls0 = [12, 13, 14, 15], 17
                    else:
                        loc, cls0 = list(range(i - 2, i + 3)), 0
                    blocks = loc + [16, 17]
                    for hh in range(2):
                        for part in range(2):
                            its.append((i, hh, part, blocks[0:4] if part == 0 else blocks[4:], cls0, len(loc)))

                def emit_score(it):
                    i, hh, part, bl, cls0, nloc = it
                    s_ps = self.psum()
                    for bi, kb in enumerate(bl):
                        c.mm(s_ps[:, bi * 128:(bi + 1) * 128], kTv[hh * 64:(hh + 1) * 64, kb * 128:(kb + 1) * 128],
                             qTv[hh * 64:(hh + 1) * 64, i * 128:(i + 1) * 128])
                    return s_ps
                LOOK = 2
                pend = [emit_score(its[n_]) for n_ in range(min(LOOK, len(its)))]
                o_ps = None
                for n_, it in enumerate(its):
                    i, hh, part, bl, cls0, nloc = it
                    s_ps = pend.pop(0)
                    if n_ + LOOK < len(its):
                        pend.append(emit_score(its[n_ + LOOK]))
                    etv = etabs[hh].bf().ap.rearrange("p (c q) -> p c q", c=21)
                    pT = self.r_p.next().bf()
                    nb = len(bl)
                    c.act(pT[:, 0:nb * 128], s_ps[:, 0:nb * 128], AF.Exp, scale=0.125)
                    if part == 0:
                        nl = 4
                        c.tt(pT[:, 0:nl * 128], pT[:, 0:nl * 128],
                             T(etv[:, cls0:cls0 + nl, :].rearrange("p c q -> p (c q)"), etabs[hh].res), ALU.mult)
                    elif nloc == 5:
                        c.tt(pT[:, 0:128], pT[:, 0:128], T(etv[:, 4, :], etabs[hh].res), ALU.mult)
                    if hh == 0 and part == 0:
                        o_ps = self.psum("acc")
                        c.mm(o_ps[:, 0:130], self.zeros_b[:, 0:128], self.zeros_b[:, 0:130], start=True, stop=False,
                             signal=False)
                    for bi, kb in enumerate(bl):
                        last = (hh == 1 and part == 1 and bi == nb - 1)
                        c.mm(o_ps[:, hh * 65:(hh + 1) * 65], pT[:, bi * 128:(bi + 1) * 128],
                             T(vav[:, kb, hh, :], va.res), start=False, stop=last, signal=last)
                    if hh == 1 and part == 1:
                        self._na_finish(o_ps, j, i * 128)
            kT.free()
            qT.free()
            va.free()
        if grp == 1:
            for t_ in [mk, hraw, eb] + etabs:
                t_.free()
        self.attn_rings(False)

    def _na_finish(self, o_ps, j, tok0):
        c = self.ctx
        ov = o_ps.ap[:, 0:130].rearrange("p (h e) -> p h e", h=2)
        rec = self.r_small.next().f32()[:, 0:2]
        c.recip(rec, T(ov[:, :, 64], o_ps.res))
        yt = self.r_yt.next()
        ytv = yt.bf(128).ap.rearrange("p (h e) -> p h e", h=2)
        c.tt(T(ytv, yt.res), T(ov[:, :, 0:64], o_ps.res),
             T(rec.ap.unsqueeze(2).broadcast_to([128, 2, 64]), rec.res), ALU.mult)
        tp = self.psum("aux")
        tpb = T(tp.ap.bitcast(BF16)[:, 0:128], tp.res)
        c.transpose(tpb, yt.bf(128), self.ident_b)
        tt = tok0 // 512
        yv = self.ycT[tt].bf().ap.rearrange("p (k t) -> p k t", k=4)
        c.copy(T(yv[:, j, tok0 % 512: tok0 % 512 + 128], self.ycT[tt].res), tpb)

    def _cmul(self, o_re, o_im, a_re, a_im, b_re, b_im, tmp):
        c = self.ctx
        c.tt(o_re, a_re, b_re, ALU.mult)
        c.tt(tmp, a_im, b_im, ALU.mult)
        c.tt(o_re, o_re, tmp, ALU.subtract)
        c.tt(o_im, a_re, b_im, ALU.mult)
        c.tt(tmp, a_im, b_re, ALU.mult)
        c.tt(o_im, o_im, tmp, ALU.add)

    def _trig(self, out, bb, shift, tmp_v, tmp_n):
        c = self.ctx
        PI = math.pi
        c.ts(tmp_v, bb, shift + PI, None, ALU.add)
        c.ts(tmp_n, tmp_v, 2 * PI, None, ALU.is_ge)
        c.stt(tmp_n, tmp_v, 4 * PI, tmp_n, ALU.is_ge, ALU.add)
        c.stt(tmp_n, tmp_v, 6 * PI, tmp_n, ALU.is_ge, ALU.add)
        c.stt(tmp_v, tmp_n, -2 * PI, tmp_v, ALU.mult, ALU.add)
        c.act(out, tmp_v, AF.Sin, bias=self.negpi, scale=1.0)

    def ssm_prep(self, l):
        c = self.ctx
        A = self.arena
        I = self.I
        if not hasattr(self, "negpi"):
            t_ = A.alloc(64, "negpi")
            self.negpi = t_.f32()[:, 0:1]
            c.memset(self.negpi, -math.pi)
        small = A.alloc(4 * 32 * 24, "s5small")
        sm = small.f32().ap.rearrange("p (i g) -> p i g", g=32)

        def S(i):
            return T(sm[:, i, :], small.res)
        pw = A.alloc(4 * 2 * 16 * 32, "s5pw")
        pwv = pw.f32().ap.rearrange("p (r k g) -> p r k g", r=2, k=16)

        def PW(r, k):
            return T(pwv[:, r, k + 7, :], pw.res)
        bt = A.alloc(4 * 2 * 512, "s5B")
        btv = bt.f32().ap.rearrange("p (r g h) -> p r g h", r=2, g=32)
        ct = A.alloc(4 * 2 * 512, "s5C")
        ctv = ct.f32().ap.rearrange("p (r g h) -> p r g h", r=2, g=32)
        bb_t = A.alloc(4 * 2 * 512, "s5Bbar")
        bbv = bb_t.f32().ap.rearrange("p (r g h) -> p r g h", r=2, g=32)
        pr = A.alloc(4 * 4096, "s5pr")
        pi_ = A.alloc(4 * 4096, "s5pi")
        tmp = A.alloc(4 * 4096, "s5tmp")
        xd = A.alloc(2 * 4096, "s5X")
        yd = A.alloc(2 * 4096, "s5Y")
        zt = [A.alloc(2 * 4096, "s5ZT%d" % i) for i in range(2)]
        macc = A.alloc(4 * 4096, "s5M")
        cin = [A.alloc(2 * 4096, "s5Cin%d" % i) for i in range(4)]
        for t_ in cin:
            c.memset(t_.bf(), 0.0)
        msk = A.alloc(4 * 256, "s5mask")
        mskv = msk.f32().ap.rearrange("p (d c) -> p d c", d=2)
        c.dma("sp", mskv[:, 0, :], I["ssm_maskF"], writes=[msk.res])
        c.dma("sp", mskv[:, 1, :], I["ssm_maskB"], writes=[msk.res])
        dcol = A.alloc(4 * 32, "s5dcol")
        for s_ in range(8):
            c.dma("sp", dcol.f32().ap[s_ * 16:(s_ + 1) * 16, :], I["ssm_d"][l].rearrange("(g h) -> h g", h=16),
                  writes=[dcol.res], allow_slow_non_contiguous=True)
        lam8 = A.alloc(4 * 2 * 2 * 32, "lam8_%d" % l)
        l8v = lam8.f32().ap.rearrange("p (a r g) -> p a r g", a=2, r=2)
        self.lam8.append(T(l8v, lam8.res))
        r4 = lambda t_: t_.f32().ap.rearrange("p (g s h) -> p g s h", g=32, s=8)
        r4b = lambda t_: t_.bf().ap.rearrange("p (g s h) -> p g s h", g=32, s=8)
        for d in range(2):
            for ri, nm in ((0, "ssm_lam_re"), (1, "ssm_lam_im")):
                st = self.r_st32.next().f32()
                for hf in range(2):
                    c.dma("sp", st.ap[0:32, hf * 64:(hf + 1) * 64], I[nm][l, d], writes=[st.res])
                pt = self.psum("aux")
                c.transpose(pt[:, 0:32], st[0:32, :], self.ident_f[0:32, 0:32])
                c.copy(S(ri), pt[:, 0:32])
            c.dma("sp", sm[:, 2, :], I["ssm_log_step"][l, d:d + 1, :].partition_broadcast(128)
                  if hasattr(I["ssm_log_step"], "partition_broadcast") else I["ssm_log_step"][l, d:d + 1, :].broadcast_to([128, 32]),
                  writes=[small.res])
            for ri, nm in ((0, "ssm_b_re"), (1, "ssm_b_im")):
                for hf in range(2):
                    c.dma("sp", btv[hf * 64:(hf + 1) * 64, ri, :, :], I[nm][l, d].rearrange("g p h -> p g h"),
                          writes=[bt.res])
            for ri, nm in ((0, "ssm_c_re"), (1, "ssm_c_im")):
                for ch in range(4):
                    st = self.r_st32.next().f32()
                    for hf in range(2):
                        c.dma("sp", st.ap[:, hf * 64:(hf + 1) * 64], I[nm][l, d, ch * 128:(ch + 1) * 128, :],
                              writes=[st.res])
                    pt = self.psum("aux")
                    c.transpose(pt[:, 0:128], st, self.ident_f)
                    c.copy(T(ctv[:, ri, ch * 8:(ch + 1) * 8, :], ct.res),
                           T(pt.ap[:, 0:128].rearrange("p (g h) -> p g h", g=8), pt.res))
            c.act(S(2), S(2), AF.Exp)
            c.tt(S(3), S(0), S(2), ALU.mult)
            c.tt(S(4), S(1), S(2), ALU.mult)
            c.act(S(5), S(3), AF.Exp)
            c.act(S(6), S(3), AF.Exp, scale=-1.0)
            self._trig(S(7), S(4), 0.0, S(9), S(10))
            self._trig(S(8), S(4), math.pi / 2, S(9), S(10))
            c.memset(PW(0, 0), 1.0)
            c.memset(PW(1, 0), 0.0)
            c.tt(PW(0, 1), S(5), S(8), ALU.mult)
            c.tt(PW(1, 1), S(5), S(7), ALU.mult)
            c.tt(PW(0, -1), S(6), S(8), ALU.mult)
            c.tt(S(11), S(6), S(7), ALU.mult)
            c.ts(PW(1, -1), S(11), -1.0, None, ALU.mult)
            for k in range(1, 8):
                self._cmul(PW(0, k + 1), PW(1, k + 1), PW(0, k), PW(1, k), PW(0, 1), PW(1, 1), S(12))
            for k in range(1, 7):
                self._cmul(PW(0, -k - 1), PW(1, -k - 1), PW(0, -k), PW(1, -k), PW(0, -1), PW(1, -1), S(12))
            c.tt(S(15), S(0), S(0), ALU.mult)
            c.tt(S(16), S(1), S(1), ALU.mult)
            c.tt(S(15), S(15), S(16), ALU.add)
            c.recip(S(15), S(15))
            c.ts(S(16), PW(0, 1), -1.0, None, ALU.add)
            c.tt(S(17), S(16), S(0), ALU.mult)
            c.tt(S(18), PW(1, 1), S(1), ALU.mult)
            c.tt(S(17), S(17), S(18), ALU.add)
            c.tt(S(13), S(17), S(15), ALU.mult)
            c.tt(S(17), PW(1, 1), S(0), ALU.mult)
            c.tt(S(18), S(16), S(1), ALU.mult)
            c.tt(S(17), S(17), S(18), ALU.subtract)
            c.tt(S(14), S(17), S(15), ALU.mult)

            def bc_g(t_):
                return T(t_.ap.unsqueeze(2).broadcast_to([128, 32, 16]), t_.res)
            t512 = T(tmp.f32().ap[:, 0:512].rearrange("p (g h) -> p g h", g=32), tmp.res)
            self._cmul(T(bbv[:, 0], bb_t.res), T(bbv[:, 1], bb_t.res), bc_g(S(13)), bc_g(S(14)),
                       T(btv[:, 0], bt.res), T(btv[:, 1], bt.res), t512)
            hs = slice(d * 64, (d + 1) * 64)
            c.copy(T(l8v[hs, 0, 0, :], lam8.res), PW(0, 8)[hs])
            c.copy(T(l8v[hs, 0, 1, :], lam8.res), PW(0, 8)[hs])
            c.copy(T(l8v[hs, 1, 1, :], lam8.res), PW(1, 8)[hs])
            c.ts(T(l8v[hs, 1, 0, :], lam8.res), PW(1, 8)[hs], -1.0, None, ALU.mult)

            def pw_view(k0, step):
                base_re = pwv[:, 0, k0 + 7, :]
                base_im = pwv[:, 1, k0 + 7, :]
                def mk(b_):
                    return T(bass.AP(b_.tensor, b_.offset, [list(b_.ap[0]), [1, 32], [32 * step, 8], [0, 16]]), pw.res)
                return mk(base_re), mk(base_im)

            def mat_view(v4):
                def mk(r):
                    b_ = v4[:, r]
                    return T(bass.AP(b_.tensor, b_.offset, [list(b_.ap[0]), [16, 32], [0, 8], [1, 16]]), None)
                return mk(0), mk(1)
            prv, piv, tmv = T(r4(pr), pr.res), T(r4(pi_), pi_.res), T(r4(tmp), tmp.res)
            a_re, a_im = pw_view(7, -1) if d == 0 else pw_view(0, 1)
            b_re, b_im = mat_view(bbv)
            b_re.res = b_im.res = bb_t.res
            self._cmul(prv, piv, a_re, a_im, b_re, b_im, tmv)
            xv, yv_ = r4b(xd), r4b(yd)
            c.copy(T(xv[0:64], xd.res), prv[0:64])
            c.copy(T(xv[64:128], xd.res), piv[64:128], eng="act")
            c.copy(T(r4b(zt[0])[hs], zt[0].res), prv[hs])
            c.copy(T(r4b(zt[1])[hs], zt[1].res), piv[hs], eng="act")
            a_re, a_im = pw_view(-7, 1) if d == 0 else pw_view(0, -1)
            b_re, b_im = mat_view(ctv)
            b_re.res = b_im.res = ct.res
            self._cmul(prv, piv, a_re, a_im, b_re, b_im, tmv)
            c.copy(T(yv_[0:64], yd.res), prv[0:64])
            c.ts(T(yv_[64:128], yd.res), piv[64:128], -1.0, None, ALU.mult)
            mv = macc.f32().ap.rearrange("p (g c) -> p g c", g=32)
            xf = xd.bf().ap.rearrange("p (g c) -> p g c", g=32)
            yf = yd.bf().ap.rearrange("p (g c) -> p g c", g=32)
            for g in range(32):
                ps = self.psum()
                c.mm(ps[:, 0:128], T(xf[:, g, :], xd.res), T(yf[:, g, :], yd.res))
                if d == 0:
                    c.tt(T(mv[:, g, :], macc.res), ps[:, 0:128], T(mskv[:, 0, :], msk.res), ALU.mult)
                else:
                    t128 = T(tmp.f32().ap[:, 0:128], tmp.res)
                    c.tt(t128, ps[:, 0:128], T(mskv[:, 1, :], msk.res), ALU.mult)
                    c.tt(T(mv[:, g, :], macc.res), T(mv[:, g, :], macc.res), t128, ALU.add)
            a_re, a_im = pw_view(1, 1) if d == 0 else pw_view(8, -1)
            self._cmul(prv, piv, a_re, a_im, b_re, b_im, tmv)
            c.copy(T(r4b(cin[2 * d])[hs], cin[2 * d].res), prv[hs])
            c.ts(T(r4b(cin[2 * d + 1])[hs], cin[2 * d + 1].res), piv[hs], -1.0, None, ALU.mult)
        mv = macc.f32().ap.rearrange("p (g c) -> p g c", g=32)
        mb = A.alloc(2 * 4096, "s5Mb")
        mbv = mb.bf().ap.rearrange("p (g c) -> p g c", g=32)
        for g in range(32):
            c.stt(T(mbv[:, g, :], mb.res), self.ident_f, T(dcol.f32().ap[:, g:g + 1], dcol.res), T(mv[:, g, :], macc.res),
                  ALU.mult, ALU.add)
        c.dma("sp", self.ssmw[l, 0].rearrange("g p c -> p g c"), mbv, reads=[mb.res], writes=[self.ssmw_res])
        for ci in range(4):
            c.dma("sp", self.ssmw[l, 3 + ci].rearrange("g p c -> p g c"),
                  cin[ci].bf().ap.rearrange("p (g c) -> p g c", g=32), reads=[cin[ci].res], writes=[self.ssmw_res])
        for ri in range(2):
            zf = zt[ri].bf().ap.rearrange("p (g c) -> p g c", g=32)
            zo = A.alloc(2 * 4096, "s5Zo")
            zov = zo.bf().ap.rearrange("p (g c) -> p g c", g=32)
            for g in range(32):
                tp = self.psum("aux")
                tpb = T(tp.ap.bitcast(BF16)[:, 0:128], tp.res)
                c.transpose(tpb, T(zf[:, g, :], zt[ri].res), self.ident_b)
                c.copy(T(zov[:, g, :], zo.res), tpb, eng=("act" if g % 2 else "dve"))
            c.dma("sp", self.ssmw[l, 1 + ri].rearrange("g p c -> p g c"), zov, reads=[zo.res], writes=[self.ssmw_res])
            zo.free()
        for t_ in [small, pw, bt, ct, bb_t, pr, pi_, tmp, xd, yd, macc, msk, dcol, mb] + zt + cin:
            t_.free()

    def mixer_s5(self, l):
        c = self.ctx
        A = self.arena
        I = self.I
        grp = self.grp
        nt = self.nt
        N = nt // 8
        nseq = 4 if grp == 0 else 1
        cps = N // nseq
        W = I["w_in"][l]
        wa = A.alloc(8 * 240 * 2, "s5wa")
        wav = wa.bf().ap.rearrange("p (a j) -> p a j", a=8)
        c.dma("pool", wav, I["ssm_wa"], writes=[wa.res])
        zu = A.alloc(4 * nt * 2, "s5zu")
        zuv = zu.bf().ap.rearrange("p (k t) -> p k t", k=4)
        for kc in range(4):
            w = self.wload(W[:, O_U + kc * 128:O_U + (kc + 1) * 128], 8)
            for tt in range(self.ntt):
                h = self.hT[tt]
                hv = h.bf().ap.rearrange("p (k t) -> p k t", k=8)
                ps = self.psum()
                for k in range(8):
                    c.mm(ps, w[:, k, :], T(hv[:, k, :], h.res), start=(k == 0), stop=(k == 7))
                c.copy(T(zuv[:, kc, tt * 512:(tt + 1) * 512], zu.res), ps, eng=("act" if tt % 2 else "dve"))
        U = A.alloc(32 * N * 2, "s5U")
        Uv = U.bf().ap.rearrange("p (g n) -> p g n", g=32)
        for g in range(32):
            kc, gl = g // 8, g % 8
            ps = self.psum()
            zs = zuv[:, kc, :].rearrange("p (n s) -> p s n", s=8)
            for s_ in range(8):
                c.mm(ps[:, 0:N], T(wav[:, gl, 112 - 16 * s_: 240 - 16 * s_], wa.res), T(zs[:, s_, :], zu.res),
                     start=(s_ == 0), stop=(s_ == 7))
            c.copy(T(Uv[:, g, :], U.res), ps[:, 0:N], eng=("act" if g % 2 else "dve"))
        zu.free()
        Z = A.alloc(2 * 32 * nseq * (cps + 2) * 2, "s5Z")
        Zv = Z.bf(2 * 32 * nseq * (cps + 2)).ap.rearrange("p (r g q k) -> p r g q k", r=2, g=32, q=nseq)
        c.memset(Z.bf(), 0.0)
        St = [A.alloc(2 * 32 * nseq * 4, "s5S%d" % i) for i in range(2)]
        Sv = [T(t_.f32(2 * 32 * nseq).ap.rearrange("p (r g q) -> p r g q", r=2, g=32), t_.res) for t_ in St]
        if grp == 0:
            c.memset(Sv[0], 0.0)
        else:
            for ri, nm in ((0, "state_re"), (1, "state_im")):
                for d in range(2):
                    st = self.r_st32.next().f32()
                    for hf in range(2):
                        c.dma("sp", st.ap[0:32, hf * 64:(hf + 1) * 64], I[nm][l, d], writes=[st.res])
                    pt = self.psum("aux")
                    c.transpose(pt[:, 0:32], st[0:32, :], self.ident_f[0:32, 0:32])
                    hs = slice(d * 64, (d + 1) * 64)
                    c.copy(T(Sv[0].ap[hs, ri, :, 0], St[0].res), pt[hs, 0:32])
        c.copy(T(Zv[0:64, :, :, :, 0], Z.res), Sv[0][0:64], eng="act")
        c.copy(T(Zv[64:128, :, :, :, cps + 1], Z.res), Sv[0][64:128], eng="act")
        wst = Ring(A, 3, 7 * 128 * 2, "s5wst")
        def wtile(g):
            t_ = wst.next()
            v = t_.bf().ap.rearrange("p (k c) -> p k c", k=7)
            c.dma("sp", v, self.ssmw[l, :, g].rearrange("k p c -> p k c"), reads=[self.ssmw_res], writes=[t_.res])
            return T(v, t_.res)
        wt_n = wtile(0)
        for g in range(32):
            wt = wt_n
            if g + 1 < 32:
                wt_n = wtile(g + 1)
            for ri in range(2):
                ps = self.psum()
                c.mm(ps[:, 0:N], wt[:, 1 + ri, :], T(Uv[:, g, :], U.res))
                pv = ps.ap[:, 0:N].rearrange("p (q k) -> p q k", q=nseq)
                c.copy(T(Zv[:, ri, g, :, 1:cps + 1], Z.res), T(pv, ps.res), eng=("act" if ri else "dve"))
        l8 = self.lam8[l]
        sh_ = [128, 2, 32, nseq]
        lamA = T(l8.ap[:, 0].unsqueeze(3).broadcast_to(sh_), l8.res)
        lamB = T(l8.ap[:, 1].unsqueeze(3).broadcast_to(sh_), l8.res)
        tA = A.alloc(2 * 32 * nseq * 4, "s5tA")
        tB = A.alloc(2 * 32 * nseq * 4, "s5tB")
        tAv = T(tA.f32(2 * 32 * nseq).ap.rearrange("p (r g q) -> p r g q", r=2, g=32), tA.res)
        tBv = T(tB.f32(2 * 32 * nseq).ap.rearrange("p (r g q) -> p r g q", r=2, g=32), tB.res)
        cur = 0
        for k in range(cps):
            S_, Sn = Sv[cur], Sv[1 - cur]
            c.tt(tAv, S_, lamA, ALU.mult)
            c.tt(tBv, T(S_.ap[:, ::-1], S_.res), lamB, ALU.mult)
            c.tt(tAv, tAv, tBv, ALU.add)
            for hs, col in ((slice(0, 64), k + 1), (slice(64, 128), cps - k)):
                c.tt(Sn[hs], tAv[hs], T(Zv[hs, :, :, :, col], Z.res), ALU.add)
                c.copy(T(Zv[hs, :, :, :, col], Z.res), Sn[hs], eng="act")
            cur = 1 - cur
        if grp == 0:
            fin = Sv[cur]
            for ri, nm in ((0, "ssm_re"), (1, "ssm_im")):
                pt = self.psum("aux")
                c.transpose(pt[:, 0:128], T(fin.ap[:, ri].rearrange("p g q -> p (g q)"), fin.res), self.ident_f)
                st = self.r_st32.next().f32()
                c.copy(st, pt[:, 0:128])
                for q in range(4):
                    c.dma("sp", self.O[nm][q, l].rearrange("d g p -> g d p"),
                          st.ap[q::4, :].rearrange("g (d p) -> g d p", d=2), reads=[st.res], is_output=True)
        for t_ in St + [tA, tB]:
            t_.free()
        Y = A.alloc(32 * N * 2, "s5Y")
        Yv = Y.bf().ap.rearrange("p (g n) -> p g n", g=32)
        wt_n = wtile(0)
        for g in range(32):
            wt = wt_n
            if g + 1 < 32:
                wt_n = wtile(g + 1)
            ps = self.psum()
            c.mm(ps[:, 0:N], wt[:, 0, :], T(Uv[:, g, :], U.res), start=True, stop=False)
            psq = T(ps.ap[:, 0:N].rearrange("p (q k) -> p q k", q=nseq), ps.res)
            for ri in range(2):
                c.mm(psq, wt[:, 3 + ri, :], T(Zv[:, ri, g, :, 0:cps], Z.res), start=False, stop=False)
                c.mm(psq, wt[:, 5 + ri, :], T(Zv[:, ri, g, :, 2:cps + 2], Z.res), start=False, stop=(ri == 1))
            c.copy(T(Yv[:, g, :], Y.res), ps[:, 0:N], eng=("act" if g % 2 else "dve"))
        U.free()
        Z.free()
        wst.free()
        gT = A.alloc(4 * nt * 2, "s5g")
        gv = gT.bf().ap.rearrange("p (k t) -> p k t", k=4)
        for kc in range(4):
            for tp_ in range(8):
                ps = self.psum()
                for gl in range(8):
                    c.mm(ps[:, 0:N], T(wav[:, tp_, 112 - 16 * gl: 240 - 16 * gl], wa.res), T(Yv[:, kc * 8 + gl, :], Y.res),
                         start=(gl == 0), stop=(gl == 7))
                x_ = self.r_f32.next().f32()[:, 0:N]
                c.copy(x_, ps[:, 0:N], eng="act")
                u_ = self.r_f32.next().f32()[:, 0:N]
                c.tt(u_, x_, x_, ALU.mult)
                c.ts(u_, u_, 0.044715, 1.0, ALU.mult, ALU.add)
                c.tt(u_, u_, x_, ALU.mult)
                c.act(u_, u_, AF.Sigmoid, scale=1.5957691216057308)
                dst = gv[:, kc, :].rearrange("p (n s) -> p s n", s=8)[:, tp_, :]
                c.tt(T(dst, gT.res), x_, u_, ALU.mult)
        Y.free()
        wa.free()
        self.ybT = [A.alloc(4 * 512 * 2, "ybT%d" % i) for i in range(self.ntt)]
        for ko in range(4):
            w = self.wload(I["w_glu"][l][:, ko * 128:(ko + 1) * 128], 4)
            for tt in range(self.ntt):
                ps = self.psum()
                for k in range(4):
                    c.mm(ps, w[:, k, :], T(gv[:, k, tt * 512:(tt + 1) * 512], gT.res), start=(k == 0), stop=(k == 3))
                sg = self.r_f32.next().f32()
                c.act(sg, ps, AF.Sigmoid)
                yv = self.ybT[tt].bf().ap.rearrange("p (k t) -> p k t", k=4)
                c.tt(T(yv[:, ko, :], self.ybT[tt].res), T(gv[:, ko, tt * 512:(tt + 1) * 512], gT.res), sg, ALU.mult)
        gT.free()

    def load_x_tile(self, tt, first):
        c = self.ctx
        A = self.arena
        I = self.I
        gt = self.t0 // 512 + tt
        xt = A.alloc(8 * 512 * 4, "xT")
        xv = xt.f32().ap.rearrange("p (k t) -> p k t", k=8)
        import os
        if not first or os.environ.get("NOFP32"):
            c.dma("sp", xv, self.xs[:, :, gt * 512:(gt + 1) * 512].rearrange("k p t -> p k t"),
                  reads=[self.xs_res[k][gt] for k in range(8)], writes=[xt.res])
            return xt
        for k in range(8):
            pass
        tm = [A.alloc(1024 * 4, "xtm%d" % j) for j in range(4)]
        for j in range(4):
            c.dma("sp", tm[j].f32().ap, I["xin"][gt * 512 + j * 128: gt * 512 + (j + 1) * 128, :], writes=[tm[j].res])
        for k in range(8):
            ps = self.psum()
            for j in range(4):
                c.transpose(ps[:, j * 128:(j + 1) * 128], tm[j].f32()[:, k * 128:(k + 1) * 128], self.ident_f,
                            signal=(j == 3))
            c.copy(T(xv[:, k, :], xt.res), ps, eng=("act" if k % 2 else "dve"))
            c.dma("sp", self.xs[k][:, gt * 512:(gt + 1) * 512], xv[:, k, :], reads=[xt.res],
                  writes=[self.xs_res[k][gt]])
        for j in range(4):
            tm[j].free()
        return xt

    def norm_in(self, l, first, which):
        c = self.ctx
        A = self.arena
        n = self.grp
        ia, ib = (0, 1) if which == 0 else (3, 4)
        for tt in range(self.ntt):
            xt = self.load_x_tile(tt, first)
            self.norm_tile(l, xt, self.hT[tt], ia, ib)
            xt.free()

    def norm_tile(self, l, xt, hT, ia, ib):
        c = self.ctx
        A = self.arena
        n = self.grp
        if True:
            xv = xt.f32().ap.rearrange("p (k t) -> p k t", k=8)
            sq = A.alloc(512 * 2 * 2, "sq")
            sqv = sq.bf().ap.rearrange("p (b t) -> p b t", b=2)
            ps = self.psum()
            for k in range(8):
                c.act(T(sqv[:, k % 2, :], sq.res), T(xv[:, k, :], xt.res), AF.Square)
                c.mm(ps, self.ones_b, T(sqv[:, k % 2, :], sq.res), start=(k == 0), stop=(k == 7), signal=True)
            rs = A.alloc(512 * 4, "rstd")
            c.act(rs.f32(), ps, AF.Ln, bias=EPS, scale=1.0)
            c.act(rs.f32(), rs.f32(), AF.Exp, scale=-0.5)
            hv = hT.bf().ap.rearrange("p (k t) -> p k t", k=8)
            for k in range(8):
                c.tt(T(xv[:, k, :], xt.res), T(xv[:, k, :], xt.res), rs.f32(), ALU.mult)
                m = self.modS
                c.act(T(hv[:, k, :], hT.res), T(xv[:, k, :], xt.res), AF.Identity,
                      bias=T(m.ap[:, l, n, ib, k:k + 1], m.res), scale=T(m.ap[:, l, n, ia, k:k + 1], m.res))
            sq.free()
            rs.free()


def _prep_inputs(inp, core):
    f = np.float32
    m = {}
    xp = np.asarray(inp["x_prompt"], f)[4 * core:4 * core + 4].reshape(NP_TOK, D)
    xsam = np.asarray(inp["x_sample"], f)[core]
    m["xin"] = np.ascontiguousarray(np.concatenate([xp, xsam], 0))
    cv = np.stack([np.asarray(inp["c_ctx"], f), np.asarray(inp["c"], f)[core]], 0)
    m["cvecT"] = np.ascontiguousarray(cv.reshape(2, 8, 128).transpose(2, 1, 0))
    m["state_re"] = np.ascontiguousarray(np.asarray(inp["state_ssm_re"], f)[core])
    m["state_im"] = np.ascontiguousarray(np.asarray(inp["state_ssm_im"], f)[core])
    m["cache_na_k"] = np.ascontiguousarray(np.asarray(inp["cache_na_k"], f)[core].reshape(DEPTH, 256, 512))
    m["cache_na_v"] = np.ascontiguousarray(np.asarray(inp["cache_na_v"], f)[core].reshape(DEPTH, 256, 512))
    m["cache_ga_k"] = np.ascontiguousarray(np.asarray(inp["cache_ga_k"], f)[core].reshape(DEPTH, 256, 128))
    m["cache_ga_v"] = np.ascontiguousarray(np.asarray(inp["cache_ga_v"], f)[core].reshape(DEPTH, 256, 128))
    return m


def _shared_inputs(inp):
    f = np.float32
    m = {}
    m["w_mod"] = np.ascontiguousarray(np.asarray(inp["w_mod"], f))
    m["b_modT"] = np.ascontiguousarray(np.asarray(inp["b_mod"], f).reshape(DEPTH, 48, 128).transpose(2, 0, 1))
    m["norm_gT"] = np.ascontiguousarray(np.asarray(inp["norm_g"], f).reshape(DEPTH, 4, 8, 128).transpose(3, 0, 1, 2))
    m["w_in"] = np.ascontiguousarray(np.asarray(inp["w_in"], f))
    m["na_rpb"] = np.ascontiguousarray(np.asarray(inp["na_rpb"], f))
    for nm in ("ssm_lam_re", "ssm_lam_im", "ssm_log_step", "ssm_b_re", "ssm_b_im", "ssm_d", "w_glu"):
        m[nm] = np.ascontiguousarray(np.asarray(inp[nm], f))
    m["ssm_c_re"] = np.ascontiguousarray(np.asarray(inp["ssm_c_re"], f).reshape(DEPTH, 2, 512, 64))
    m["ssm_c_im"] = np.ascontiguousarray(np.asarray(inp["ssm_c_im"], f).reshape(DEPTH, 2, 512, 64))
    for nm in ("w_br_a", "w_br_b", "w_br_c", "w_out", "w_up", "w_down"):
        m[nm] = np.ascontiguousarray(np.asarray(inp[nm], f))
    m["conv_wT"] = np.ascontiguousarray(np.asarray(inp["conv_w"], f).reshape(DEPTH, 3, 44, 128).transpose(3, 0, 1, 2))
    m["conv_bT"] = np.ascontiguousarray(np.asarray(inp["conv_b"], f).reshape(DEPTH, 44, 128).transpose(2, 0, 1))
    qk = np.asarray(inp["qk_norm_g"], f)
    m["qk_gT"] = np.ascontiguousarray(np.concatenate([qk, qk], -1).transpose(2, 0, 1))
    m.update(_host_consts())
    return m


_DEBUG = []


def run(inputs, debug=None):
    dbg = debug if debug is not None else _DEBUG
    dry = Builder(debug=dbg)
    dry.build()
    b = Builder(debug=dbg, wplan=dry.wrec)
    nc = b.build()
    shared = _shared_inputs(inputs)
    in_maps = []
    for core in range(NCORES):
        m = dict(shared)
        m.update(_prep_inputs(inputs, core))
        in_maps.append(m)
    res = run_bass_kernel_spmd(nc, in_maps, core_ids=list(range(NCORES)))
    return res.results, b


def kernel(**inputs):
    results, b = run(inputs)
    f = np.float32
    y_p = np.concatenate([r["y"][:NP_TOK].reshape(4, SEQ, D) for r in results], 0).astype(f)
    y_s = np.stack([r["y"][NP_TOK:] for r in results], 0).astype(f)
    ga_k = np.concatenate([r["ga_k"].reshape(4, DEPTH, SEQ, 2, HD) for r in results], 0).astype(f)
    ga_v = np.concatenate([r["ga_v"].reshape(4, DEPTH, SEQ, 2, HD) for r in results], 0).astype(f)
    na_k = np.concatenate([r["na_k"].reshape(4, DEPTH, SEQ, 8, HD) for r in results], 0).astype(f)
    na_v = np.concatenate([r["na_v"].reshape(4, DEPTH, SEQ, 8, HD) for r in results], 0).astype(f)
    s_re = np.concatenate([r["ssm_re"].reshape(4, DEPTH, 2, 32, 64) for r in results], 0).astype(f)
    s_im = np.concatenate([r["ssm_im"].reshape(4, DEPTH, 2, 32, 64) for r in results], 0).astype(f)
    return (y_p, y_s, ga_k, ga_v, na_k, na_v, s_re, s_im)
```
